# Optimizing a Trainium2 kernel written in Bass

```python
import math
import jax
import jax.numpy as jnp
from jax import lax
import numpy as np


D_MODEL = 1024
BATCH = 8
SEQ = 4096
DEPTH = 4

SSM_WIDTH = D_MODEL // 2
SSM_GROUP_WIDTH = 16
N_SSM_GROUPS = SSM_WIDTH // SSM_GROUP_WIDTH
SSM_STATE = 64
HEAD_DIM = 64
N_Q_HEADS = (D_MODEL - SSM_WIDTH) // HEAD_DIM
N_KV_HEADS = 2
GQA_GROUP = N_Q_HEADS // N_KV_HEADS
ATTN_WIDTH = N_Q_HEADS * HEAD_DIM
KV_WIDTH = N_KV_HEADS * HEAD_DIM
IN_WIDTH = SSM_WIDTH + ATTN_WIDTH + 2 * KV_WIDTH
MIX_WIDTH = SSM_WIDTH + ATTN_WIDTH
WINDOW = 128
BLOCK = 128
FFN_HIDDEN = int(math.ceil(8 * D_MODEL / 3 / 256) * 256)
PLE_DIM = 256
RMS_EPS = 1e-6
DT_MIN = 1e-3
DT_MAX = 1e-1

kernel_name = "hymba_s5_swa_sink_hybrid"


def rms_norm(x, g):
    xf = x.astype(jnp.float32)
    y = xf * lax.rsqrt(jnp.mean(xf * xf, axis=-1, keepdims=True) + RMS_EPS)
    return (y * g.astype(jnp.float32)).astype(x.dtype)


def s5_mixer(u, a_re, a_im, log_dt, b_re, b_im, c_re, c_im, d_skip, w_glu):
    bsz, seq, _ = u.shape
    f32 = jnp.float32
    uf = u.astype(f32).reshape(bsz, seq, N_SSM_GROUPS, SSM_GROUP_WIDTH)
    ar = a_re.astype(f32)
    ai = a_im.astype(f32)
    dt = jnp.exp(log_dt.astype(f32))[:, None]
    mag = jnp.exp(ar * dt)
    lb_re = mag * jnp.cos(ai * dt)
    lb_im = mag * jnp.sin(ai * dt)
    den = ar * ar + ai * ai
    nr = lb_re - 1.0
    ni = lb_im
    f_re = (nr * ar + ni * ai) / den
    f_im = (ni * ar - nr * ai) / den
    br = b_re.astype(f32)
    bi = b_im.astype(f32)
    bb_re = f_re[..., None] * br - f_im[..., None] * bi
    bb_im = f_re[..., None] * bi + f_im[..., None] * br
    bu_re = jnp.einsum('blgh,gph->blgp', uf, bb_re)
    bu_im = jnp.einsum('blgh,gph->blgp', uf, bb_im)
    a_t_re = jnp.broadcast_to(lb_re[None, None], (1, seq, N_SSM_GROUPS, SSM_STATE))
    a_t_im = jnp.broadcast_to(lb_im[None, None], (1, seq, N_SSM_GROUPS, SSM_STATE))

    def combine(e1, e2):
        a1r, a1i, b1r, b1i = e1
        a2r, a2i, b2r, b2i = e2
        return (a2r * a1r - a2i * a1i,
                a2r * a1i + a2i * a1r,
                a2r * b1r - a2i * b1i + b2r,
                a2r * b1i + a2i * b1r + b2i)

    _, _, xr, xi = lax.associative_scan(combine, (a_t_re, a_t_im, bu_re, bu_im), axis=1)
    y = (jnp.einsum('blgp,ghp->blgh', xr, c_re.astype(f32))
         - jnp.einsum('blgp,ghp->blgh', xi, c_im.astype(f32)))
    y = y.reshape(bsz, seq, SSM_WIDTH) + d_skip.astype(f32) * uf.reshape(bsz, seq, SSM_WIDTH)
    y = jax.nn.gelu(y)
    y = y * jax.nn.sigmoid(y @ w_glu.astype(f32))
    return y.astype(u.dtype)


def sliding_window_attention(q, k, v, sinks):
    bsz, seq, _, dh = q.shape
    nb = seq // BLOCK
    qb = q.reshape(bsz, nb, BLOCK, N_KV_HEADS, GQA_GROUP, dh)
    kb = k.reshape(bsz, nb, BLOCK, N_KV_HEADS, dh)
    vb = v.reshape(bsz, nb, BLOCK, N_KV_HEADS, dh)
    pad = ((0, 0), (1, 0), (0, 0), (0, 0), (0, 0))
    k_prev = jnp.pad(kb, pad)[:, :-1]
    v_prev = jnp.pad(vb, pad)[:, :-1]
    keys = jnp.concatenate([k_prev, kb], axis=2)
    vals = jnp.concatenate([v_prev, vb], axis=2)
    scale = 1.0 / math.sqrt(dh)
    scores = jnp.einsum('bnqhgd,bnkhd->bnhgqk', qb, keys).astype(jnp.float32) * scale
    qpos = jnp.arange(BLOCK)[:, None]
    kpos = jnp.arange(2 * BLOCK)[None, :] - BLOCK
    rel = qpos - kpos
    band = (rel >= 0) & (rel < WINDOW)
    blk = jnp.arange(nb)[:, None, None]
    valid = band[None] & ((blk > 0) | (kpos[None] >= 0))
    scores = jnp.where(valid[None, :, None, None], scores, -jnp.inf)
    sink = sinks.astype(jnp.float32).reshape(N_KV_HEADS, GQA_GROUP)[None, None, :, :, None, None]
    sink = jnp.broadcast_to(sink, scores.shape[:-1] + (1,))
    probs = jax.nn.softmax(jnp.concatenate([scores, sink], axis=-1), axis=-1)[..., :-1]
    out = jnp.einsum('bnhgqk,bnkhd->bnqhgd', probs.astype(v.dtype), vals)
    return out.reshape(bsz, seq, N_Q_HEADS * dh)


def swiglu(h, w_in, w_out):
    gu = h @ w_in
    gate, up = jnp.split(gu, 2, axis=-1)
    return (jax.nn.silu(gate) * up) @ w_out


def setup_inputs(seed: int = 0) -> dict:
    key = jax.random.key(seed)
    ks = jax.random.split(key, 26)
    f32 = jnp.float32

    def nrm(k, shape, scale):
        return jax.random.normal(k, shape, f32) * scale

    def gain(k, shape):
        return 1.0 + 0.02 * jax.random.normal(k, shape, f32)

    G, P, H = N_SSM_GROUPS, SSM_STATE, SSM_GROUP_WIDTH
    n_idx = jnp.arange(P, dtype=f32)
    ssm_a_re = -0.5 + 0.01 * jax.random.normal(ks[3], (DEPTH, G, P), f32)
    ssm_a_im = math.pi * n_idx[None, None, :] + 0.01 * jax.random.normal(ks[4], (DEPTH, G, P), f32)
    ssm_log_dt = jax.random.uniform(ks[5], (DEPTH, G), f32,
                                    math.log(DT_MIN), math.log(DT_MAX))
    return {
        "x": nrm(ks[0], (BATCH, SEQ, D_MODEL), 1.0),
        "p": nrm(ks[1], (DEPTH, BATCH, SEQ, PLE_DIM), 1.0),
        "norm_mix": gain(ks[2], (DEPTH, D_MODEL)),
        "w_in": nrm(ks[6], (DEPTH, D_MODEL, IN_WIDTH), D_MODEL ** -0.5),
        "ssm_a_re": ssm_a_re,
        "ssm_a_im": ssm_a_im,
        "ssm_log_dt": ssm_log_dt,
        "ssm_b_re": nrm(ks[7], (DEPTH, G, P, H), (2 * H) ** -0.5),
        "ssm_b_im": nrm(ks[8], (DEPTH, G, P, H), (2 * H) ** -0.5),
        "ssm_c_re": nrm(ks[9], (DEPTH, G, H, P), (2 * P) ** -0.5),
        "ssm_c_im": nrm(ks[10], (DEPTH, G, H, P), (2 * P) ** -0.5),
        "ssm_d": nrm(ks[11], (DEPTH, SSM_WIDTH), 1.0),
        "ssm_w_glu": nrm(ks[12], (DEPTH, SSM_WIDTH, SSM_WIDTH), SSM_WIDTH ** -0.5),
        "attn_sinks": nrm(ks[13], (DEPTH, N_Q_HEADS), 0.5),
        "norm_ssm_out": gain(ks[14], (DEPTH, SSM_WIDTH)),
        "norm_attn_out": gain(ks[15], (DEPTH, ATTN_WIDTH)),
        "w_out": nrm(ks[16], (DEPTH, MIX_WIDTH, D_MODEL), MIX_WIDTH ** -0.5),
        "norm_ffn": gain(ks[17], (DEPTH, D_MODEL)),
        "w_ffn_in": nrm(ks[18], (DEPTH, D_MODEL, 2 * FFN_HIDDEN), D_MODEL ** -0.5),
        "w_ffn_out": nrm(ks[19], (DEPTH, FFN_HIDDEN, D_MODEL), FFN_HIDDEN ** -0.5),
        "norm_ple": gain(ks[20], (DEPTH, D_MODEL)),
        "w_ple_gate": nrm(ks[21], (DEPTH, D_MODEL, D_MODEL), D_MODEL ** -0.5),
        "w_ple_proj": nrm(ks[22], (DEPTH, PLE_DIM, D_MODEL), PLE_DIM ** -0.5),
        "norm_final": gain(ks[23], (D_MODEL,)),
    }


def reference(x, p, norm_mix, w_in, ssm_a_re, ssm_a_im, ssm_log_dt, ssm_b_re, ssm_b_im,
              ssm_c_re, ssm_c_im, ssm_d, ssm_w_glu, attn_sinks, norm_ssm_out, norm_attn_out,
              w_out, norm_ffn, w_ffn_in, w_ffn_out, norm_ple, w_ple_gate, w_ple_proj,
              norm_final):
    bsz, seq, _ = x.shape
    h = x
    splits = [SSM_WIDTH, SSM_WIDTH + ATTN_WIDTH, SSM_WIDTH + ATTN_WIDTH + KV_WIDTH]
    for i in range(DEPTH):
        hn = rms_norm(h, norm_mix[i])
        proj = hn @ w_in[i]
        u, q, k, v = jnp.split(proj, splits, axis=-1)
        ssm_out = s5_mixer(u, ssm_a_re[i], ssm_a_im[i], ssm_log_dt[i], ssm_b_re[i], ssm_b_im[i],
                           ssm_c_re[i], ssm_c_im[i], ssm_d[i], ssm_w_glu[i])
        attn_out = sliding_window_attention(
            q.reshape(bsz, seq, N_Q_HEADS, HEAD_DIM),
            k.reshape(bsz, seq, N_KV_HEADS, HEAD_DIM),
            v.reshape(bsz, seq, N_KV_HEADS, HEAD_DIM),
            attn_sinks[i])
        mixed = jnp.concatenate([rms_norm(ssm_out, norm_ssm_out[i]),
                                 rms_norm(attn_out, norm_attn_out[i])], axis=-1)
        h = h + mixed @ w_out[i]
        h = h + swiglu(rms_norm(h, norm_ffn[i]), w_ffn_in[i], w_ffn_out[i])
        gate = jax.nn.sigmoid(rms_norm(h, norm_ple[i]) @ w_ple_gate[i])
        h = h + gate * (p[i] @ w_ple_proj[i])
    return rms_norm(h, norm_final)
```

```python
import contextlib
import math

import numpy as np
import concourse.bass as bass
import concourse.mybir as mybir
from concourse.bass_utils import run_bass_kernel_spmd

F32 = mybir.dt.float32
BF16 = mybir.dt.bfloat16
I32 = mybir.dt.int32
ALU = mybir.AluOpType
AF = mybir.ActivationFunctionType

D_MODEL = 1024
SEQ = 4096
NL = 4
T = 512
NTILES = SEQ // T
FFN = 2816
NG = 32
NCH = T // 8
EPS = 1e-6
TWO_PI = 2.0 * math.pi
MAGIC = 12582912.0
RING_COLS = 6144
NRING = 4

ENGS = ("pe", "act", "dve", "pool", "sp")


class _Op:
    __slots__ = ("eng", "fn", "deps", "dma", "marked", "tok", "raw", "idx")

    def __init__(self, eng, fn, deps, dma):
        self.eng = eng
        self.fn = fn
        self.deps = deps
        self.dma = dma
        self.marked = False
        self.tok = None


class Prog:
    def __init__(self, nc, strict_same=True):
        self.nc = nc
        self.strict_same = strict_same
        self.ops = {e: [] for e in ENGS}
        self.last_write = {}
        self.readers = {}
        self.dma_slots = {}
        self.all_ops = []
        self.pending = {}

    def barrier(self, exclude=("cast",)):
        fr = []
        for e in ENGS:
            lst = [o for o in self.ops[e] if o.dma is None]
            if lst:
                fr.append(lst[-1])
        for s, lst in self.dma_slots.items():
            if not s.startswith(exclude):
                fr.append(lst[-1])
        self.pending = {e: list(fr) for e in ENGS}

    def op(self, eng, fn, reads=(), writes=(), dma=None):
        deps = []
        if self.pending.get(eng):
            deps.extend(self.pending[eng])
            self.pending[eng] = None
        raw = set()
        for k in reads:
            w = self.last_write.get(k)
            if w is not None:
                deps.append(w)
                raw.add(id(w))
        for k in writes:
            w = self.last_write.get(k)
            if w is not None:
                deps.append(w)
            deps.extend(self.readers.get(k, ()))
        o = _Op(eng, fn, deps, dma)
        o.raw = raw
        if dma is not None:
            self.dma_slots.setdefault(dma, []).append(o)
        for k in writes:
            self.last_write[k] = o
            self.readers[k] = []
        for k in reads:
            self.readers.setdefault(k, []).append(o)
        o.idx = len(self.all_ops)
        self.ops[eng].append(o)
        self.all_ops.append(o)
        return o

    def emit(self, final_waits=()):
        nc = self.nc
        for o in self.all_ops:
            nd = []
            seen = set()
            for d in o.deps:
                if d is o or id(d) in seen:
                    continue
                seen.add(id(d))
                if d.dma is None and o.dma is None and d.eng == o.eng:
                    if d.eng in ("pe", "sp") or not self.strict_same:
                        continue
                    if id(d) not in o.raw:
                        continue
                nd.append(d)
            best = {}
            for d in nd:
                g = ("d", d.dma) if d.dma is not None else ("e", d.eng)
                if g not in best or best[g].idx < d.idx:
                    best[g] = d
            o.deps = list(best.values())
            for d in o.deps:
                d.marked = True
        for o in final_waits:
            o.marked = True
        for e in ENGS:
            c = 0
            for o in self.ops[e]:
                if o.dma is None and o.marked:
                    c += 1
                    o.tok = (("e", e), c)
        for s, lst in self.dma_slots.items():
            c = 0
            for o in lst:
                c += 16
                o.tok = (("d", s), c)
        semnames = [("e", e) for e in ENGS] + [("d", s) for s in self.dma_slots]
        assert len(semnames) < 140, len(semnames)
        with contextlib.ExitStack() as st:
            sems = {}
            for i, k in enumerate(semnames):
                sems[k] = st.enter_context(nc.semaphore("s%d" % i))
            block = st.enter_context(nc.Block())
            prog = self

            def run(e, eng):
                waited = {}
                for o in prog.ops[e]:
                    need = {}
                    for d in o.deps:
                        k, v = d.tok
                        if need.get(k, 0) < v:
                            need[k] = v
                    for k, v in need.items():
                        if waited.get(k, 0) < v:
                            eng.wait_ge(sems[k], v)
                            waited[k] = v
                    ins = o.fn(eng)
                    if o.dma is not None:
                        ins.then_inc(sems[o.tok[0]], 16)
                    elif o.marked:
                        ins.then_inc(sems[o.tok[0]], 1)
                if e == "sp":
                    for o in final_waits:
                        k, v = o.tok
                        eng.wait_ge(sems[k], v)

            @block.tensor
            def _(eng):
                run("pe", eng)

            @block.scalar
            def _(eng):
                run("act", eng)

            @block.vector
            def _(eng):
                run("dve", eng)

            @block.gpsimd
            def _(eng):
                run("pool", eng)

            @block.sync
            def _(eng):
                run("sp", eng)


def _chunk_table():
    ch = [("inU", 8, 512), ("inQ", 8, 512), ("inKV", 8, 256), ("glu", 4, 512),
          ("out0", 8, 512), ("out1", 8, 512)]
    ch += [("ffi%d" % c, 8, 512) for c in range(11)]
    ch += [("ffo%d" % c, 22, 256) for c in range(4)]
    ch += [("gate0", 8, 512), ("gate1", 8, 512), ("proj", 2, 1024)]
    offs = {}
    o = 0
    for n, kt, nc_ in ch:
        offs[n] = (o, kt, nc_)
        o += kt * nc_
    return ch, offs, o


CHUNKS, CH_OFF, WCOLS = _chunk_table()
assert WCOLS == 98304

GV_L = 36
GV_COLS = NL * GV_L + 8
SSMP_L = 32 * (3 + 4 * 16)


def _pack_chunk(Wr, cols):
    kt = Wr.shape[0] // 128
    a = Wr[:, cols].reshape(kt, 128, len(cols)).transpose(1, 0, 2)
    return a.reshape(128, kt * len(cols))


def _host_pack(inp):
    f32 = np.float32
    wpack = np.empty((NL, 128, WCOLS), f32)
    qcols = np.concatenate([np.concatenate([512 + 64 * i + np.arange(64),
                                            512 + 64 * (4 + i) + np.arange(64)]) for i in range(4)])
    orow = np.concatenate([np.arange(512)] +
                          [np.concatenate([512 + 64 * i + np.arange(64),
                                           512 + 64 * (4 + i) + np.arange(64)]) for i in range(4)])
    for l in range(NL):
        parts = []
        w_in = np.asarray(inp["w_in"][l], f32)
        parts.append(_pack_chunk(w_in, np.arange(512)))
        parts.append(_pack_chunk(w_in, qcols))
        parts.append(_pack_chunk(w_in, np.arange(1024, 1280)))
        parts.append(_pack_chunk(np.asarray(inp["ssm_w_glu"][l], f32), np.arange(512)))
        w_out = np.asarray(inp["w_out"][l], f32)[orow]
        parts.append(_pack_chunk(w_out, np.arange(512)))
        parts.append(_pack_chunk(w_out, np.arange(512, 1024)))
        wfi = np.asarray(inp["w_ffn_in"][l], f32)
        for c in range(11):
            cols = np.concatenate([128 * (2 * c) + np.arange(128), FFN + 128 * (2 * c) + np.arange(128),
                                   128 * (2 * c + 1) + np.arange(128), FFN + 128 * (2 * c + 1) + np.arange(128)])
            parts.append(_pack_chunk(wfi, cols))
        wfo = np.asarray(inp["w_ffn_out"][l], f32)
        for c in range(4):
            parts.append(_pack_chunk(wfo, 256 * c + np.arange(256)))
        wg = np.asarray(inp["w_ple_gate"][l], f32)
        parts.append(_pack_chunk(wg, np.arange(512)))
        parts.append(_pack_chunk(wg, np.arange(512, 1024)))
        parts.append(_pack_chunk(np.asarray(inp["w_ple_proj"][l], f32), np.arange(1024)))
        wpack[l] = np.concatenate(parts, axis=1)

    def fm(v):
        return np.asarray(v, f32).reshape(-1, 128).T

    aperm = np.stack([np.concatenate([64 * i + np.arange(64), 64 * (4 + i) + np.arange(64)]) for i in range(4)])
    gv = np.empty((128, GV_COLS), f32)
    for l in range(NL):
        b = l * GV_L
        gv[:, b:b + 8] = fm(inp["norm_mix"][l])
        gv[:, b + 8:b + 16] = fm(inp["norm_ffn"][l])
        gv[:, b + 16:b + 24] = fm(inp["norm_ple"][l])
        gv[:, b + 24:b + 28] = fm(inp["norm_ssm_out"][l])
        gv[:, b + 28:b + 32] = np.asarray(inp["norm_attn_out"][l], f32)[aperm].T
        gv[:, b + 32:b + 36] = fm(inp["ssm_d"][l])
    gv[:, NL * GV_L:] = fm(inp["norm_final"])
    sk = np.asarray(inp["attn_sinks"], f32).reshape(1, NL * 8)
    ssmp = np.empty((128, NL, SSMP_L), f32)
    for l in range(NL):
        ar = np.asarray(inp["ssm_a_re"][l], f32).T
        ai = np.asarray(inp["ssm_a_im"][l], f32).T
        ld = np.broadcast_to(np.asarray(inp["ssm_log_dt"][l], f32)[None, :], (64, 32))
        bre = np.asarray(inp["ssm_b_re"][l], f32).transpose(1, 0, 2).reshape(64, 512)
        bim = np.asarray(inp["ssm_b_im"][l], f32).transpose(1, 0, 2).reshape(64, 512)
        cre = np.asarray(inp["ssm_c_re"][l], f32).transpose(2, 0, 1).reshape(64, 512)
        cim = np.asarray(inp["ssm_c_im"][l], f32).transpose(2, 0, 1).reshape(64, 512)
        row = np.concatenate([ar, ai, ld, bre, bim, cre, cim], axis=1)
        ssmp[0:64, l] = row
        ssmp[64:128, l] = row
    return wpack, gv, sk, ssmp


ATT_DELAY = 6
INCR = False


def build(nl=NL, ntiles=NTILES, dbg=None, strict_same=True):
    nc = bass.Bass("TRN2", target_bir_lowering=False)
    seq = ntiles * T
    xT = nc.dram_tensor("xT", [D_MODEL, seq], F32, kind="ExternalInput").ap()
    pT = nc.dram_tensor("pT", [nl, 256, seq], F32, kind="ExternalInput").ap()
    wpack = nc.dram_tensor("wpack", [nl, 128, WCOLS], F32, kind="ExternalInput").ap()
    gvd = nc.dram_tensor("gv", [128, GV_COLS], F32, kind="ExternalInput").ap()
    skd = nc.dram_tensor("sk", [1, NL * 8], F32, kind="ExternalInput").ap()
    ssmpd = nc.dram_tensor("ssmp", [128, NL, SSMP_L], F32, kind="ExternalInput").ap()
    outT = nc.dram_tensor("outT", [D_MODEL, seq], F32, kind="ExternalOutput").ap()
    WS = nc.dram_tensor("WS", [nl, 128, WCOLS], BF16, kind="Internal").ap()
    SSMC = nc.dram_tensor("SSMC", [nl, 4, 128, RING_COLS], BF16, kind="Internal").ap()
    TAB = nc.dram_tensor("TAB", [nl, 128, 2 * NG * NCH], BF16, kind="Internal").ap()
    dbg = dbg or {}
    dbg_out = {}
    for name, shape in dbg.items():
        dbg_out[name] = nc.dram_tensor("dbg_" + name, list(shape), F32, kind="ExternalOutput").ap()

    st = contextlib.ExitStack()

    def sb(name, shape, dt=F32):
        return st.enter_context(nc.sbuf_tensor(name, list(shape), dt))

    with st:
        P = Prog(nc, strict_same=strict_same)
        finals = []
        ARENA_BYTES = 196 * 1024
        arena = sb("arena", [128, ARENA_BYTES // 4], F32)

        class Carver:
            def __init__(self):
                self.off = 0

            def __call__(self, shape, dt=F32, parts=128):
                esz = 4 if dt == F32 else 2
                n = 1
                for d in shape[1:]:
                    n *= d
                nbytes = (n * esz + 3) // 4 * 4
                a = arena[0:shape[0], self.off // 4:(self.off + nbytes) // 4]
                self.off += nbytes
                assert self.off <= ARENA_BYTES, self.off
                if dt != F32:
                    a = a.bitcast(dt)
                if len(shape) > 2:
                    names = " ".join("d%d" % i for i in range(1, len(shape)))
                    kw = {"d%d" % i: shape[i] for i in range(1, len(shape) - 1)}
                    a = a.rearrange("p (%s) -> p %s" % (names, names), **kw)
                return a

        gv = sb("gvs", [128, GV_COLS])
        gs4 = sb("gs4", [128, NL, 4])
        Rsb = sb("Rsb", [128, NL, NG])
        skeb = sb("skeb", [1, NL * 8], BF16)
        sks = sb("sks", [1, NL * 8])
        ske = sb("ske", [1, NL * 8])
        ones10 = sb("ones10", [128, 128], BF16)
        ones13 = sb("ones13", [128, 128], BF16)
        ones9 = sb("ones9", [128, 128], BF16)
        OA = sb("OA", [128, 128], BF16)
        OB = sb("OB", [128, 128], BF16)
        orow = sb("orow", [1, 2, 128], BF16)
        maskC = sb("maskC", [128, 128], BF16)
        maskP = sb("maskP", [128, 128], BF16)
        identb = sb("identb", [128, 128], BF16)
        mneg = sb("mneg", [128, 2, 4, 128], BF16)
        iot = sb("iot", [128, 128])
        kcar = sb("kcar", [128, NL, 128], BF16)
        vcar = sb("vcar", [128, NL, 256], BF16)
        Zcar = sb("Zcar", [128, NL, NG])
        sgnc = sb("sgnc", [128, 2])
        epsc = sb("epsc", [128, 1])
        perm = sb("perm", [128, 128])
        c64 = sb("c64", [128, NL, 2, NG])
        zrot = sb("zrot", [128, 2, NG])

        cm = Carver()
        hT = cm([128, 8, T])
        hn = cm([128, 8, T], BF16)
        mixed = hn
        sq = cm([128, 4, T], BF16)
        msb = cm([128, T])
        rstd = cm([128, T])
        uT = cm([128, 4, T], BF16)
        qT = cm([128, 4, T], BF16)
        kT = cm([128, 128 + T], BF16)
        Vx = cm([128, 5, 2, 128], BF16)
        hid = cm([128, 22, T], BF16)
        pTs = cm([128, 2, T], BF16)
        ring = cm([128, NRING, RING_COLS], BF16)
        tabs = cm([128, 2 * NG * NCH], BF16)
        ysb = cm([128, 2, T])
        t1 = cm([128, 2, T])
        gl2 = cm([128, 4, T])
        atf = cm([128, 4, T])
        gb = cm([128, 4, T], BF16)
        srot = cm([128, 2, 2, 4 * NCH])
        Zb = cm([128, 2, 16, NCH + 1])
        ZCb = cm([128, 2, 16, NCH], BF16)
        ZSb = cm([128, 2, 16, NCH], BF16)
        Ycs = cm([64, 2, 8, 8, 16], BF16)
        Pb = cm([128, 2, 2, 2, T], BF16)
        rden = cm([128, T])
        sgt = cm([128, 2, T], BF16)
        tht = cm([128, 2, T])
        ott = cm([128, 1, T])
        main_bytes = cm.off
        print('arena main bytes', main_bytes, 'of', ARENA_BYTES)

        cp = Carver()
        ssmp = cp([128, SSMP_L])
        I2 = cp([128, 64])
        bmask = cp([128, 128])
        cidx = cp([128, NCH])
        K07 = cp([128, 8])
        Kv = cp([128, 4, 8])
        offc = cp([128, 8])
        NLG = nl * NG
        ssm3 = cp([128, 3, NL, 32])
        sv = cp([128, 24, NLG])
        trg = cp([128, 8, NLG, 8])
        ttmp = cp([128, 4, NLG, 8])
        LBu = cp([128, NG, 16])
        LBp = cp([128, 2, NG, 32])
        CL = cp([128, 8, NG, 16])
        cltmp = cp([128, 2, 4, NG, 16])
        DS = cp([128, 2, 8, 64])
        CST = cp([128, 2, RING_COLS], BF16)
        vtmp = cp([128, 2, 8, 8, 16])
        tg = cp([128, 4, NG, NCH])
        tgo = cp([128, 1, 2, NG * NCH], BF16)
        thr = cp([128, NLG])
        pz1 = cp([128, 128])
        pz2 = cp([128, 128])

        banks = [st.enter_context(nc.psum_tensor("pb%d" % i, [128, 512], F32)) for i in range(8)]

        class RR:
            def __init__(self, ids):
                self.ids = ids
                self.i = 0

            def next(self):
                b = self.ids[self.i % len(self.ids)]
                self.i += 1
                return b

        mmp = RR([0, 1, 2, 3, 4, 5, 6])
        NB = 7
        auxp = mmp
        pA = RR([0, 1, 2, 3])
        pB = RR([4, 5, 6, 7])

        def bk(b):
            return ("ps", b)

        def dve(fn, reads, writes):
            return P.op("dve", fn, reads, writes)

        def act(fn, reads, writes):
            return P.op("act", fn, reads, writes)

        def pe(fn, reads, writes):
            return P.op("pe", fn, reads, writes)

        def pool(fn, reads, writes):
            return P.op("pool", fn, reads, writes)

        tapn = [0]

        def tap(name, src_ap, reads):
            if name in dbg_out:
                tapn[0] += 1
                o = P.op("pool", lambda e: e.dma_start(out=dbg_out[name], in_=src_ap), reads=reads, dma="dbg%d" % tapn[0])
                finals.append(o)

        P.op("sp", lambda e: e.dma_start(out=gv[:], in_=gvd), writes=["gv"], dma="c0")
        P.op("sp", lambda e: e.dma_start(out=sks[:], in_=skd), writes=["sks"], dma="c1")
        for t_, key, v in ((ones10, "ones10", 2.0 ** -10), (ones13, "ones13", 2.0 ** -13), (ones9, "ones9", 2.0 ** -9)):
            pool(lambda e, t_=t_, v=v: e.memset(t_[:], v), [], [key])
        pool(lambda e: e.memset(OA[:], 0.0), [], ["OA"])
        pool(lambda e: e.memset(OB[:], 0.0), [], ["OB"])
        pool(lambda e: e.memset(OA[:, 0:64], 1.0), [], ["OA"])
        pool(lambda e: e.memset(OB[:, 64:128], 1.0), [], ["OB"])
        pool(lambda e: e.memset(orow[:], 0.0), [], ["orow"])
        pool(lambda e: e.memset(orow[:, 0, 0:64], 1.0), [], ["orow"])
        pool(lambda e: e.memset(orow[:, 1, 64:128], 1.0), [], ["orow"])
        pool(lambda e: e.memset(epsc[:], EPS), [], ["epsc"])
        pool(lambda e: e.memset(vcar[:], 0.0), [], ["vcar"])
        pool(lambda e: e.memset(kcar[:], 0.0), [], ["kcar"])
        pool(lambda e: e.memset(Zcar[:], 0.0), [], ["Zcar"])
        pool(lambda e: e.memset(LBp.rearrange("p a g c -> p (a g c)"), 0.0), [], ["LBp"])
        pool(lambda e: e.iota(iot[:], pattern=[[1, 128]], base=0, channel_multiplier=-1,
                              allow_small_or_imprecise_dtypes=True), [], ["iot"])
        dve(lambda e: e.tensor_scalar(out=maskC[:], in0=iot[:], scalar1=0.0, scalar2=None, op0=ALU.is_ge), ["iot"], ["maskC"])
        dve(lambda e: e.tensor_scalar(out=maskP[:], in0=iot[:], scalar1=0.0, scalar2=None, op0=ALU.is_lt), ["iot"], ["maskP"])
        dve(lambda e: e.tensor_scalar(out=identb[:], in0=iot[:], scalar1=0.0, scalar2=None, op0=ALU.is_equal),
            ["iot"], ["identb"])
        for i_ in range(4):
            dve(lambda e, i_=i_: e.tensor_scalar(out=mneg[:, 0, i_, :], in0=iot[:], scalar1=0.0, scalar2=-30000.0, op0=ALU.is_lt, op1=ALU.mult),
                ["iot"], ["mneg"])
            dve(lambda e, i_=i_: e.tensor_scalar(out=mneg[:, 1, i_, :], in0=iot[:], scalar1=0.0, scalar2=-30000.0, op0=ALU.is_ge, op1=ALU.mult),
                ["iot"], ["mneg"])
        dve(lambda e: e.tensor_scalar(out=I2, in0=iot[:, 0:64], scalar1=0.0, scalar2=None, op0=ALU.is_equal), ["iot"], ["I2"])
        dve(lambda e: e.tensor_scalar(out=perm[:], in0=iot[:], scalar1=64.0, scalar2=None, op0=ALU.is_equal), ["iot"], ["perm"])
        dve(lambda e: e.tensor_scalar(out=pz2, in0=iot[:], scalar1=-64.0, scalar2=None, op0=ALU.is_equal), ["iot"], ["pz2"])
        dve(lambda e: e.tensor_tensor(out=perm[:], in0=perm[:], in1=pz2, op=ALU.add), ["perm", "pz2"], ["perm"])
        dve(lambda e: e.tensor_scalar(out=pz1[:, 0:64], in0=iot[:, 0:64], scalar1=-64.0, scalar2=None, op0=ALU.is_equal),
            ["iot"], ["pz1"])
        dve(lambda e: e.tensor_tensor(out=I2, in0=I2, in1=pz1[:, 0:64], op=ALU.add), ["I2", "pz1"], ["I2"])
        pool(lambda e: e.iota(bmask.rearrange("p (g h) -> p g h", h=16), pattern=[[-16, 8], [0, 16]], base=0,
                              channel_multiplier=1, allow_small_or_imprecise_dtypes=True), [], ["bmask"])
        dve(lambda e: e.tensor_scalar(out=pz1, in0=bmask, scalar1=0.0, scalar2=None, op0=ALU.is_ge), ["bmask", "I2"], ["pz1"])
        dve(lambda e: e.tensor_scalar(out=pz2, in0=bmask, scalar1=15.0, scalar2=None, op0=ALU.is_le), ["bmask"], ["pz2"])
        dve(lambda e: e.tensor_tensor(out=bmask, in0=pz1, in1=pz2, op=ALU.mult), ["pz1", "pz2"], ["bmask"])
        pool(lambda e: e.iota(cidx, pattern=[[1, NCH]], base=0, channel_multiplier=0,
                              allow_small_or_imprecise_dtypes=True), [], ["cidx"])
        pool(lambda e: e.iota(K07, pattern=[[1, 8]], base=0, channel_multiplier=0,
                              allow_small_or_imprecise_dtypes=True), [], ["K07"])
        dve(lambda e: e.tensor_copy(out=Kv[:, 0, :], in_=K07), ["K07"], ["Kv"])
        dve(lambda e: e.tensor_scalar(out=Kv[:, 1, :], in0=K07, scalar1=-1.0, scalar2=7.0, op0=ALU.mult, op1=ALU.add), ["K07"], ["Kv"])
        dve(lambda e: e.tensor_scalar(out=Kv[:, 2, :], in0=K07, scalar1=-7.0, scalar2=None, op0=ALU.add), ["K07"], ["Kv"])
        dve(lambda e: e.tensor_scalar(out=Kv[:, 3, :], in0=K07, scalar1=1.0, scalar2=None, op0=ALU.add), ["K07"], ["Kv"])
        HP = math.pi / 2
        offs = [(HP, 0.0), (math.pi, HP), (HP, math.pi), (math.pi, 3 * HP), (math.pi, HP), (3 * HP, math.pi)]
        for i, (a, b) in enumerate(offs):
            pool(lambda e, i=i, a=a: e.memset(offc[0:64, i:i + 1], a), [], ["offc"])
            pool(lambda e, i=i, b=b: e.memset(offc[64:128, i:i + 1], b), [], ["offc"])
        pool(lambda e: e.memset(sgnc[:, 0:1], 1.0), [], ["sgnc"])
        pool(lambda e: e.memset(sgnc[64:128, 0:1], -1.0), [], ["sgnc"])
        pool(lambda e: e.memset(sgnc[:, 1:2], HP), [], ["sgnc"])
        act(lambda e: e.activation(out=ske[:], in_=sks[:], func=AF.Exp), ["sks"], ["ske"])
        dve(lambda e: e.tensor_copy(out=skeb[:], in_=ske[:]), ["ske"], ["skeb"])
        for l in range(nl):
            dve(lambda e, l=l: e.tensor_scalar(out=gs4[:, l, :], in0=gv[:, l * GV_L + 24:l * GV_L + 28], scalar1=0.25,
                                               scalar2=None, op0=ALU.mult), ["gv"], ["gs4"])

        CAST_PIECE = 16384

        def cast_layer(l):
            for i_, c0 in enumerate(range(0, WCOLS, CAST_PIECE)):
                pace = []
                if l > 0:
                    pace = [("SSMC", l - 1, i_)] if i_ < 4 else [("TAB", l - 1)]
                P.op("pool", lambda e, l=l, c0=c0: e.dma_start(out=WS[l, :, c0:c0 + CAST_PIECE],
                                                                 in_=wpack[l, :, c0:c0 + CAST_PIECE]),
                     reads=pace, writes=[("WS", l, c0 // CAST_PIECE)], dma="cast%d" % ((c0 // CAST_PIECE) % 3))
        cast_layer(0)

        def rsin(out_ap, in_ap, off, tmpa, tmpb, rkeys, wkeys, tkeys):
            if off is None:
                src = in_ap
            else:
                dve(lambda e: e.tensor_scalar(out=tmpb, in0=in_ap, scalar1=off, scalar2=None, op0=ALU.add), rkeys + tkeys, tkeys)
                src = tmpb
            dve(lambda e: e.tensor_scalar(out=tmpa, in0=src, scalar1=1.0 / TWO_PI, scalar2=MAGIC, op0=ALU.mult, op1=ALU.add),
                rkeys + tkeys, tkeys)
            dve(lambda e: e.tensor_scalar(out=tmpa, in0=tmpa, scalar1=-MAGIC, scalar2=-TWO_PI, op0=ALU.add, op1=ALU.mult),
                tkeys, tkeys)
            dve(lambda e: e.tensor_tensor(out=tmpa, in0=tmpa, in1=src, op=ALU.add), rkeys + tkeys, tkeys)
            act(lambda e: e.activation(out=out_ap, in_=tmpa, func=AF.Sin), tkeys + rkeys, wkeys)

        def svv(i):
            return sv[:, i, :]

        def prepass_small():
            for q_ in range(3):
                P.op("sp", lambda e, q_=q_: e.dma_start(out=ssm3[:, q_, 0:nl, :], in_=ssmpd[:, 0:nl, q_ * 32:(q_ + 1) * 32]),
                     writes=["ssm3"], dma="c3")
            ar = ssm3[:, 0, 0:nl, :].rearrange("p l g -> p (l g)")
            ai = ssm3[:, 1, 0:nl, :].rearrange("p l g -> p (l g)")
            ldt = ssm3[:, 2, 0:nl, :].rearrange("p l g -> p (l g)")
            S = ["sv"]
            SP_ = ["ssm3", "sv"]
            act(lambda e: e.activation(out=svv(0), in_=ldt, func=AF.Exp), SP_, S)
            dve(lambda e: e.tensor_tensor(out=svv(1), in0=ar, in1=svv(0), op=ALU.mult), SP_, S)
            dve(lambda e: e.tensor_tensor(out=svv(2), in0=ai, in1=svv(0), op=ALU.mult), SP_, S)
            act(lambda e: e.activation(out=svv(3), in_=svv(1), func=AF.Exp), S, S)
            act(lambda e: e.activation(out=Rsb[:, 0:nl, :].rearrange("p l g -> p (l g)"), in_=svv(1), func=AF.Exp, scale=8.0), S, ["Rsb"])
            rsin(svv(4), svv(2), HP, svv(12), svv(13), S, S, S)
            rsin(svv(5), svv(2), None, svv(12), svv(13), S, S, S)
            dve(lambda e: e.tensor_tensor(out=svv(6), in0=svv(3), in1=svv(4), op=ALU.mult), S, S)
            dve(lambda e: e.tensor_tensor(out=svv(7), in0=svv(3), in1=svv(5), op=ALU.mult), S, S)
            dve(lambda e: e.tensor_tensor(out=svv(8), in0=ar, in1=ar, op=ALU.mult), SP_, S)
            dve(lambda e: e.tensor_tensor(out=svv(12), in0=ai, in1=ai, op=ALU.mult), SP_, S)
            dve(lambda e: e.tensor_tensor(out=svv(8), in0=svv(8), in1=svv(12), op=ALU.add), S, S)
            dve(lambda e: e.reciprocal(out=svv(8), in_=svv(8)), S, S)
            dve(lambda e: e.tensor_scalar(out=svv(9), in0=svv(6), scalar1=-1.0, scalar2=None, op0=ALU.add), S, S)
            dve(lambda e: e.tensor_tensor(out=svv(12), in0=svv(9), in1=ar, op=ALU.mult), SP_, S)
            dve(lambda e: e.tensor_tensor(out=svv(13), in0=svv(7), in1=ai, op=ALU.mult), SP_, S)
            dve(lambda e: e.tensor_tensor(out=svv(12), in0=svv(12), in1=svv(13), op=ALU.add), S, S)
            dve(lambda e: e.tensor_tensor(out=svv(10), in0=svv(12), in1=svv(8), op=ALU.mult), S, S)
            dve(lambda e: e.tensor_tensor(out=svv(12), in0=svv(7), in1=ar, op=ALU.mult), SP_, S)
            dve(lambda e: e.tensor_tensor(out=svv(13), in0=svv(9), in1=ai, op=ALU.mult), SP_, S)
            dve(lambda e: e.tensor_tensor(out=svv(12), in0=svv(12), in1=svv(13), op=ALU.subtract), S, S)
            dve(lambda e: e.tensor_tensor(out=svv(11), in0=svv(12), in1=svv(8), op=ALU.mult), S, S)
            dve(lambda e: e.tensor_copy(out=sv[0:64, 14, :], in_=sv[0:64, 10, :]), S, S)
            dve(lambda e: e.tensor_scalar(out=sv[64:128, 14, :], in0=sv[64:128, 11, :], scalar1=-1.0, scalar2=None, op0=ALU.mult), S, S)
            dve(lambda e: e.tensor_copy(out=sv[0:64, 15, :], in_=sv[0:64, 11, :]), S, S)
            dve(lambda e: e.tensor_copy(out=sv[64:128, 15, :], in_=sv[64:128, 10, :]), S, S)
            dve(lambda e: e.tensor_scalar(out=sv[0:64, 16, :], in0=sv[0:64, 10, :], scalar1=-1.0, scalar2=None, op0=ALU.mult), S, S)
            dve(lambda e: e.tensor_copy(out=sv[64:128, 16, :], in_=sv[64:128, 11, :]), S, S)

            spec = [(1, 1, 0), (0, 0, 0), (0, 0, 1), (3, 2, 2), (3, 2, 3), (3, 2, 4), (3, 2, 5)]
            TT_ = ["ttmp"]

            def kb(i):
                return Kv[:, i, :].unsqueeze(1).to_broadcast([128, NLG, 8])

            def gb8(i):
                return sv[:, i, :].unsqueeze(2).to_broadcast([128, NLG, 8])
            for ti, (ke, ka, oi) in enumerate(spec):
                dve(lambda e, ke=ke: e.tensor_tensor(out=ttmp[:, 0], in0=gb8(1), in1=kb(ke), op=ALU.mult), S + ["Kv"] + TT_, TT_)
                act(lambda e: e.activation(out=ttmp[:, 0], in_=ttmp[:, 0], func=AF.Exp), TT_, TT_)
                dve(lambda e, ka=ka: e.tensor_tensor(out=ttmp[:, 1], in0=gb8(2), in1=kb(ka), op=ALU.mult), S + ["Kv"] + TT_, TT_)
                rsin(ttmp[:, 1], ttmp[:, 1], offc[:, oi:oi + 1], ttmp[:, 2], ttmp[:, 3], ["offc"], TT_, TT_)
                dve(lambda e, ti=ti: e.tensor_tensor(out=trg[:, ti], in0=ttmp[:, 0], in1=ttmp[:, 1], op=ALU.mult),
                    TT_, [("trg", ti)])

            dve(lambda e: e.tensor_scalar(out=svv(17), in0=svv(2), scalar1=8.0, scalar2=None, op0=ALU.mult), S, S)
            dve(lambda e: e.tensor_scalar(out=svv(12), in0=svv(17), scalar1=1.0 / TWO_PI, scalar2=MAGIC, op0=ALU.mult, op1=ALU.add), S, S)
            dve(lambda e: e.tensor_scalar(out=svv(12), in0=svv(12), scalar1=-MAGIC, scalar2=-TWO_PI, op0=ALU.add, op1=ALU.mult), S, S)
            dve(lambda e: e.tensor_tensor(out=thr, in0=svv(12), in1=svv(17), op=ALU.add), S + ["thr"], ["thr"])
            dve(lambda e: e.tensor_scalar(out=svv(18), in0=thr, scalar1=float(NCH), scalar2=None, op0=ALU.mult), ["thr"] + S, S)
            dve(lambda e: e.tensor_scalar(out=svv(12), in0=svv(18), scalar1=1.0 / TWO_PI, scalar2=MAGIC, op0=ALU.mult, op1=ALU.add), S, S)
            dve(lambda e: e.tensor_scalar(out=svv(12), in0=svv(12), scalar1=-MAGIC, scalar2=-TWO_PI, op0=ALU.add, op1=ALU.mult), S, S)
            dve(lambda e: e.tensor_tensor(out=svv(12), in0=svv(12), in1=svv(18), op=ALU.add), S, S)
            act(lambda e: e.activation(out=svv(13), in_=svv(12), func=AF.Abs), S, S)
            act(lambda e: e.activation(out=c64[:, 0:nl, 0, :], in_=svv(13).rearrange("p (l g) -> p l g", l=nl), func=AF.Sin, scale=-1.0, bias=sgnc[:, 1:2]), S + ["sgnc"], ["c64"])
            act(lambda e: e.activation(out=svv(13), in_=svv(12), func=AF.Sin, scale=sgnc[:, 0:1]), S + ["sgnc"], S)
            dve(lambda e: e.tensor_scalar(out=c64[:, 0:nl, 1, :], in0=svv(13).rearrange("p (l g) -> p l g", l=nl), scalar1=-1.0, scalar2=None, op0=ALU.mult), S, ["c64"])

        def prepass(l):
            P.op("sp", lambda e: e.dma_start(out=ssmp, in_=ssmpd[:, l, :]), writes=["ssmp"], dma="c2")
            bre = ssmp[:, 96:608].rearrange("p (g h) -> p g h", h=16)
            bim = ssmp[:, 608:1120].rearrange("p (g h) -> p g h", h=16)
            cre = ssmp[:, 1120:1632].rearrange("p (g h) -> p g h", h=16)
            cim = ssmp[:, 1632:2144].rearrange("p (g h) -> p g h", h=16)
            S = ["sv"]
            SP_ = ["ssmp", "sv"]
            def bc_h(i):
                return sv[:, i, l * NG:(l + 1) * NG].unsqueeze(2).to_broadcast([128, NG, 16])
            ta = cltmp[:, 0, 0]
            tb = cltmp[:, 1, 0]
            C_ = ["cltmp"]
            dve(lambda e: e.tensor_tensor(out=ta, in0=bre, in1=bc_h(14), op=ALU.mult), SP_ + C_, C_)
            dve(lambda e: e.tensor_tensor(out=tb, in0=bim, in1=bc_h(15), op=ALU.mult), SP_ + C_, C_)
            dve(lambda e: e.tensor_tensor(out=LBu, in0=ta, in1=tb, op=ALU.subtract), C_, ["LBu"])
            LBuv = LBu.rearrange("p (a e) h -> p a e h", e=2)
            for e_ in range(2):
                dve(lambda e, e_=e_: e.tensor_copy(
                    out=LBp[:, 0].rearrange("p (a e) (f h) -> p a e f h", e=2, h=16)[:, :, e_, e_, :],
                    in_=LBuv[:, :, e_, :]), ["LBu"], ["LBp"])
            dve(lambda e: e.tensor_tensor(out=ta, in0=bre, in1=bc_h(15), op=ALU.mult), SP_ + C_, C_)
            dve(lambda e: e.tensor_tensor(out=tb, in0=bim, in1=bc_h(16), op=ALU.mult), SP_ + C_, C_)
            dve(lambda e: e.tensor_tensor(out=ta, in0=ta, in1=tb, op=ALU.subtract), C_, C_)
            tav = ta.rearrange("p (a e) h -> p a e h", e=2)
            for e_ in range(2):
                dve(lambda e, e_=e_: e.tensor_copy(
                    out=LBp[:, 1].rearrange("p (a e) (f h) -> p a e f h", e=2, h=16)[:, :, e_, e_, :],
                    in_=tav[:, :, e_, :]), C_, ["LBp"])
            def c_bc(c):
                return c.unsqueeze(1).to_broadcast([128, 4, NG, 16])

            def t_bc(ti, hf):
                return trg[:, ti, l * NG:(l + 1) * NG, 4 * hf:4 * hf + 4].rearrange("p g k -> p k g").unsqueeze(3).to_broadcast([128, 4, NG, 16])
            for hf in range(2):
                dve(lambda e, hf=hf: e.tensor_tensor(out=cltmp[:, 0], in0=c_bc(cre), in1=t_bc(1, hf), op=ALU.mult), ["ssmp", ("trg", 1), "LBp"] + C_, C_)
                dve(lambda e, hf=hf: e.tensor_tensor(out=cltmp[:, 1], in0=c_bc(cim), in1=t_bc(2, hf), op=ALU.mult), ["ssmp", ("trg", 2)] + C_, C_)
                dve(lambda e, hf=hf: e.tensor_tensor(out=CL[:, 4 * hf:4 * hf + 4], in0=cltmp[:, 0], in1=cltmp[:, 1], op=ALU.add), C_, ["CL"])

            for pt in range(4):
                cs = pt % 2
                CK = ("CST", cs)
                cst = CST[:, cs]
                for half in range(2):
                    b = mmp.next()
                    for tt in range(4):
                        tau = half * 4 + tt
                        pe(lambda e, b=b, tt=tt, tau=tau, pt=pt: e.matmul(
                            banks[b][:, tt * 128:(tt + 1) * 128],
                            lhsT=LBu[:, 8 * pt:8 * pt + 8, :].rearrange("p g h -> p (g h)"),
                            rhs=CL[:, tau, 8 * pt:8 * pt + 8, :].rearrange("p g h -> p (g h)"),
                            start=True, stop=True), ["LBu", "CL"], [bk(b)])
                    dve(lambda e, b=b, half=half, cst=cst: e.tensor_tensor(
                        out=cst[:, half * 512:(half + 1) * 512].rearrange("p (t n) -> p t n", n=128),
                        in0=banks[b][:].rearrange("p (t n) -> p t n", n=128),
                        in1=bmask.unsqueeze(1).to_broadcast([128, 4, 128]), op=ALU.mult), [bk(b), "bmask"], [CK])
                for e_ in range(2):
                    bx = mmp.next()
                    by = mmp.next()
                    for q in range(4):
                        g = 8 * pt + 2 * q + e_
                        ds = q % 2
                        dve(lambda e, ds=ds, g=g: e.tensor_tensor(
                            out=DS[:, ds], in0=I2.unsqueeze(1).to_broadcast([128, 8, 64]),
                            in1=trg[:, 0, l * NG + g, :].unsqueeze(2).to_broadcast([128, 8, 64]), op=ALU.mult),
                            ["I2", ("trg", 0)], [("DS", ds)])
                        for v_, b in ((0, bx), (1, by)):
                            pe(lambda e, b=b, v_=v_, g=g, q=q, ds=ds: e.matmul(
                                banks[b][32 * q:32 * q + 32, :], lhsT=LBp[:, v_, g, :],
                                rhs=DS[:, ds].rearrange("p s c -> p (s c)"), start=True, stop=True,
                                tile_position=(0, 32 * q)), ["LBp", ("DS", ds)], [bk(b)])
                    wv = cst[:, 1024:4096].rearrange("p (e s c) -> p e s c", e=2, c=192)
                    act(lambda e, bx=bx, wv=wv, e_=e_: e.copy(out=wv[:, e_, :, 0:64],
                                                               in_=banks[bx][:].rearrange("p (s c) -> p s c", c=64)), [bk(bx)], [CK])
                    dve(lambda e, bx=bx, wv=wv, e_=e_: e.tensor_copy(out=wv[:, e_, :, 128:192],
                                                                     in_=banks[bx][:].rearrange("p (s c) -> p s c", c=64)), [bk(bx)], [CK])
                    act(lambda e, by=by, wv=wv, e_=e_: e.copy(out=wv[:, e_, :, 64:128],
                                                               in_=banks[by][:].rearrange("p (s c) -> p s c", c=64)), [bk(by)], [CK])
                vv = cst[:, 4096:6144].rearrange("p (g v j h) -> p g v j h", v=2, j=8, h=16)
                gsl = slice(8 * pt, 8 * pt + 8)

                def c8(c, gsl=gsl):
                    return c[:, gsl, :].unsqueeze(2).to_broadcast([128, 8, 8, 16])

                def t8(ti, gsl=gsl):
                    return trg[:, ti, l * NG + gsl.start:l * NG + gsl.stop, :].unsqueeze(3).to_broadcast([128, 8, 8, 16])
                for v_, (t_a, t_b) in enumerate(((3, 4), (5, 6))):
                    dve(lambda e, t_a=t_a, c8=c8, t8=t8: e.tensor_tensor(out=vtmp[:, 0], in0=c8(cre), in1=t8(t_a), op=ALU.mult),
                        ["ssmp", ("trg", t_a), "vtmp"], ["vtmp"])
                    dve(lambda e, t_b=t_b, c8=c8, t8=t8: e.tensor_tensor(out=vtmp[:, 1], in0=c8(cim), in1=t8(t_b), op=ALU.mult),
                        ["ssmp", ("trg", t_b), "vtmp"], ["vtmp"])
                    dve(lambda e, v_=v_, vv=vv: e.tensor_tensor(out=vv[:, :, v_], in0=vtmp[:, 0], in1=vtmp[:, 1], op=ALU.add),
                        ["vtmp"], [CK])
                P.op("sp", lambda e, cst=cst, pt=pt: e.dma_start(out=SSMC[l, pt], in_=cst), reads=[CK],
                     writes=[("SSMC", l, pt)], dma="cst%d" % cs)
            TG = ["tg"]
            dve(lambda e: e.tensor_tensor(out=tg[:, 0], in0=thr[:, l * NG:(l + 1) * NG].unsqueeze(2).to_broadcast([128, NG, NCH]),
                                          in1=cidx.unsqueeze(1).to_broadcast([128, NG, NCH]), op=ALU.mult),
                ["thr", "cidx"] + TG, TG)
            dve(lambda e: e.tensor_scalar(out=tg[:, 1], in0=tg[:, 0], scalar1=1.0 / TWO_PI, scalar2=MAGIC, op0=ALU.mult, op1=ALU.add), TG, TG)
            dve(lambda e: e.tensor_scalar(out=tg[:, 1], in0=tg[:, 1], scalar1=-MAGIC, scalar2=-TWO_PI, op0=ALU.add, op1=ALU.mult), TG, TG)
            dve(lambda e: e.tensor_tensor(out=tg[:, 1], in0=tg[:, 1], in1=tg[:, 0], op=ALU.add), TG, TG)
            act(lambda e: e.activation(out=tg[:, 2], in_=tg[:, 1], func=AF.Abs), TG, TG)
            OK_ = ("tgo", 0)
            act(lambda e: e.activation(out=tgo[:, 0, 0], in_=tg[:, 2].rearrange("p g c -> p (g c)"), func=AF.Sin,
                                       scale=-1.0, bias=sgnc[:, 1:2]), TG + ["sgnc"], [OK_])
            act(lambda e: e.activation(out=tgo[:, 0, 1], in_=tg[:, 1].rearrange("p g c -> p (g c)"), func=AF.Sin,
                                       scale=sgnc[:, 0:1]), TG + ["sgnc"], [OK_])
            P.op("sp", lambda e: e.dma_start(out=TAB[l], in_=tgo[:, 0].rearrange("p a n -> p (a n)")),
                 reads=[OK_], writes=[("TAB", l)], dma="tgo0")

        prepass_small()
        for l in range(nl):
            prepass(l)
            if l + 1 < nl:
                cast_layer(l + 1)
        if "KD" in dbg_out:
            pass
        P.barrier()
        pool(lambda e: e.memset(Vx.rearrange("p a b c -> p (a b c)"), 0.0), [], [("Vx", i) for i in range(5)])

        order = ["inU", "inQ", "inKV", "S0", "S1", "S2", "S3", "glu", "out0", "out1"] + \
                ["ffi%d" % c for c in range(11)] + ["ffo%d" % c for c in range(4)] + ["gate0", "proj", "gate1"]
        items = []
        for ti_ in range(ntiles):
            for l in range(nl):
                for nm in order:
                    if nm[0] == "S":
                        pt = int(nm[1])
                        items.append(((ti_, l, nm), SSMC[l, pt], RING_COLS, [("SSMC", l, pt)]))
                    else:
                        o, kt, ncol = CH_OFF[nm]
                        rd = [("WS", l, c) for c in range(o // CAST_PIECE, (o + kt * ncol - 1) // CAST_PIECE + 1)]
                        items.append(((ti_, l, nm), WS[l, :, o:o + kt * ncol], kt * ncol, rd))
        issued = [0]
        free_slots = list(range(NRING))
        slot_of = {}

        def spump():
            while free_slots and issued[0] < len(items):
                k2, src, cols, rd = items[issued[0]]
                sl = free_slots.pop(0)
                slot_of[k2] = sl
                P.op("sp", lambda e, sl=sl, src=src, cols=cols: e.dma_start(out=ring[:, sl, 0:cols], in_=src),
                     reads=rd, writes=[("ring", sl)], dma="ring%d" % sl)
                issued[0] += 1

        def sget(key):
            spump()
            assert key in slot_of, ("ring chunk not resident", key)
            return slot_of[key]

        def srel(key):
            free_slots.append(slot_of.pop(key))
            spump()

        def wtile(slot, ncols, kt, m):
            v = ring[:, slot, :].rearrange("p (k n) -> p k n", n=ncols)
            return v[:, kt, m * 128:(m + 1) * 128]

        sqn = [0]

        def stat_sq(m):
            if not INCR:
                return 0
            sl_ = sqn[0] % 4
            sqn[0] += 1
            act(lambda e, m=m, sl_=sl_: e.activation(out=sq[:, sl_, :], in_=hT[:, m, :], func=AF.Square), [("hT", m)], [("sq", sl_)])
            return sl_

        def stat_mm(sl_, first_, last_):
            if not INCR:
                return
            pe(lambda e, sl_=sl_: e.matmul(banks[NB][:], lhsT=ones10[:], rhs=sq[:, sl_, :], start=first_, stop=last_),
               [("sq", sl_), "ones10"], [bk(NB)])

        def rmsnorm(src_fn, nk, gcol, ones_t, ones_key, dst_fn, srckeys, dstkeys, after=None, bank=None, pre=False):
            pre = pre and INCR
            b = (NB if pre else mmp.next()) if bank is None else bank
            for k in range(0 if pre else nk):
                act(lambda e, k=k: e.activation(out=sq[:, k % 4, :], in_=src_fn(k), func=AF.Square), [srckeys[k]], [("sq", k % 4)])
                pe(lambda e, k=k, b=b: e.matmul(banks[b][:], lhsT=ones_t[:], rhs=sq[:, k % 4, :], start=(k == 0), stop=(k == nk - 1)),
                   [("sq", k % 4), ones_key], [bk(b)])
            act(lambda e, b=b: e.activation(out=msb, in_=banks[b][:], func=AF.Ln, bias=epsc[:, 0:1]), [bk(b), "epsc"], ["msb"])
            act(lambda e: e.activation(out=rstd, in_=msb, func=AF.Exp, scale=-0.5), ["msb"], ["rstd"])
            for k in range(nk):
                dve(lambda e, k=k: e.scalar_tensor_tensor(out=dst_fn(k), in0=src_fn(k), scalar=gcol(k), in1=rstd,
                                                          op0=ALU.mult, op1=ALU.mult),
                    [srckeys[k], "rstd", "gv", "gs4"], [dstkeys[k]])
                if after is not None:
                    after(k)

        HK = [("hT", k) for k in range(8)]
        HNK = [("hn", k) for k in range(8)]

        def layer(ti_, l):
            gvb = l * GV_L
            first = (ti_ == 0)
            tsl = slice(ti_ * T, (ti_ + 1) * T)
            P.op("sp", lambda e: e.dma_start(out=tabs, in_=TAB[l]), reads=[("TAB", l)], writes=["tabs"], dma="tabs")
            P.op("pool", lambda e: e.dma_start(out=pTs, in_=pT[l, :, tsl].rearrange("(k p) t -> p k t", p=128)),
                 writes=["pTs"], dma="pTs")
            dve(lambda e: e.tensor_copy(out=kT[:, 0:128], in_=kcar[:, l, :]), ["kcar"], [("kT", 0)])
            dve(lambda e: e.tensor_copy(out=Vx[:, 0].rearrange("p a n -> p (a n)"), in_=vcar[:, l, :]), ["vcar"], [("Vx", 0)])

            rmsnorm(lambda k: hT[:, k, :], 8, lambda k: gv[:, gvb + k:gvb + k + 1], ones10, "ones10", lambda k: hn[:, k, :], HK, HNK, pre=(l > 0))

            def proj_tiles(name, ntile, dst_fn, dkeys, bp=None):
                s = sget((ti_, l, name))
                _, kt_total, ncols = CH_OFF[name]
                for m in range(ntile):
                    b = (bp or mmp).next()
                    for k in range(8):
                        pe(lambda e, b=b, k=k, m=m, s=s: e.matmul(banks[b][:], lhsT=wtile(s, ncols, k, m), rhs=hn[:, k, :],
                                                                  start=(k == 0), stop=(k == 7)),
                           [("ring", s), ("hn", k)], [bk(b)])
                    if m % 2 == 0:
                        act(lambda e, b=b, m=m: e.copy(out=dst_fn(m), in_=banks[b][:]), [bk(b)], [dkeys[m]])
                    else:
                        dve(lambda e, b=b, m=m: e.tensor_copy(out=dst_fn(m), in_=banks[b][:]), [bk(b)], [dkeys[m]])
                return s
            proj_tiles("inU", 4, lambda m: uT[:, m, :], [("uT", m) for m in range(4)])
            srel((ti_, l, "inU"))

            proj_tiles("inQ", 4, lambda m: qT[:, m, :], [("qT", m) for m in range(4)])
            srel((ti_, l, "inQ"))
            s = proj_tiles("inKV", 1, lambda m: kT[:, 128:128 + T], [("kT", 1)])
            for blk in range(4):
                b = mmp.next()
                for k in range(8):
                    pe(lambda e, b=b, k=k, blk=blk, s=s: e.matmul(
                        banks[b][:, 0:128], lhsT=hn[:, k, blk * 128:(blk + 1) * 128],
                        rhs=ring[:, s, 0:2048].rearrange("p (k n) -> p k n", n=256)[:, k, 128:256],
                        start=(k == 0), stop=(k == 7)), [("ring", s), ("hn", k)], [bk(b)])
                act(lambda e, b=b, blk=blk: e.copy(out=Vx[:, blk + 1, 0, 0:64], in_=banks[b][:, 0:64]), [bk(b)], [("Vx", blk + 1)])
                dve(lambda e, b=b, blk=blk: e.tensor_copy(out=Vx[:, blk + 1, 1, 64:128], in_=banks[b][:, 64:128]), [bk(b)], [("Vx", blk + 1)])
            srel((ti_, l, "inKV"))
            dve(lambda e: e.tensor_copy(out=kcar[:, l, :], in_=kT[:, T:T + 128]), [("kT", 1)], ["kcar"])
            dve(lambda e: e.tensor_copy(out=vcar[:, l, :], in_=Vx[:, 4].rearrange("p a n -> p (a n)")), [("Vx", 4)], ["vcar"])
            if l == 0 and ti_ == 0:
                tap("uT", uT[:, 0, :], [("uT", 0)])

            freeb = list(range(8))

            def balloc():
                while not freeb:
                    yield
                return freeb.pop(0)

            def bfree(b):
                freeb.append(b)

            cosv = tabs[:, 0:NG * NCH].rearrange("p (g c) -> p g c", c=NCH)
            sinv = tabs[:, NG * NCH:2 * NG * NCH].rearrange("p (g c) -> p g c", c=NCH)

            def gsel(tv, pp, q):
                return tv[:, 16 * pp:16 * pp + 16, :].rearrange("p (a r) c -> p a r c", a=2)[:, :, 2 * q:2 * q + 2, :]

            def zsel(tv, pp, q, lo, hi):
                return tv[:, pp, 4 * q:4 * q + 4, lo:hi].rearrange("p (a e) c -> p a e c", a=2)

            def attn_chain():
                for blk in range(4):
                    has_prev = not (first and blk == 0)
                    ps_ = blk % 2
                    sbk = {}
                    ncp = 2 if has_prev else 1
                    for kh in range(2):
                        for cp_ in range(ncp):
                            b = yield from balloc()
                            sbk[(kh, cp_)] = b
                            koff = 128 + blk * 128 - cp_ * 128
                            pe(lambda e, b=b, kh=kh, koff=koff, blk=blk: e.matmul(
                                banks[b][:].rearrange("p (i q) -> p i q", q=128),
                                lhsT=kT[64 * kh:64 * kh + 64, koff:koff + 128],
                                rhs=qT[64 * kh:64 * kh + 64, :, blk * 128:(blk + 1) * 128], start=True, stop=False,
                                tile_position=(64 * kh, 0)),
                               [("kT", 0), ("kT", 1)] + [("qT", m) for m in range(4)], [bk(b)])
                            pe(lambda e, b=b, cp_=cp_: e.matmul(banks[b][:], lhsT=identb[:], rhs=mneg[:, cp_].rearrange("p i q -> p (i q)"),
                                                                start=False, stop=True), ["identb", "mneg"], [bk(b)])
                    yield
                    for kh in range(2):
                        for cp_ in range(ncp):
                            b = sbk[(kh, cp_)]
                            pk = ("Pb", ps_, kh, cp_)
                            act(lambda e, b=b, kh=kh, cp_=cp_, ps_=ps_: e.activation(out=Pb[:, ps_, kh, cp_, :], in_=banks[b][:], func=AF.Exp, scale=0.125),
                                [bk(b)], [pk])
                            bfree(b)
                    yield
                    ob = yield from balloc()
                    db = yield from balloc()
                    seq_ = [(kh, cp_) for kh in range(2) for cp_ in range(ncp)]
                    for n_, (kh, cp_) in enumerate(seq_):
                        pe(lambda e, kh=kh, cp_=cp_, n_=n_, ps_=ps_, blk=blk, ob=ob: e.matmul(
                            banks[ob][:], lhsT=Vx[:, blk + 1 - cp_, kh, :], rhs=Pb[:, ps_, kh, cp_, :],
                            start=(n_ == 0), stop=(n_ == len(seq_) - 1)),
                           [("Pb", ps_, kh, cp_), ("Vx", blk + 1 - cp_)], [bk(ob)])
                    for n_, (kh, cp_) in enumerate(seq_):
                        oo, okey = (OA, "OA") if kh == 0 else (OB, "OB")
                        pe(lambda e, kh=kh, cp_=cp_, n_=n_, oo=oo, ps_=ps_, db=db: e.matmul(
                            banks[db][:], lhsT=oo[:], rhs=Pb[:, ps_, kh, cp_, :], start=(n_ == 0), stop=False),
                           [("Pb", ps_, kh, cp_), okey], [bk(db)])
                    for kh in range(2):
                        pe(lambda e, kh=kh, db=db: e.matmul(banks[db][:].rearrange("p (i q) -> p i q", q=128), lhsT=orow[:, kh, :],
                                                            rhs=skeb[:, l * 8 + 4 * kh:l * 8 + 4 * kh + 4].unsqueeze(2).to_broadcast([1, 4, 128]),
                                                            start=False, stop=(kh == 1)),
                           ["orow", "skeb"], [bk(db)])
                    yield
                    dve(lambda e, db=db: e.reciprocal(out=rden, in_=banks[db][:]), [bk(db)], ["rden"])
                    bfree(db)
                    dve(lambda e, ob=ob, blk=blk: e.tensor_tensor(
                        out=atf[:, :, blk * 128:(blk + 1) * 128],
                        in0=banks[ob][:].rearrange("p (i q) -> p i q", q=128),
                        in1=rden.rearrange("p (i q) -> p i q", q=128), op=ALU.mult),
                        [bk(ob), "rden"], [("atf", i) for i in range(4)])
                    bfree(ob)
                    yield
                if l == 0 and ti_ == 0:
                    tap("at0", atf[:, 0, :], [("atf", 0)])
                nb_ = yield from balloc()
                rmsnorm(lambda k: atf[:, k, :], 4, lambda k: gv[:, gvb + 28 + k:gvb + 29 + k], ones9, "ones9", lambda k: mixed[:, 4 + k, :],
                        [("atf", k) for k in range(4)], [("hn", 4 + k) for k in range(4)], bank=nb_)
                bfree(nb_)

            def ssm_chain():
                slots = [sget((ti_, l, "S%d" % pt)) for pt in range(4)]
                if not first:
                    rb = yield from balloc()
                    pe(lambda e, rb=rb: e.matmul(banks[rb][:, 0:NG], lhsT=perm[:], rhs=Zcar[:, l, :], start=True, stop=True),
                       ["perm", "Zcar"], [bk(rb)])
                    dve(lambda e, rb=rb: e.tensor_tensor(out=zrot[:, 0, :], in0=banks[rb][:, 0:NG], in1=c64[:, l, 1, :], op=ALU.mult),
                        [bk(rb), "c64"], ["zrot0"])
                    bfree(rb)
                    dve(lambda e: e.tensor_tensor(out=zrot[:, 1, :], in0=Zcar[:, l, :], in1=c64[:, l, 0, :], op=ALU.mult),
                        ["Zcar", "c64"], ["zrot1"])
                    dve(lambda e: e.tensor_tensor(out=Zcar[:, l, :], in0=zrot[:, 0, :], in1=zrot[:, 1, :], op=ALU.add),
                        ["zrot0", "zrot1"], ["Zcar"])
                for pp in range(2):
                    dve(lambda e, pp=pp: e.tensor_copy(
                        out=Zb[:, pp, :, 0].rearrange("p (q t e) -> p q t e", q=4, t=2),
                        in_=Zcar[:, l, 16 * pp:16 * pp + 16].rearrange("p (t q e) -> p q t e", t=2, q=4)),
                        ["Zcar"], [("Zb", pp, q) for q in range(4)])
                for pp in range(2):
                    sb4 = []
                    for q in range(4):
                        b = yield from balloc()
                        sb4.append(b)
                    for ptl in range(2):
                        pt = 2 * pp + ptl
                        sl = slots[pt]
                        wv = ring[:, sl, 1024:4096].rearrange("p (e s c) -> p e s c", e=2, c=192)
                        for e_ in range(2):
                            for v_ in range(2):
                                for s_ in range(8):
                                    for q in range(4):
                                        b = sb4[q]
                                        col = ((ptl * 2 + e_) * 2 + v_) * NCH
                                        pe(lambda e, b=b, q=q, e_=e_, v_=v_, s_=s_, pt=pt, wv=wv, col=col: e.matmul(
                                            banks[b][:, col:col + NCH],
                                            lhsT=wv[32 * q:32 * q + 32, e_, s_, 64 * v_:64 * v_ + 128],
                                            rhs=uT[32 * q:32 * q + 32, pt, :].rearrange("p (c j) -> p c j", j=8)[:, :, s_],
                                            start=(s_ == 0), stop=(s_ == 7), tile_position=(32 * q, 0)),
                                           [("ring", sl), ("uT", pt)], [bk(b)])
                    yield
                    for q in range(4):
                        b = sb4[q]
                        rs = q % 2
                        bv = banks[b][:].rearrange("p (a e v c) -> p a e v c", a=2, e=2, v=2)
                        dve(lambda e, bv=bv, rs=rs, q=q, pp=pp: e.tensor_tensor(
                            out=srot[:, rs, 0, :].rearrange("p (a e c) -> p a e c", a=2, e=2),
                            in0=bv[:, :, :, 0, :], in1=gsel(cosv, pp, q), op=ALU.mult), [bk(b), "tabs"], [("srot", rs, 0)])
                        dve(lambda e, bv=bv, rs=rs, q=q, pp=pp: e.tensor_tensor(
                            out=srot[:, rs, 1, :].rearrange("p (a e c) -> p a e c", a=2, e=2),
                            in0=bv[:, :, :, 1, :], in1=gsel(sinv, pp, q), op=ALU.mult), [bk(b), "tabs"], [("srot", rs, 1)])
                        bfree(b)
                        dve(lambda e, rs=rs: e.tensor_tensor(out=srot[:, rs, 0, :], in0=srot[:, rs, 0, :], in1=srot[:, rs, 1, :], op=ALU.add),
                            [("srot", rs, 0), ("srot", rs, 1)], [("srot", rs, 0)])
                        zk = ("Zb", pp, q)
                        for idx in range(4):
                            ptl, e_ = idx // 2, idx % 2
                            g = 16 * pp + 8 * ptl + 2 * q + e_
                            r = 4 * q + idx
                            dve(lambda e, g=g, r=r, idx=idx, rs=rs, pp=pp: e.tensor_tensor_scan(
                                out=Zb[:, pp, r, 1:NCH + 1], data0=Rsb[:, l, g:g + 1].to_broadcast([128, NCH]),
                                data1=srot[:, rs, 0, idx * NCH:(idx + 1) * NCH], initial=Zb[:, pp, r, 0:1],
                                op0=ALU.mult, op1=ALU.add), [("srot", rs, 0), "Rsb", zk], [zk])
                        dve(lambda e, q=q, pp=pp: e.tensor_tensor(out=zsel(ZCb, pp, q, 0, NCH), in0=zsel(Zb, pp, q, 0, NCH),
                                                                  in1=gsel(cosv, pp, q), op=ALU.mult), [zk, "tabs"], [("ZCb", pp, q)])
                        dve(lambda e, q=q, pp=pp: e.tensor_tensor(out=zsel(ZSb, pp, q, 0, NCH), in0=zsel(Zb, pp, q, 0, NCH),
                                                                  in1=gsel(sinv, pp, q), op=ALU.mult), [zk, "tabs"], [("ZSb", pp, q)])
                        if q % 2 == 1:
                            yield
                    dve(lambda e, pp=pp: e.tensor_copy(
                        out=Zcar[:, l, 16 * pp:16 * pp + 16].rearrange("p (t q e) -> p q t e", t=2, q=4),
                        in_=Zb[:, pp, :, NCH].rearrange("p (q t e) -> p q t e", q=4, t=2)),
                        [("Zb", pp, q) for q in range(4)], ["Zcar"])
                for pt in range(4):
                    pp, ptl = pt // 2, pt % 2
                    sl = slots[pt]
                    yb = yield from balloc()
                    kd = ring[:, sl, 0:1024].rearrange("p (t n) -> p t n", n=128)
                    vv = ring[:, sl, 4096:6144].rearrange("p (g v n) -> p g v n", v=2, n=128)
                    uv = uT[:, pt, :].rearrange("p (c j) -> p c j", j=8)
                    yv = banks[yb][:].rearrange("p (c j) -> p c j", j=8)
                    for tau in range(8):
                        pe(lambda e, tau=tau, kd=kd, uv=uv, yv=yv: e.matmul(yv[:, :, tau:8], lhsT=kd[:, tau, :], rhs=uv[:, :, 0:8 - tau],
                                                                           start=(tau == 0), stop=False),
                           [("ring", sl), ("uT", pt)], [bk(yb)])
                    ycb = []
                    for _ in range(2):
                        b = yield from balloc()
                        ycb.append(b)
                    for gi in range(8):
                        q, e_ = gi // 2, gi % 2
                        r = 4 * q + 2 * ptl + e_
                        b = ycb[gi // 4]
                        cs_ = (gi % 4) * 128
                        zkeys = [("ZCb", pp, q), ("ZSb", pp, q)]
                        pe(lambda e, r=r, gi=gi, b=b, cs_=cs_, vv=vv, pp=pp: e.matmul(banks[b][0:64, cs_:cs_ + 128], lhsT=ZCb[:, pp, r, :],
                                                                                     rhs=vv[:, gi, 0, :], start=True, stop=False),
                           zkeys + [("ring", sl)], [bk(b)])
                        pe(lambda e, r=r, gi=gi, b=b, cs_=cs_, vv=vv, pp=pp: e.matmul(banks[b][0:64, cs_:cs_ + 128], lhsT=ZSb[:, pp, r, :],
                                                                                     rhs=vv[:, gi, 1, :], start=False, stop=True),
                           zkeys + [("ring", sl)], [bk(b)])
                    yield
                    yk = ("Ycs", ptl)
                    ycv = Ycs[:, ptl].rearrange("p j g h -> p g j h")
                    act(lambda e, ycv=ycv, b0=ycb[0]: e.copy(out=ycv[:, 0:4], in_=banks[b0][0:64, :].rearrange("p (g j h) -> p g j h", j=8, h=16)),
                        [bk(ycb[0])], [yk])
                    act(lambda e, ycv=ycv, b1=ycb[1]: e.copy(out=ycv[:, 4:8], in_=banks[b1][0:64, :].rearrange("p (g j h) -> p g j h", j=8, h=16)),
                        [bk(ycb[1])], [yk])
                    bfree(ycb[0])
                    bfree(ycb[1])
                    for j in range(8):
                        pe(lambda e, j=j, ptl=ptl, yv=yv: e.matmul(yv[:, :, j], lhsT=Ycs[:, ptl, j].rearrange("p g h -> p (g h)"),
                                                                  rhs=identb[0:64, 0:64], start=False, stop=(j == 7)),
                           [yk, "identb"], [bk(yb)])
                    srel((ti_, l, "S%d" % pt))
                    yield
                    ys = pt % 2
                    dcol = gv[:, gvb + 32 + pt:gvb + 33 + pt]
                    dve(lambda e, ys=ys, pt=pt, yb=yb, dcol=dcol: e.scalar_tensor_tensor(
                        out=ysb[:, ys, :], in0=uT[:, pt, :], scalar=dcol, in1=banks[yb][:], op0=ALU.mult, op1=ALU.add),
                        [("uT", pt), bk(yb), "gv"], [("ysb", ys)])
                    bfree(yb)
                    if l == 0 and ti_ == 0 and pt == 0:
                        tap("y0", ysb[:, 0, :], [("ysb", 0)])
                    dve(lambda e, ys=ys: e.scalar_tensor_tensor(out=t1[:, ys, :], in0=ysb[:, ys, :], scalar=0.044715, in1=ysb[:, ys, :],
                                                                op0=ALU.mult, op1=ALU.mult), [("ysb", ys)], [("t1", ys)])
                    dve(lambda e, ys=ys: e.scalar_tensor_tensor(out=t1[:, ys, :], in0=t1[:, ys, :], scalar=1.0, in1=ysb[:, ys, :],
                                                                op0=ALU.add, op1=ALU.mult), [("t1", ys), ("ysb", ys)], [("t1", ys)])
                    act(lambda e, ys=ys: e.activation(out=t1[:, ys, :], in_=t1[:, ys, :], func=AF.Tanh, scale=0.7978845608028654),
                        [("t1", ys)], [("t1", ys)])
                    dve(lambda e, ys=ys, pt=pt: e.scalar_tensor_tensor(out=gb[:, pt, :], in0=t1[:, ys, :], scalar=1.0, in1=ysb[:, ys, :],
                                                                      op0=ALU.add, op1=ALU.mult),
                        [("t1", ys), ("ysb", ys)], [("gb", pt)])
                    yield
                s = sget((ti_, l, "glu"))
                gbk = []
                for m in range(4):
                    b = yield from balloc()
                    gbk.append(b)
                for k in range(4):
                    for m in range(4):
                        b = gbk[m]
                        pe(lambda e, b=b, k=k, m=m, s=s: e.matmul(banks[b][:], lhsT=wtile(s, 512, k, m), rhs=gb[:, k, :],
                                                                  start=(k == 0), stop=(k == 3)), [("ring", s), ("gb", k)], [bk(b)])
                for m in range(4):
                    b = gbk[m]
                    ts_ = m % 2
                    act(lambda e, b=b, ts_=ts_: e.activation(out=tht[:, ts_, :], in_=banks[b][:], func=AF.Tanh, scale=0.25), [bk(b)], [("tht", ts_)])
                    bfree(b)
                    dve(lambda e, m=m, ts_=ts_: e.scalar_tensor_tensor(out=gl2[:, m, :], in0=tht[:, ts_, :], scalar=1.0, in1=gb[:, m, :],
                                                                      op0=ALU.add, op1=ALU.mult), [("tht", ts_), ("gb", m)], [("gl2", m)])
                srel((ti_, l, "glu"))
                yield
                if l == 0 and ti_ == 0:
                    tap("so0", gl2[:, 0, :], [("gl2", 0)])
                nb_ = yield from balloc()
                rmsnorm(lambda k: gl2[:, k, :], 4, lambda k: gs4[:, l, k:k + 1], ones13, "ones13", lambda k: mixed[:, k, :],
                        [("gl2", k) for k in range(4)], [("hn", k) for k in range(4)], bank=nb_)
                bfree(nb_)

            ssm_c = ssm_chain()
            att_c = attn_chain()
            chains = [ssm_c, att_c]
            stall = 0
            nstep = 0
            while chains:
                nfree0 = len(freeb)
                for c_ in list(chains):
                    if c_ is att_c and ssm_c in chains and nstep < ATT_DELAY:
                        continue
                    try:
                        next(c_)
                    except StopIteration:
                        chains.remove(c_)
                nstep += 1
                stall = stall + 1 if (len(freeb) == 0 and nfree0 == 0) else 0
                assert stall < 50, "bank allocation deadlock"

            LAG = 3

            def resid_mm(names, nkt, rhs_fn, rkeys):
                pendq = []
                for ci, name in enumerate(names):
                    s = sget((ti_, l, name))
                    _, kt_total, ncols = CH_OFF[name]
                    for mm_ in range(ncols // 128):
                        m = ci * (ncols // 128) + mm_
                        b = mmp.next()
                        for k in range(nkt):
                            pe(lambda e, b=b, k=k, mm_=mm_, s=s, ncols=ncols: e.matmul(
                                banks[b][:], lhsT=wtile(s, ncols, k, mm_), rhs=rhs_fn(k), start=(k == 0), stop=(k == nkt - 1)),
                               [("ring", s), rkeys[k]], [bk(b)])
                        if len(pendq) >= LAG:
                            p_ = pendq.pop(0)
                            stat_mm(p_[0], p_[1] == 0, False)
                        dve(lambda e, b=b, m=m: e.tensor_tensor(out=hT[:, m, :], in0=banks[b][:], in1=hT[:, m, :], op=ALU.add),
                            [bk(b), ("hT", m)], [("hT", m)])
                        pendq.append((stat_sq(m), m))
                    srel((ti_, l, name))
                while pendq:
                    p_ = pendq.pop(0)
                    stat_mm(p_[0], p_[1] == 0, len(pendq) == 0)
            resid_mm(["out0", "out1"], 8, lambda k: mixed[:, k, :], HNK)
            if l == 0 and ti_ == 0:
                tap("h1", hT[:, 0, :], [("hT", 0)])

            rmsnorm(lambda k: hT[:, k, :], 8, lambda k: gv[:, gvb + 8 + k:gvb + 9 + k], ones10, "ones10", lambda k: hn[:, k, :], HK, HNK, pre=True)
            for c in range(11):
                s = sget((ti_, l, "ffi%d" % c))
                for jj in range(2):
                    j = 2 * c + jj
                    bg = mmp.next()
                    bu = mmp.next()
                    for k in range(8):
                        pe(lambda e, k=k, s=s, jj=jj, bg=bg: e.matmul(banks[bg][:], lhsT=wtile(s, 512, k, 2 * jj), rhs=hn[:, k, :],
                                                                     start=(k == 0), stop=(k == 7)), [("ring", s), ("hn", k)], [bk(bg)])
                    for k in range(8):
                        pe(lambda e, k=k, s=s, jj=jj, bu=bu: e.matmul(banks[bu][:], lhsT=wtile(s, 512, k, 2 * jj + 1), rhs=hn[:, k, :],
                                                                     start=(k == 0), stop=(k == 7)), [("ring", s), ("hn", k)], [bk(bu)])
                    ss = j % 2
                    act(lambda e, bg=bg, ss=ss: e.activation(out=sgt[:, ss, :], in_=banks[bg][:], func=AF.Silu), [bk(bg)], [("sgt", ss)])
                    dve(lambda e, bu=bu, ss=ss, j=j: e.tensor_tensor(out=hid[:, j, :], in0=banks[bu][:], in1=sgt[:, ss, :], op=ALU.mult),
                        [bk(bu), ("sgt", ss)], [("hid", j)])
                srel((ti_, l, "ffi%d" % c))
            resid_mm(["ffo%d" % c for c in range(4)], 22, lambda k: hid[:, k, :], [("hid", k) for k in range(22)])
            if l == 0 and ti_ == 0:
                tap("h2", hT[:, 0, :], [("hT", 0)])

            rmsnorm(lambda k: hT[:, k, :], 8, lambda k: gv[:, gvb + 16 + k:gvb + 17 + k], ones10, "ones10", lambda k: hn[:, k, :], HK, HNK, pre=True)
            sg0 = sget((ti_, l, "gate0"))
            spj = sget((ti_, l, "proj"))
            pendq = []
            for half in range(2):
                if half == 1:
                    srel((ti_, l, "gate0"))
                    sg0 = sget((ti_, l, "gate1"))
                bps = []
                for mm_ in range(4):
                    m = 4 * half + mm_
                    bp = mmp.next()
                    bps.append(bp)
                    for k in range(2):
                        pe(lambda e, bp=bp, k=k, m=m, spj=spj: e.matmul(banks[bp][:], lhsT=wtile(spj, 1024, k, m), rhs=pTs[:, k, :],
                                                                       start=(k == 0), stop=(k == 1)), [("ring", spj), "pTs"], [bk(bp)])
                for mm_ in range(4):
                    m = 4 * half + mm_
                    bp = bps[mm_]
                    bg = mmp.next()
                    for k in range(8):
                        pe(lambda e, bg=bg, k=k, m=m, sg0=sg0: e.matmul(banks[bg][:], lhsT=wtile(sg0, 512, k, m % 4), rhs=hn[:, k, :],
                                                                       start=(k == 0), stop=(k == 7)), [("ring", sg0), ("hn", k)], [bk(bg)])
                    if len(pendq) >= LAG:
                        p_ = pendq.pop(0)
                        stat_mm(p_[0], p_[1] == 0, False)
                    ts_ = m % 2
                    act(lambda e, bg=bg, ts_=ts_: e.activation(out=tht[:, ts_, :], in_=banks[bg][:], func=AF.Tanh, scale=0.5),
                        [bk(bg)], [("tht", ts_)])
                    dve(lambda e, bp=bp, ts_=ts_: e.scalar_tensor_tensor(out=ott[:, 0, :], in0=tht[:, ts_, :], scalar=1.0, in1=banks[bp][:],
                                                                        op0=ALU.add, op1=ALU.mult), [("tht", ts_), bk(bp)], [("ott", 0)])
                    dve(lambda e, ts_=ts_, m=m: e.scalar_tensor_tensor(out=hT[:, m, :], in0=ott[:, 0, :], scalar=0.5, in1=hT[:, m, :],
                                                                      op0=ALU.mult, op1=ALU.add), [("ott", 0), ("hT", m)], [("hT", m)])
                    pendq.append((stat_sq(m), m))
            while pendq:
                p_ = pendq.pop(0)
                stat_mm(p_[0], p_[1] == 0, len(pendq) == 0)
            srel((ti_, l, "gate1"))
            srel((ti_, l, "proj"))
            if l == 0 and ti_ == 0:
                tap("h3", hT[:, 0, :], [("hT", 0)])

        for ti_ in range(ntiles):
            tsl = slice(ti_ * T, (ti_ + 1) * T)
            if ti_ == 0:
                for k in range(8):
                    P.op("sp", lambda e, tsl=tsl, k=k: e.dma_start(out=hT[:, k, :], in_=xT[k * 128:(k + 1) * 128, tsl]),
                         writes=[("hT", k)], dma="xld%d" % (k % 4))
            for l in range(nl):
                layer(ti_, l)
            gfb = NL * GV_L

            def store(k, tsl=tsl, ti_=ti_):
                o = P.op("sp", lambda e: e.dma_start(out=outT[k * 128:(k + 1) * 128, tsl], in_=(gl2 if k < 4 else atf)[:, k % 4, :]),
                         reads=[("gl2" if k < 4 else "atf", k % 4)], dma="out%d" % k)
                finals.append(o)
                if ti_ + 1 < ntiles:
                    nsl = slice((ti_ + 1) * T, (ti_ + 2) * T)
                    P.op("sp", lambda e: e.dma_start(out=hT[:, k, :], in_=xT[k * 128:(k + 1) * 128, nsl]),
                         writes=[("hT", k)], dma="xld%d" % (k % 4))
            rmsnorm(lambda k: hT[:, k, :], 8, lambda k: gv[:, gfb + k:gfb + k + 1], ones10, "ones10",
                    lambda k: (gl2 if k < 4 else atf)[:, k % 4, :], HK, [("gl2" if k < 4 else "atf", k % 4) for k in range(8)],
                    after=store, pre=True)
        P.emit(final_waits=finals)
    return nc


def kernel(**inputs):
    wpack, gv, sk, ssmp = _host_pack(inputs)
    x = np.asarray(inputs["x"], np.float32)
    p = np.asarray(inputs["p"], np.float32)
    nc = build()
    in_maps = []
    for c in range(8):
        in_maps.append({"xT": np.ascontiguousarray(x[c].T),
                        "pT": np.ascontiguousarray(p[:, c].transpose(0, 2, 1)),
                        "wpack": wpack, "gv": gv, "sk": sk, "ssmp": ssmp})
    res = run_bass_kernel_spmd(nc, in_maps, core_ids=list(range(8)))
    out = np.stack([np.asarray(res.results[c]["outT"]).T for c in range(8)])
    return np.ascontiguousarray(out.astype(np.float32))
```

```python
import contextlib
import math

import numpy as np
import concourse.bass as bass
import concourse.mybir as mybir
from concourse.bass_utils import run_bass_kernel_spmd

F32 = mybir.dt.float32
BF16 = mybir.dt.bfloat16
I32 = mybir.dt.int32
ALU = mybir.AluOpType
AF = mybir.ActivationFunctionType

D_MODEL = 1024
SEQ = 4096
NL = 4
T = 512
NTILES = SEQ // T
FFN = 2816
NG = 32
NCH = T // 8
EPS = 1e-6
TWO_PI = 2.0 * math.pi
MAGIC = 12582912.0
RING_COLS = 6144
NRING = 4

ENGS = ("pe", "act", "dve", "pool", "sp")


class _Op:
    __slots__ = ("eng", "fn", "deps", "dma", "marked", "tok", "raw", "idx")

    def __init__(self, eng, fn, deps, dma):
        self.eng = eng
        self.fn = fn
        self.deps = deps
        self.dma = dma
        self.marked = False
        self.tok = None


class Prog:
    def __init__(self, nc, strict_same=True):
        self.nc = nc
        self.strict_same = strict_same
        self.ops = {e: [] for e in ENGS}
        self.last_write = {}
        self.readers = {}
        self.dma_slots = {}
        self.all_ops = []
        self.pending = {}

    def barrier(self, exclude=("cast",)):
        fr = []
        for e in ENGS:
            lst = [o for o in self.ops[e] if o.dma is None]
            if lst:
                fr.append(lst[-1])
        for s, lst in self.dma_slots.items():
            if not s.startswith(exclude):
                fr.append(lst[-1])
        self.pending = {e: list(fr) for e in ENGS}

    def op(self, eng, fn, reads=(), writes=(), dma=None):
        deps = []
        if self.pending.get(eng):
            deps.extend(self.pending[eng])
            self.pending[eng] = None
        raw = set()
        for k in reads:
            w = self.last_write.get(k)
            if w is not None:
                deps.append(w)
                raw.add(id(w))
        for k in writes:
            w = self.last_write.get(k)
            if w is not None:
                deps.append(w)
            deps.extend(self.readers.get(k, ()))
        o = _Op(eng, fn, deps, dma)
        o.raw = raw
        if dma is not None:
            self.dma_slots.setdefault(dma, []).append(o)
        for k in writes:
            self.last_write[k] = o
            self.readers[k] = []
        for k in reads:
            self.readers.setdefault(k, []).append(o)
        o.idx = len(self.all_ops)
        self.ops[eng].append(o)
        self.all_ops.append(o)
        return o

    def emit(self, final_waits=()):
        nc = self.nc
        for o in self.all_ops:
            nd = []
            seen = set()
            for d in o.deps:
                if d is o or id(d) in seen:
                    continue
                seen.add(id(d))
                if d.dma is None and o.dma is None and d.eng == o.eng:
                    if d.eng in ("pe", "sp") or not self.strict_same:
                        continue
                    if id(d) not in o.raw and d.eng != "pool":
                        continue
                nd.append(d)
            best = {}
            for d in nd:
                g = ("d", d.dma) if d.dma is not None else ("e", d.eng)
                if g not in best or best[g].idx < d.idx:
                    best[g] = d
            o.deps = list(best.values())
            for d in o.deps:
                d.marked = True
        for o in final_waits:
            o.marked = True
        for e in ENGS:
            c = 0
            for o in self.ops[e]:
                if o.dma is None and o.marked:
                    c += 1
                    o.tok = (("e", e), c)
        for s, lst in self.dma_slots.items():
            c = 0
            for o in lst:
                c += 16
                o.tok = (("d", s), c)
        semnames = [("e", e) for e in ENGS] + [("d", s) for s in self.dma_slots]
        assert len(semnames) < 140, len(semnames)
        with contextlib.ExitStack() as st:
            sems = {}
            for i, k in enumerate(semnames):
                sems[k] = st.enter_context(nc.semaphore("s%d" % i))
            block = st.enter_context(nc.Block())
            prog = self

            def run(e, eng):
                waited = {}
                for o in prog.ops[e]:
                    need = {}
                    for d in o.deps:
                        k, v = d.tok
                        if need.get(k, 0) < v:
                            need[k] = v
                    for k, v in need.items():
                        if waited.get(k, 0) < v:
                            eng.wait_ge(sems[k], v)
                            waited[k] = v
                    ins = o.fn(eng)
                    if o.dma is not None:
                        ins.then_inc(sems[o.tok[0]], 16)
                    elif o.marked:
                        ins.then_inc(sems[o.tok[0]], 1)
                if e == "sp":
                    for o in final_waits:
                        k, v = o.tok
                        eng.wait_ge(sems[k], v)

            @block.tensor
            def _(eng):
                run("pe", eng)

            @block.scalar
            def _(eng):
                run("act", eng)

            @block.vector
            def _(eng):
                run("dve", eng)

            @block.gpsimd
            def _(eng):
                run("pool", eng)

            @block.sync
            def _(eng):
                run("sp", eng)


def _chunk_table():
    ch = [("inU", 8, 512), ("inQ", 8, 512), ("inKV", 8, 256), ("glu", 4, 512),
          ("out0", 8, 512), ("out1", 8, 512)]
    ch += [("ffi%d" % c, 8, 512) for c in range(11)]
    ch += [("ffo%d" % c, 22, 256) for c in range(4)]
    ch += [("gate0", 8, 512), ("gate1", 8, 512), ("proj", 2, 1024)]
    offs = {}
    o = 0
    for n, kt, nc_ in ch:
        offs[n] = (o, kt, nc_)
        o += kt * nc_
    return ch, offs, o


CHUNKS, CH_OFF, WCOLS = _chunk_table()
assert WCOLS == 98304

GV_L = 36
GV_COLS = NL * GV_L + 8
SSMP_L = 32 * (3 + 4 * 16)


def _pack_chunk(Wr, cols):
    kt = Wr.shape[0] // 128
    a = Wr[:, cols].reshape(kt, 128, len(cols)).transpose(1, 0, 2)
    return a.reshape(128, kt * len(cols))


def _host_pack(inp):
    f32 = np.float32
    wpack = np.empty((NL, 128, WCOLS), f32)
    qcols = np.concatenate([np.concatenate([512 + 64 * i + np.arange(64),
                                            512 + 64 * (4 + i) + np.arange(64)]) for i in range(4)])
    orow = np.concatenate([np.arange(512)] +
                          [np.concatenate([512 + 64 * i + np.arange(64),
                                           512 + 64 * (4 + i) + np.arange(64)]) for i in range(4)])
    for l in range(NL):
        parts = []
        w_in = np.asarray(inp["w_in"][l], f32)
        parts.append(_pack_chunk(w_in, np.arange(512)))
        parts.append(_pack_chunk(w_in, qcols))
        parts.append(_pack_chunk(w_in, np.arange(1024, 1280)))
        parts.append(_pack_chunk(np.asarray(inp["ssm_w_glu"][l], f32), np.arange(512)))
        w_out = np.asarray(inp["w_out"][l], f32)[orow]
        parts.append(_pack_chunk(w_out, np.arange(512)))
        parts.append(_pack_chunk(w_out, np.arange(512, 1024)))
        wfi = np.asarray(inp["w_ffn_in"][l], f32)
        for c in range(11):
            cols = np.concatenate([128 * (2 * c) + np.arange(128), FFN + 128 * (2 * c) + np.arange(128),
                                   128 * (2 * c + 1) + np.arange(128), FFN + 128 * (2 * c + 1) + np.arange(128)])
            parts.append(_pack_chunk(wfi, cols))
        wfo = np.asarray(inp["w_ffn_out"][l], f32)
        for c in range(4):
            parts.append(_pack_chunk(wfo, 256 * c + np.arange(256)))
        wg = np.asarray(inp["w_ple_gate"][l], f32)
        parts.append(_pack_chunk(wg, np.arange(512)))
        parts.append(_pack_chunk(wg, np.arange(512, 1024)))
        parts.append(_pack_chunk(np.asarray(inp["w_ple_proj"][l], f32), np.arange(1024)))
        wpack[l] = np.concatenate(parts, axis=1)

    def fm(v):
        return np.asarray(v, f32).reshape(-1, 128).T

    aperm = np.stack([np.concatenate([64 * i + np.arange(64), 64 * (4 + i) + np.arange(64)]) for i in range(4)])
    gv = np.empty((128, GV_COLS), f32)
    for l in range(NL):
        b = l * GV_L
        gv[:, b:b + 8] = fm(inp["norm_mix"][l])
        gv[:, b + 8:b + 16] = fm(inp["norm_ffn"][l])
        gv[:, b + 16:b + 24] = fm(inp["norm_ple"][l])
        gv[:, b + 24:b + 28] = fm(inp["norm_ssm_out"][l])
        gv[:, b + 28:b + 32] = np.asarray(inp["norm_attn_out"][l], f32)[aperm].T
        gv[:, b + 32:b + 36] = fm(inp["ssm_d"][l])
    gv[:, NL * GV_L:] = fm(inp["norm_final"])
    sk = np.asarray(inp["attn_sinks"], f32).reshape(1, NL * 8)
    ssmp = np.empty((128, NL, SSMP_L), f32)
    for l in range(NL):
        ar = np.asarray(inp["ssm_a_re"][l], f32).T
        ai = np.asarray(inp["ssm_a_im"][l], f32).T
        ld = np.broadcast_to(np.asarray(inp["ssm_log_dt"][l], f32)[None, :], (64, 32))
        bre = np.asarray(inp["ssm_b_re"][l], f32).transpose(1, 0, 2).reshape(64, 512)
        bim = np.asarray(inp["ssm_b_im"][l], f32).transpose(1, 0, 2).reshape(64, 512)
        cre = np.asarray(inp["ssm_c_re"][l], f32).transpose(2, 0, 1).reshape(64, 512)
        cim = np.asarray(inp["ssm_c_im"][l], f32).transpose(2, 0, 1).reshape(64, 512)
        row = np.concatenate([ar, ai, ld, bre, bim, cre, cim], axis=1)
        ssmp[0:64, l] = row
        ssmp[64:128, l] = row
    return wpack, gv, sk, ssmp


ATT_DELAY = 6
INCR = False
WARM = {}


def build(nl=NL, ntiles=NTILES, dbg=None, strict_same=True):
    nc = bass.Bass("TRN2", target_bir_lowering=False)
    seq = ntiles * T
    xT = nc.dram_tensor("xT", [D_MODEL, seq], F32, kind="ExternalInput").ap()
    pT = nc.dram_tensor("pT", [nl, 256, seq], F32, kind="ExternalInput").ap()
    wpack = nc.dram_tensor("wpack", [nl, 128, WCOLS], F32, kind="ExternalInput").ap()
    gvd = nc.dram_tensor("gv", [128, GV_COLS], F32, kind="ExternalInput").ap()
    skd = nc.dram_tensor("sk", [1, NL * 8], F32, kind="ExternalInput").ap()
    ssmpd = nc.dram_tensor("ssmp", [128, NL, SSMP_L], F32, kind="ExternalInput").ap()
    outT = nc.dram_tensor("outT", [D_MODEL, seq], F32, kind="ExternalOutput").ap()
    WS = nc.dram_tensor("WS", [nl, 128, WCOLS], BF16, kind="Internal").ap()
    SSMC = nc.dram_tensor("SSMC", [nl, 4, 128, RING_COLS], BF16, kind="Internal").ap()
    TAB = nc.dram_tensor("TAB", [nl, 128, 2 * NG * NCH], BF16, kind="Internal").ap()
    dbg = dbg or {}
    dbg_out = {}
    for name, shape in dbg.items():
        dbg_out[name] = nc.dram_tensor("dbg_" + name, list(shape), F32, kind="ExternalOutput").ap()

    st = contextlib.ExitStack()

    def sb(name, shape, dt=F32):
        return st.enter_context(nc.sbuf_tensor(name, list(shape), dt))

    with st:
        P = Prog(nc, strict_same=strict_same)
        finals = []
        ARENA_BYTES = 196 * 1024
        arena = sb("arena", [128, ARENA_BYTES // 4], F32)

        class Carver:
            def __init__(self):
                self.off = 0

            def __call__(self, shape, dt=F32, parts=128):
                esz = 4 if dt == F32 else 2
                n = 1
                for d in shape[1:]:
                    n *= d
                nbytes = (n * esz + 3) // 4 * 4
                a = arena[0:shape[0], self.off // 4:(self.off + nbytes) // 4]
                self.off += nbytes
                assert self.off <= ARENA_BYTES, self.off
                if dt != F32:
                    a = a.bitcast(dt)
                if len(shape) > 2:
                    names = " ".join("d%d" % i for i in range(1, len(shape)))
                    kw = {"d%d" % i: shape[i] for i in range(1, len(shape) - 1)}
                    a = a.rearrange("p (%s) -> p %s" % (names, names), **kw)
                return a

        gv = sb("gvs", [128, GV_COLS])
        gs4 = sb("gs4", [128, NL, 4])
        Rsb = sb("Rsb", [128, NL, NG])
        skeb = sb("skeb", [1, NL * 8], BF16)
        sks = sb("sks", [1, NL * 8])
        ske = sb("ske", [1, NL * 8])
        ones10 = sb("ones10", [128, 128], BF16)
        ones13 = sb("ones13", [128, 128], BF16)
        ones9 = sb("ones9", [128, 128], BF16)
        OA = sb("OA", [128, 128], BF16)
        OB = sb("OB", [128, 128], BF16)
        orow = sb("orow", [1, 2, 128], BF16)
        maskC = sb("maskC", [128, 128], BF16)
        maskP = sb("maskP", [128, 128], BF16)
        identb = sb("identb", [128, 128], BF16)
        mneg = sb("mneg", [128, 2, 4, 128], BF16)
        iot = sb("iot", [128, 128])
        kcar = sb("kcar", [128, NL, 128], BF16)
        vcar = sb("vcar", [128, NL, 256], BF16)
        Zcar = sb("Zcar", [128, NL, NG])
        sgnc = sb("sgnc", [128, 2])
        epsc = sb("epsc", [128, 1])
        perm = sb("perm", [128, 128])
        c64 = sb("c64", [128, NL, 2, NG])
        zrot = sb("zrot", [128, 2, NG])

        cm = Carver()
        hT = cm([128, 8, T])
        hn = cm([128, 8, T], BF16)
        mixed = hn
        sq = cm([128, 4, T], BF16)
        msb = cm([128, T])
        rstd = cm([128, T])
        uT = cm([128, 4, T], BF16)
        qT = cm([128, 4, T], BF16)
        kT = cm([128, 128 + T], BF16)
        Vx = cm([128, 5, 2, 128], BF16)
        hid = cm([128, 22, T], BF16)
        pTs = cm([128, 2, T], BF16)
        ring = cm([128, NRING, RING_COLS], BF16)
        tabs = cm([128, 2 * NG * NCH], BF16)
        ysb = cm([128, 2, T])
        t1 = cm([128, 2, T])
        gl2 = cm([128, 4, T])
        atf = cm([128, 4, T])
        gb = cm([128, 4, T], BF16)
        srot = cm([128, 2, 2, 4 * NCH])
        Zb = cm([128, 2, 16, NCH + 1])
        ZCb = cm([128, 2, 16, NCH], BF16)
        ZSb = cm([128, 2, 16, NCH], BF16)
        Ycs = cm([64, 2, 8, 8, 16], BF16)
        Pb = cm([128, 2, 2, 2, T], BF16)
        rden = cm([128, T])
        sgt = cm([128, 2, T], BF16)
        tht = cm([128, 2, T])
        ott = cm([128, 1, T])
        main_bytes = cm.off
        print('arena main bytes', main_bytes, 'of', ARENA_BYTES)

        cp = Carver()
        ssmp = cp([128, SSMP_L])
        I2 = cp([128, 64])
        bmask = cp([128, 128])
        cidx = cp([128, NCH])
        K07 = cp([128, 8])
        Kv = cp([128, 4, 8])
        offc = cp([128, 8])
        NLG = nl * NG
        ssm3 = cp([128, 3, NL, 32])
        sv = cp([128, 24, NLG])
        trg = cp([128, 8, NLG, 8])
        ttmp = cp([128, 4, NLG, 8])
        LBu = cp([128, NG, 16])
        LBp = cp([128, 2, NG, 32])
        CL = cp([128, 8, NG, 16])
        cltmp = cp([128, 2, 4, NG, 16])
        DS = cp([128, 2, 8, 64])
        CST = cp([128, 2, RING_COLS], BF16)
        vtmp = cp([128, 2, 8, 8, 16])
        tg = cp([128, 4, NG, NCH])
        tgo = cp([128, 1, 2, NG * NCH], BF16)
        thr = cp([128, NLG])
        pz1 = cp([128, 128])
        pz2 = cp([128, 128])

        banks = [st.enter_context(nc.psum_tensor("pb%d" % i, [128, 512], F32)) for i in range(8)]

        class RR:
            def __init__(self, ids):
                self.ids = ids
                self.i = 0

            def next(self):
                b = self.ids[self.i % len(self.ids)]
                self.i += 1
                return b

        mmp = RR([0, 1, 2, 3, 4, 5, 6])
        NB = 7
        auxp = mmp
        pA = RR([0, 1, 2, 3])
        pB = RR([4, 5, 6, 7])

        def bk(b):
            return ("ps", b)

        def dve(fn, reads, writes):
            return P.op("dve", fn, reads, writes)

        def act(fn, reads, writes):
            return P.op("act", fn, reads, writes)

        def pe(fn, reads, writes):
            return P.op("pe", fn, reads, writes)

        def pool(fn, reads, writes):
            return P.op("pool", fn, reads, writes)

        tapn = [0]

        def tap(name, src_ap, reads):
            if name in dbg_out:
                tapn[0] += 1
                o = P.op("pool", lambda e: e.dma_start(out=dbg_out[name], in_=src_ap), reads=reads, dma="dbg%d" % tapn[0])
                finals.append(o)

        P.op("sp", lambda e: e.dma_start(out=gv[:], in_=gvd), writes=["gv"], dma="c0")
        P.op("sp", lambda e: e.dma_start(out=sks[:], in_=skd), writes=["sks"], dma="c1")
        for t_, key, v in ((ones10, "ones10", 2.0 ** -10), (ones13, "ones13", 2.0 ** -13), (ones9, "ones9", 2.0 ** -9)):
            pool(lambda e, t_=t_, v=v: e.memset(t_[:], v), [], [key])
        pool(lambda e: e.memset(OA[:], 0.0), [], ["OA"])
        pool(lambda e: e.memset(OB[:], 0.0), [], ["OB"])
        pool(lambda e: e.memset(OA[:, 0:64], 1.0), [], ["OA"])
        pool(lambda e: e.memset(OB[:, 64:128], 1.0), [], ["OB"])
        pool(lambda e: e.memset(orow[:], 0.0), [], ["orow"])
        pool(lambda e: e.memset(orow[:, 0, 0:64], 1.0), [], ["orow"])
        pool(lambda e: e.memset(orow[:, 1, 64:128], 1.0), [], ["orow"])
        pool(lambda e: e.memset(epsc[:], EPS), [], ["epsc"])
        pool(lambda e: e.memset(vcar[:], 0.0), [], ["vcar"])
        pool(lambda e: e.memset(kcar[:], 0.0), [], ["kcar"])
        pool(lambda e: e.memset(Zcar[:], 0.0), [], ["Zcar"])
        pool(lambda e: e.memset(LBp.rearrange("p a g c -> p (a g c)"), 0.0), [], ["LBp"])
        pool(lambda e: e.iota(iot[:], pattern=[[1, 128]], base=0, channel_multiplier=-1,
                              allow_small_or_imprecise_dtypes=True), [], ["iot"])
        dve(lambda e: e.tensor_scalar(out=maskC[:], in0=iot[:], scalar1=0.0, scalar2=None, op0=ALU.is_ge), ["iot"], ["maskC"])
        dve(lambda e: e.tensor_scalar(out=maskP[:], in0=iot[:], scalar1=0.0, scalar2=None, op0=ALU.is_lt), ["iot"], ["maskP"])
        dve(lambda e: e.tensor_scalar(out=identb[:], in0=iot[:], scalar1=0.0, scalar2=None, op0=ALU.is_equal),
            ["iot"], ["identb"])
        for i_ in range(4):
            dve(lambda e, i_=i_: e.tensor_scalar(out=mneg[:, 0, i_, :], in0=iot[:], scalar1=0.0, scalar2=-30000.0, op0=ALU.is_lt, op1=ALU.mult),
                ["iot"], ["mneg"])
            dve(lambda e, i_=i_: e.tensor_scalar(out=mneg[:, 1, i_, :], in0=iot[:], scalar1=0.0, scalar2=-30000.0, op0=ALU.is_ge, op1=ALU.mult),
                ["iot"], ["mneg"])
        dve(lambda e: e.tensor_scalar(out=I2, in0=iot[:, 0:64], scalar1=0.0, scalar2=None, op0=ALU.is_equal), ["iot"], ["I2"])
        dve(lambda e: e.tensor_scalar(out=perm[:], in0=iot[:], scalar1=64.0, scalar2=None, op0=ALU.is_equal), ["iot"], ["perm"])
        dve(lambda e: e.tensor_scalar(out=pz2, in0=iot[:], scalar1=-64.0, scalar2=None, op0=ALU.is_equal), ["iot"], ["pz2"])
        dve(lambda e: e.tensor_tensor(out=perm[:], in0=perm[:], in1=pz2, op=ALU.add), ["perm", "pz2"], ["perm"])
        dve(lambda e: e.tensor_scalar(out=pz1[:, 0:64], in0=iot[:, 0:64], scalar1=-64.0, scalar2=None, op0=ALU.is_equal),
            ["iot"], ["pz1"])
        dve(lambda e: e.tensor_tensor(out=I2, in0=I2, in1=pz1[:, 0:64], op=ALU.add), ["I2", "pz1"], ["I2"])
        pool(lambda e: e.iota(bmask.rearrange("p (g h) -> p g h", h=16), pattern=[[-16, 8], [0, 16]], base=0,
                              channel_multiplier=1, allow_small_or_imprecise_dtypes=True), [], ["bmask"])
        dve(lambda e: e.tensor_scalar(out=pz1, in0=bmask, scalar1=0.0, scalar2=None, op0=ALU.is_ge), ["bmask", "I2"], ["pz1"])
        dve(lambda e: e.tensor_scalar(out=pz2, in0=bmask, scalar1=15.0, scalar2=None, op0=ALU.is_le), ["bmask"], ["pz2"])
        dve(lambda e: e.tensor_tensor(out=bmask, in0=pz1, in1=pz2, op=ALU.mult), ["pz1", "pz2"], ["bmask"])
        pool(lambda e: e.iota(cidx, pattern=[[1, NCH]], base=0, channel_multiplier=0,
                              allow_small_or_imprecise_dtypes=True), [], ["cidx"])
        pool(lambda e: e.iota(K07, pattern=[[1, 8]], base=0, channel_multiplier=0,
                              allow_small_or_imprecise_dtypes=True), [], ["K07"])
        dve(lambda e: e.tensor_copy(out=Kv[:, 0, :], in_=K07), ["K07"], ["Kv"])
        dve(lambda e: e.tensor_scalar(out=Kv[:, 1, :], in0=K07, scalar1=-1.0, scalar2=7.0, op0=ALU.mult, op1=ALU.add), ["K07"], ["Kv"])
        dve(lambda e: e.tensor_scalar(out=Kv[:, 2, :], in0=K07, scalar1=-7.0, scalar2=None, op0=ALU.add), ["K07"], ["Kv"])
        dve(lambda e: e.tensor_scalar(out=Kv[:, 3, :], in0=K07, scalar1=1.0, scalar2=None, op0=ALU.add), ["K07"], ["Kv"])
        HP = math.pi / 2
        offs = [(HP, 0.0), (math.pi, HP), (HP, math.pi), (math.pi, 3 * HP), (math.pi, HP), (3 * HP, math.pi)]
        for i, (a, b) in enumerate(offs):
            pool(lambda e, i=i, a=a: e.memset(offc[0:64, i:i + 1], a), [], ["offc"])
            pool(lambda e, i=i, b=b: e.memset(offc[64:128, i:i + 1], b), [], ["offc"])
        pool(lambda e: e.memset(sgnc[:, 0:1], 1.0), [], ["sgnc"])
        pool(lambda e: e.memset(sgnc[64:128, 0:1], -1.0), [], ["sgnc"])
        pool(lambda e: e.memset(sgnc[:, 1:2], HP), [], ["sgnc"])
        act(lambda e: e.activation(out=ske[:], in_=sks[:], func=AF.Exp), ["sks"], ["ske"])
        dve(lambda e: e.tensor_copy(out=skeb[:], in_=ske[:]), ["ske"], ["skeb"])
        for l in range(nl):
            dve(lambda e, l=l: e.tensor_scalar(out=gs4[:, l, :], in0=gv[:, l * GV_L + 24:l * GV_L + 28], scalar1=0.25,
                                               scalar2=None, op0=ALU.mult), ["gv"], ["gs4"])

        CAST_PIECE = 16384

        def cast_layer(l):
            for i_, c0 in enumerate(range(0, WCOLS, CAST_PIECE)):
                pace = []
                if l > 0:
                    pace = [("SSMC", l - 1, i_)] if i_ < 4 else [("TAB", l - 1)]
                P.op("pool", lambda e, l=l, c0=c0: e.dma_start(out=WS[l, :, c0:c0 + CAST_PIECE],
                                                                 in_=wpack[l, :, c0:c0 + CAST_PIECE]),
                     reads=pace, writes=[("WS", l, c0 // CAST_PIECE)], dma="cast%d" % ((c0 // CAST_PIECE) % 3))
        cast_layer(0)

        def rsin(out_ap, in_ap, off, tmpa, tmpb, rkeys, wkeys, tkeys):
            if off is None:
                src = in_ap
            else:
                dve(lambda e: e.tensor_scalar(out=tmpb, in0=in_ap, scalar1=off, scalar2=None, op0=ALU.add), rkeys + tkeys, tkeys)
                src = tmpb
            dve(lambda e: e.tensor_scalar(out=tmpa, in0=src, scalar1=1.0 / TWO_PI, scalar2=MAGIC, op0=ALU.mult, op1=ALU.add),
                rkeys + tkeys, tkeys)
            dve(lambda e: e.tensor_scalar(out=tmpa, in0=tmpa, scalar1=-MAGIC, scalar2=-TWO_PI, op0=ALU.add, op1=ALU.mult),
                tkeys, tkeys)
            dve(lambda e: e.tensor_tensor(out=tmpa, in0=tmpa, in1=src, op=ALU.add), rkeys + tkeys, tkeys)
            act(lambda e: e.activation(out=out_ap, in_=tmpa, func=AF.Sin), tkeys + rkeys, wkeys)

        def svv(i):
            return sv[:, i, :]

        def prepass_small():
            for q_ in range(3):
                P.op("sp", lambda e, q_=q_: e.dma_start(out=ssm3[:, q_, 0:nl, :], in_=ssmpd[:, 0:nl, q_ * 32:(q_ + 1) * 32]),
                     writes=["ssm3"], dma="c3")
            ar = ssm3[:, 0, 0:nl, :].rearrange("p l g -> p (l g)")
            ai = ssm3[:, 1, 0:nl, :].rearrange("p l g -> p (l g)")
            ldt = ssm3[:, 2, 0:nl, :].rearrange("p l g -> p (l g)")
            S = ["sv"]
            SP_ = ["ssm3", "sv"]
            act(lambda e: e.activation(out=svv(0), in_=ldt, func=AF.Exp), SP_, S)
            dve(lambda e: e.tensor_tensor(out=svv(1), in0=ar, in1=svv(0), op=ALU.mult), SP_, S)
            dve(lambda e: e.tensor_tensor(out=svv(2), in0=ai, in1=svv(0), op=ALU.mult), SP_, S)
            act(lambda e: e.activation(out=svv(3), in_=svv(1), func=AF.Exp), S, S)
            act(lambda e: e.activation(out=Rsb[:, 0:nl, :].rearrange("p l g -> p (l g)"), in_=svv(1), func=AF.Exp, scale=8.0), S, ["Rsb"])
            rsin(svv(4), svv(2), HP, svv(12), svv(13), S, S, S)
            rsin(svv(5), svv(2), None, svv(12), svv(13), S, S, S)
            dve(lambda e: e.tensor_tensor(out=svv(6), in0=svv(3), in1=svv(4), op=ALU.mult), S, S)
            dve(lambda e: e.tensor_tensor(out=svv(7), in0=svv(3), in1=svv(5), op=ALU.mult), S, S)
            dve(lambda e: e.tensor_tensor(out=svv(8), in0=ar, in1=ar, op=ALU.mult), SP_, S)
            dve(lambda e: e.tensor_tensor(out=svv(12), in0=ai, in1=ai, op=ALU.mult), SP_, S)
            dve(lambda e: e.tensor_tensor(out=svv(8), in0=svv(8), in1=svv(12), op=ALU.add), S, S)
            dve(lambda e: e.reciprocal(out=svv(8), in_=svv(8)), S, S)
            dve(lambda e: e.tensor_scalar(out=svv(9), in0=svv(6), scalar1=-1.0, scalar2=None, op0=ALU.add), S, S)
            dve(lambda e: e.tensor_tensor(out=svv(12), in0=svv(9), in1=ar, op=ALU.mult), SP_, S)
            dve(lambda e: e.tensor_tensor(out=svv(13), in0=svv(7), in1=ai, op=ALU.mult), SP_, S)
            dve(lambda e: e.tensor_tensor(out=svv(12), in0=svv(12), in1=svv(13), op=ALU.add), S, S)
            dve(lambda e: e.tensor_tensor(out=svv(10), in0=svv(12), in1=svv(8), op=ALU.mult), S, S)
            dve(lambda e: e.tensor_tensor(out=svv(12), in0=svv(7), in1=ar, op=ALU.mult), SP_, S)
            dve(lambda e: e.tensor_tensor(out=svv(13), in0=svv(9), in1=ai, op=ALU.mult), SP_, S)
            dve(lambda e: e.tensor_tensor(out=svv(12), in0=svv(12), in1=svv(13), op=ALU.subtract), S, S)
            dve(lambda e: e.tensor_tensor(out=svv(11), in0=svv(12), in1=svv(8), op=ALU.mult), S, S)
            dve(lambda e: e.tensor_copy(out=sv[0:64, 14, :], in_=sv[0:64, 10, :]), S, S)
            dve(lambda e: e.tensor_scalar(out=sv[64:128, 14, :], in0=sv[64:128, 11, :], scalar1=-1.0, scalar2=None, op0=ALU.mult), S, S)
            dve(lambda e: e.tensor_copy(out=sv[0:64, 15, :], in_=sv[0:64, 11, :]), S, S)
            dve(lambda e: e.tensor_copy(out=sv[64:128, 15, :], in_=sv[64:128, 10, :]), S, S)
            dve(lambda e: e.tensor_scalar(out=sv[0:64, 16, :], in0=sv[0:64, 10, :], scalar1=-1.0, scalar2=None, op0=ALU.mult), S, S)
            dve(lambda e: e.tensor_copy(out=sv[64:128, 16, :], in_=sv[64:128, 11, :]), S, S)

            spec = [(1, 1, 0), (0, 0, 0), (0, 0, 1), (3, 2, 2), (3, 2, 3), (3, 2, 4), (3, 2, 5)]
            TT_ = ["ttmp"]

            def kb(i):
                return Kv[:, i, :].unsqueeze(1).to_broadcast([128, NLG, 8])

            def gb8(i):
                return sv[:, i, :].unsqueeze(2).to_broadcast([128, NLG, 8])
            for ti, (ke, ka, oi) in enumerate(spec):
                dve(lambda e, ke=ke: e.tensor_tensor(out=ttmp[:, 0], in0=gb8(1), in1=kb(ke), op=ALU.mult), S + ["Kv"] + TT_, TT_)
                act(lambda e: e.activation(out=ttmp[:, 0], in_=ttmp[:, 0], func=AF.Exp), TT_, TT_)
                dve(lambda e, ka=ka: e.tensor_tensor(out=ttmp[:, 1], in0=gb8(2), in1=kb(ka), op=ALU.mult), S + ["Kv"] + TT_, TT_)
                rsin(ttmp[:, 1], ttmp[:, 1], offc[:, oi:oi + 1], ttmp[:, 2], ttmp[:, 3], ["offc"], TT_, TT_)
                dve(lambda e, ti=ti: e.tensor_tensor(out=trg[:, ti], in0=ttmp[:, 0], in1=ttmp[:, 1], op=ALU.mult),
                    TT_, [("trg", ti)])

            dve(lambda e: e.tensor_scalar(out=svv(17), in0=svv(2), scalar1=8.0, scalar2=None, op0=ALU.mult), S, S)
            dve(lambda e: e.tensor_scalar(out=svv(12), in0=svv(17), scalar1=1.0 / TWO_PI, scalar2=MAGIC, op0=ALU.mult, op1=ALU.add), S, S)
            dve(lambda e: e.tensor_scalar(out=svv(12), in0=svv(12), scalar1=-MAGIC, scalar2=-TWO_PI, op0=ALU.add, op1=ALU.mult), S, S)
            dve(lambda e: e.tensor_tensor(out=thr, in0=svv(12), in1=svv(17), op=ALU.add), S + ["thr"], ["thr"])
            dve(lambda e: e.tensor_scalar(out=svv(18), in0=thr, scalar1=float(NCH), scalar2=None, op0=ALU.mult), ["thr"] + S, S)
            dve(lambda e: e.tensor_scalar(out=svv(12), in0=svv(18), scalar1=1.0 / TWO_PI, scalar2=MAGIC, op0=ALU.mult, op1=ALU.add), S, S)
            dve(lambda e: e.tensor_scalar(out=svv(12), in0=svv(12), scalar1=-MAGIC, scalar2=-TWO_PI, op0=ALU.add, op1=ALU.mult), S, S)
            dve(lambda e: e.tensor_tensor(out=svv(12), in0=svv(12), in1=svv(18), op=ALU.add), S, S)
            act(lambda e: e.activation(out=svv(13), in_=svv(12), func=AF.Abs), S, S)
            act(lambda e: e.activation(out=c64[:, 0:nl, 0, :], in_=svv(13).rearrange("p (l g) -> p l g", l=nl), func=AF.Sin, scale=-1.0, bias=sgnc[:, 1:2]), S + ["sgnc"], ["c64"])
            act(lambda e: e.activation(out=svv(13), in_=svv(12), func=AF.Sin, scale=sgnc[:, 0:1]), S + ["sgnc"], S)
            dve(lambda e: e.tensor_scalar(out=c64[:, 0:nl, 1, :], in0=svv(13).rearrange("p (l g) -> p l g", l=nl), scalar1=-1.0, scalar2=None, op0=ALU.mult), S, ["c64"])

        def prepass(l):
            P.op("sp", lambda e: e.dma_start(out=ssmp, in_=ssmpd[:, l, :]), writes=["ssmp"], dma="c2")
            bre = ssmp[:, 96:608].rearrange("p (g h) -> p g h", h=16)
            bim = ssmp[:, 608:1120].rearrange("p (g h) -> p g h", h=16)
            cre = ssmp[:, 1120:1632].rearrange("p (g h) -> p g h", h=16)
            cim = ssmp[:, 1632:2144].rearrange("p (g h) -> p g h", h=16)
            S = ["sv"]
            SP_ = ["ssmp", "sv"]
            def bc_h(i):
                return sv[:, i, l * NG:(l + 1) * NG].unsqueeze(2).to_broadcast([128, NG, 16])
            ta = cltmp[:, 0, 0]
            tb = cltmp[:, 1, 0]
            C_ = ["cltmp"]
            dve(lambda e: e.tensor_tensor(out=ta, in0=bre, in1=bc_h(14), op=ALU.mult), SP_ + C_, C_)
            dve(lambda e: e.tensor_tensor(out=tb, in0=bim, in1=bc_h(15), op=ALU.mult), SP_ + C_, C_)
            dve(lambda e: e.tensor_tensor(out=LBu, in0=ta, in1=tb, op=ALU.subtract), C_, ["LBu"])
            LBuv = LBu.rearrange("p (a e) h -> p a e h", e=2)
            for e_ in range(2):
                dve(lambda e, e_=e_: e.tensor_copy(
                    out=LBp[:, 0].rearrange("p (a e) (f h) -> p a e f h", e=2, h=16)[:, :, e_, e_, :],
                    in_=LBuv[:, :, e_, :]), ["LBu"], ["LBp"])
            dve(lambda e: e.tensor_tensor(out=ta, in0=bre, in1=bc_h(15), op=ALU.mult), SP_ + C_, C_)
            dve(lambda e: e.tensor_tensor(out=tb, in0=bim, in1=bc_h(16), op=ALU.mult), SP_ + C_, C_)
            dve(lambda e: e.tensor_tensor(out=ta, in0=ta, in1=tb, op=ALU.subtract), C_, C_)
            tav = ta.rearrange("p (a e) h -> p a e h", e=2)
            for e_ in range(2):
                dve(lambda e, e_=e_: e.tensor_copy(
                    out=LBp[:, 1].rearrange("p (a e) (f h) -> p a e f h", e=2, h=16)[:, :, e_, e_, :],
                    in_=tav[:, :, e_, :]), C_, ["LBp"])
            def c_bc(c):
                return c.unsqueeze(1).to_broadcast([128, 4, NG, 16])

            def t_bc(ti, hf):
                return trg[:, ti, l * NG:(l + 1) * NG, 4 * hf:4 * hf + 4].rearrange("p g k -> p k g").unsqueeze(3).to_broadcast([128, 4, NG, 16])
            for hf in range(2):
                dve(lambda e, hf=hf: e.tensor_tensor(out=cltmp[:, 0], in0=c_bc(cre), in1=t_bc(1, hf), op=ALU.mult), ["ssmp", ("trg", 1), "LBp"] + C_, C_)
                dve(lambda e, hf=hf: e.tensor_tensor(out=cltmp[:, 1], in0=c_bc(cim), in1=t_bc(2, hf), op=ALU.mult), ["ssmp", ("trg", 2)] + C_, C_)
                dve(lambda e, hf=hf: e.tensor_tensor(out=CL[:, 4 * hf:4 * hf + 4], in0=cltmp[:, 0], in1=cltmp[:, 1], op=ALU.add), C_, ["CL"])

            for pt in range(4):
                cs = pt % 2
                CK = ("CST", cs)
                cst = CST[:, cs]
                for half in range(2):
                    b = mmp.next()
                    for tt in range(4):
                        tau = half * 4 + tt
                        pe(lambda e, b=b, tt=tt, tau=tau, pt=pt: e.matmul(
                            banks[b][:, tt * 128:(tt + 1) * 128],
                            lhsT=LBu[:, 8 * pt:8 * pt + 8, :].rearrange("p g h -> p (g h)"),
                            rhs=CL[:, tau, 8 * pt:8 * pt + 8, :].rearrange("p g h -> p (g h)"),
                            start=True, stop=True), ["LBu", "CL"], [bk(b)])
                    dve(lambda e, b=b, half=half, cst=cst: e.tensor_tensor(
                        out=cst[:, half * 512:(half + 1) * 512].rearrange("p (t n) -> p t n", n=128),
                        in0=banks[b][:].rearrange("p (t n) -> p t n", n=128),
                        in1=bmask.unsqueeze(1).to_broadcast([128, 4, 128]), op=ALU.mult), [bk(b), "bmask"], [CK])
                for e_ in range(2):
                    bx = mmp.next()
                    by = mmp.next()
                    for q in range(4):
                        g = 8 * pt + 2 * q + e_
                        ds = q % 2
                        dve(lambda e, ds=ds, g=g: e.tensor_tensor(
                            out=DS[:, ds], in0=I2.unsqueeze(1).to_broadcast([128, 8, 64]),
                            in1=trg[:, 0, l * NG + g, :].unsqueeze(2).to_broadcast([128, 8, 64]), op=ALU.mult),
                            ["I2", ("trg", 0)], [("DS", ds)])
                        for v_, b in ((0, bx), (1, by)):
                            pe(lambda e, b=b, v_=v_, g=g, q=q, ds=ds: e.matmul(
                                banks[b][32 * q:32 * q + 32, :], lhsT=LBp[:, v_, g, :],
                                rhs=DS[:, ds].rearrange("p s c -> p (s c)"), start=True, stop=True,
                                tile_position=(0, 32 * q)), ["LBp", ("DS", ds)], [bk(b)])
                    wv = cst[:, 1024:4096].rearrange("p (e s c) -> p e s c", e=2, c=192)
                    act(lambda e, bx=bx, wv=wv, e_=e_: e.copy(out=wv[:, e_, :, 0:64],
                                                               in_=banks[bx][:].rearrange("p (s c) -> p s c", c=64)), [bk(bx)], [CK])
                    dve(lambda e, bx=bx, wv=wv, e_=e_: e.tensor_copy(out=wv[:, e_, :, 128:192],
                                                                     in_=banks[bx][:].rearrange("p (s c) -> p s c", c=64)), [bk(bx)], [CK])
                    act(lambda e, by=by, wv=wv, e_=e_: e.copy(out=wv[:, e_, :, 64:128],
                                                               in_=banks[by][:].rearrange("p (s c) -> p s c", c=64)), [bk(by)], [CK])
                vv = cst[:, 4096:6144].rearrange("p (g v j h) -> p g v j h", v=2, j=8, h=16)
                gsl = slice(8 * pt, 8 * pt + 8)

                def c8(c, gsl=gsl):
                    return c[:, gsl, :].unsqueeze(2).to_broadcast([128, 8, 8, 16])

                def t8(ti, gsl=gsl):
                    return trg[:, ti, l * NG + gsl.start:l * NG + gsl.stop, :].unsqueeze(3).to_broadcast([128, 8, 8, 16])
                for v_, (t_a, t_b) in enumerate(((3, 4), (5, 6))):
                    dve(lambda e, t_a=t_a, c8=c8, t8=t8: e.tensor_tensor(out=vtmp[:, 0], in0=c8(cre), in1=t8(t_a), op=ALU.mult),
                        ["ssmp", ("trg", t_a), "vtmp"], ["vtmp"])
                    dve(lambda e, t_b=t_b, c8=c8, t8=t8: e.tensor_tensor(out=vtmp[:, 1], in0=c8(cim), in1=t8(t_b), op=ALU.mult),
                        ["ssmp", ("trg", t_b), "vtmp"], ["vtmp"])
                    dve(lambda e, v_=v_, vv=vv: e.tensor_tensor(out=vv[:, :, v_], in0=vtmp[:, 0], in1=vtmp[:, 1], op=ALU.add),
                        ["vtmp"], [CK])
                P.op("sp", lambda e, cst=cst, pt=pt: e.dma_start(out=SSMC[l, pt], in_=cst), reads=[CK],
                     writes=[("SSMC", l, pt)], dma="cst%d" % cs)
            TG = ["tg"]
            dve(lambda e: e.tensor_tensor(out=tg[:, 0], in0=thr[:, l * NG:(l + 1) * NG].unsqueeze(2).to_broadcast([128, NG, NCH]),
                                          in1=cidx.unsqueeze(1).to_broadcast([128, NG, NCH]), op=ALU.mult),
                ["thr", "cidx"] + TG, TG)
            dve(lambda e: e.tensor_scalar(out=tg[:, 1], in0=tg[:, 0], scalar1=1.0 / TWO_PI, scalar2=MAGIC, op0=ALU.mult, op1=ALU.add), TG, TG)
            dve(lambda e: e.tensor_scalar(out=tg[:, 1], in0=tg[:, 1], scalar1=-MAGIC, scalar2=-TWO_PI, op0=ALU.add, op1=ALU.mult), TG, TG)
            dve(lambda e: e.tensor_tensor(out=tg[:, 1], in0=tg[:, 1], in1=tg[:, 0], op=ALU.add), TG, TG)
            act(lambda e: e.activation(out=tg[:, 2], in_=tg[:, 1], func=AF.Abs), TG, TG)
            OK_ = ("tgo", 0)
            act(lambda e: e.activation(out=tgo[:, 0, 0], in_=tg[:, 2].rearrange("p g c -> p (g c)"), func=AF.Sin,
                                       scale=-1.0, bias=sgnc[:, 1:2]), TG + ["sgnc"], [OK_])
            act(lambda e: e.activation(out=tgo[:, 0, 1], in_=tg[:, 1].rearrange("p g c -> p (g c)"), func=AF.Sin,
                                       scale=sgnc[:, 0:1]), TG + ["sgnc"], [OK_])
            P.op("sp", lambda e: e.dma_start(out=TAB[l], in_=tgo[:, 0].rearrange("p a n -> p (a n)")),
                 reads=[OK_], writes=[("TAB", l)], dma="tgo0")

        prepass_small()
        for l in range(nl):
            prepass(l)
            if l + 1 < nl:
                cast_layer(l + 1)
        if "KD" in dbg_out:
            pass
        P.barrier()
        pool(lambda e: e.memset(Vx.rearrange("p a b c -> p (a b c)"), 0.0), [], [("Vx", i) for i in range(5)])

        order = ["inU", "inQ", "inKV", "S0", "S1", "S2", "S3", "glu", "out0", "out1"] + \
                ["ffi%d" % c for c in range(11)] + ["ffo%d" % c for c in range(4)] + ["gate0", "proj", "gate1"]
        items = []
        for ti_ in range(ntiles):
            for l in range(nl):
                for nm in order:
                    if nm[0] == "S":
                        pt = int(nm[1])
                        items.append(((ti_, l, nm), SSMC[l, pt], RING_COLS, [("SSMC", l, pt)]))
                    else:
                        o, kt, ncol = CH_OFF[nm]
                        rd = [("WS", l, c) for c in range(o // CAST_PIECE, (o + kt * ncol - 1) // CAST_PIECE + 1)]
                        items.append(((ti_, l, nm), WS[l, :, o:o + kt * ncol], kt * ncol, rd))
        issued = [0]
        free_slots = list(range(NRING))
        slot_of = {}

        def spump():
            while free_slots and issued[0] < len(items):
                k2, src, cols, rd = items[issued[0]]
                sl = free_slots.pop(0)
                slot_of[k2] = sl
                P.op("sp", lambda e, sl=sl, src=src, cols=cols: e.dma_start(out=ring[:, sl, 0:cols], in_=src),
                     reads=rd, writes=[("ring", sl)], dma="ring%d" % sl)
                issued[0] += 1

        def sget(key):
            spump()
            assert key in slot_of, ("ring chunk not resident", key)
            return slot_of[key]

        def srel(key):
            free_slots.append(slot_of.pop(key))
            spump()

        def wtile(slot, ncols, kt, m):
            v = ring[:, slot, :].rearrange("p (k n) -> p k n", n=ncols)
            return v[:, kt, m * 128:(m + 1) * 128]

        def warm(kind):
            for _ in range(WARM.get(kind, 0)):
                pe(lambda e: e.matmul(banks[NB][:], lhsT=identb[:], rhs=mneg[:, 0].rearrange("p i q -> p (i q)"), start=True, stop=True),
                   ["identb", "mneg"], [("warm",)])

        sqn = [0]

        def stat_sq(m):
            if not INCR:
                return 0
            sl_ = sqn[0] % 4
            sqn[0] += 1
            act(lambda e, m=m, sl_=sl_: e.activation(out=sq[:, sl_, :], in_=hT[:, m, :], func=AF.Square), [("hT", m)], [("sq", sl_)])
            return sl_

        def stat_mm(sl_, first_, last_):
            if not INCR:
                return
            pe(lambda e, sl_=sl_: e.matmul(banks[NB][:], lhsT=ones10[:], rhs=sq[:, sl_, :], start=first_, stop=last_),
               [("sq", sl_), "ones10"], [bk(NB)])

        def rmsnorm(src_fn, nk, gcol, ones_t, ones_key, dst_fn, srckeys, dstkeys, after=None, bank=None, pre=False):
            pre = pre and INCR
            b = (NB if pre else mmp.next()) if bank is None else bank
            for k in range(0 if pre else nk):
                act(lambda e, k=k: e.activation(out=sq[:, k % 4, :], in_=src_fn(k), func=AF.Square), [srckeys[k]], [("sq", k % 4)])
                pe(lambda e, k=k, b=b: e.matmul(banks[b][:], lhsT=ones_t[:], rhs=sq[:, k % 4, :], start=(k == 0), stop=(k == nk - 1)),
                   [("sq", k % 4), ones_key], [bk(b)])
            act(lambda e, b=b: e.activation(out=msb, in_=banks[b][:], func=AF.Ln, bias=epsc[:, 0:1]), [bk(b), "epsc"], ["msb"])
            act(lambda e: e.activation(out=rstd, in_=msb, func=AF.Exp, scale=-0.5), ["msb"], ["rstd"])
            for k in range(nk):
                dve(lambda e, k=k: e.scalar_tensor_tensor(out=dst_fn(k), in0=src_fn(k), scalar=gcol(k), in1=rstd,
                                                          op0=ALU.mult, op1=ALU.mult),
                    [srckeys[k], "rstd", "gv", "gs4"], [dstkeys[k]])
                if after is not None:
                    after(k)

        HK = [("hT", k) for k in range(8)]
        HNK = [("hn", k) for k in range(8)]

        def layer(ti_, l):
            gvb = l * GV_L
            first = (ti_ == 0)
            tsl = slice(ti_ * T, (ti_ + 1) * T)
            P.op("sp", lambda e: e.dma_start(out=tabs, in_=TAB[l]), reads=[("TAB", l)], writes=["tabs"], dma="tabs")
            P.op("pool", lambda e: e.dma_start(out=pTs, in_=pT[l, :, tsl].rearrange("(k p) t -> p k t", p=128)),
                 writes=["pTs"], dma="pTs")
            dve(lambda e: e.tensor_copy(out=kT[:, 0:128], in_=kcar[:, l, :]), ["kcar"], [("kT", 0)])
            dve(lambda e: e.tensor_copy(out=Vx[:, 0].rearrange("p a n -> p (a n)"), in_=vcar[:, l, :]), ["vcar"], [("Vx", 0)])

            rmsnorm(lambda k: hT[:, k, :], 8, lambda k: gv[:, gvb + k:gvb + k + 1], ones10, "ones10", lambda k: hn[:, k, :], HK, HNK, pre=(l > 0))

            def proj_tiles(name, ntile, dst_fn, dkeys, bp=None):
                s = sget((ti_, l, name))
                _, kt_total, ncols = CH_OFF[name]
                for m in range(ntile):
                    b = (bp or mmp).next()
                    for k in range(8):
                        pe(lambda e, b=b, k=k, m=m, s=s: e.matmul(banks[b][:], lhsT=wtile(s, ncols, k, m), rhs=hn[:, k, :],
                                                                  start=(k == 0), stop=(k == 7)),
                           [("ring", s), ("hn", k)], [bk(b)])
                    if m % 2 == 0:
                        act(lambda e, b=b, m=m: e.copy(out=dst_fn(m), in_=banks[b][:]), [bk(b)], [dkeys[m]])
                    else:
                        dve(lambda e, b=b, m=m: e.tensor_copy(out=dst_fn(m), in_=banks[b][:]), [bk(b)], [dkeys[m]])
                return s
            warm("win")
            proj_tiles("inU", 4, lambda m: uT[:, m, :], [("uT", m) for m in range(4)])
            srel((ti_, l, "inU"))

            proj_tiles("inQ", 4, lambda m: qT[:, m, :], [("qT", m) for m in range(4)])
            srel((ti_, l, "inQ"))
            s = proj_tiles("inKV", 1, lambda m: kT[:, 128:128 + T], [("kT", 1)])
            for blk in range(4):
                b = mmp.next()
                for k in range(8):
                    pe(lambda e, b=b, k=k, blk=blk, s=s: e.matmul(
                        banks[b][:, 0:128], lhsT=hn[:, k, blk * 128:(blk + 1) * 128],
                        rhs=ring[:, s, 0:2048].rearrange("p (k n) -> p k n", n=256)[:, k, 128:256],
                        start=(k == 0), stop=(k == 7)), [("ring", s), ("hn", k)], [bk(b)])
                act(lambda e, b=b, blk=blk: e.copy(out=Vx[:, blk + 1, 0, 0:64], in_=banks[b][:, 0:64]), [bk(b)], [("Vx", blk + 1)])
                dve(lambda e, b=b, blk=blk: e.tensor_copy(out=Vx[:, blk + 1, 1, 64:128], in_=banks[b][:, 64:128]), [bk(b)], [("Vx", blk + 1)])
            srel((ti_, l, "inKV"))
            dve(lambda e: e.tensor_copy(out=kcar[:, l, :], in_=kT[:, T:T + 128]), [("kT", 1)], ["kcar"])
            dve(lambda e: e.tensor_copy(out=vcar[:, l, :], in_=Vx[:, 4].rearrange("p a n -> p (a n)")), [("Vx", 4)], ["vcar"])
            if l == 0 and ti_ == 0:
                tap("uT", uT[:, 0, :], [("uT", 0)])

            freeb = list(range(7))

            def balloc():
                while not freeb:
                    yield
                return freeb.pop(0)

            def bfree(b):
                freeb.append(b)

            cosv = tabs[:, 0:NG * NCH].rearrange("p (g c) -> p g c", c=NCH)
            sinv = tabs[:, NG * NCH:2 * NG * NCH].rearrange("p (g c) -> p g c", c=NCH)

            def gsel(tv, pp, q):
                return tv[:, 16 * pp:16 * pp + 16, :].rearrange("p (a r) c -> p a r c", a=2)[:, :, 2 * q:2 * q + 2, :]

            def zsel(tv, pp, q, lo, hi):
                return tv[:, pp, 4 * q:4 * q + 4, lo:hi].rearrange("p (a e) c -> p a e c", a=2)

            def attn_chain():
                for blk in range(4):
                    has_prev = not (first and blk == 0)
                    ps_ = blk % 2
                    sbk = {}
                    ncp = 2 if has_prev else 1
                    for kh in range(2):
                        for cp_ in range(ncp):
                            b = yield from balloc()
                            sbk[(kh, cp_)] = b
                            koff = 128 + blk * 128 - cp_ * 128
                            pe(lambda e, b=b, kh=kh, koff=koff, blk=blk: e.matmul(
                                banks[b][:].rearrange("p (i q) -> p i q", q=128),
                                lhsT=kT[64 * kh:64 * kh + 64, koff:koff + 128],
                                rhs=qT[64 * kh:64 * kh + 64, :, blk * 128:(blk + 1) * 128], start=True, stop=False,
                                tile_position=(64 * kh, 0)),
                               [("kT", 0), ("kT", 1)] + [("qT", m) for m in range(4)], [bk(b)])
                            pe(lambda e, b=b, cp_=cp_: e.matmul(banks[b][:], lhsT=identb[:], rhs=mneg[:, cp_].rearrange("p i q -> p (i q)"),
                                                                start=False, stop=True), ["identb", "mneg"], [bk(b)])
                    yield
                    for kh in range(2):
                        for cp_ in range(ncp):
                            b = sbk[(kh, cp_)]
                            pk = ("Pb", ps_, kh, cp_)
                            act(lambda e, b=b, kh=kh, cp_=cp_, ps_=ps_: e.activation(out=Pb[:, ps_, kh, cp_, :], in_=banks[b][:], func=AF.Exp, scale=0.125),
                                [bk(b)], [pk])
                            bfree(b)
                    yield
                    ob = yield from balloc()
                    db = yield from balloc()
                    warm("pv")
                    seq_ = [(kh, cp_) for kh in range(2) for cp_ in range(ncp)]
                    for n_, (kh, cp_) in enumerate(seq_):
                        pe(lambda e, kh=kh, cp_=cp_, n_=n_, ps_=ps_, blk=blk, ob=ob: e.matmul(
                            banks[ob][:], lhsT=Vx[:, blk + 1 - cp_, kh, :], rhs=Pb[:, ps_, kh, cp_, :],
                            start=(n_ == 0), stop=(n_ == len(seq_) - 1)),
                           [("Pb", ps_, kh, cp_), ("Vx", blk + 1 - cp_)], [bk(ob)])
                    for n_, (kh, cp_) in enumerate(seq_):
                        oo, okey = (OA, "OA") if kh == 0 else (OB, "OB")
                        pe(lambda e, kh=kh, cp_=cp_, n_=n_, oo=oo, ps_=ps_, db=db: e.matmul(
                            banks[db][:], lhsT=oo[:], rhs=Pb[:, ps_, kh, cp_, :], start=(n_ == 0), stop=False),
                           [("Pb", ps_, kh, cp_), okey], [bk(db)])
                    for kh in range(2):
                        pe(lambda e, kh=kh, db=db: e.matmul(banks[db][:].rearrange("p (i q) -> p i q", q=128), lhsT=orow[:, kh, :],
                                                            rhs=skeb[:, l * 8 + 4 * kh:l * 8 + 4 * kh + 4].unsqueeze(2).to_broadcast([1, 4, 128]),
                                                            start=False, stop=(kh == 1)),
                           ["orow", "skeb"], [bk(db)])
                    yield
                    dve(lambda e, db=db: e.reciprocal(out=rden, in_=banks[db][:]), [bk(db)], ["rden"])
                    bfree(db)
                    dve(lambda e, ob=ob, blk=blk: e.tensor_tensor(
                        out=atf[:, :, blk * 128:(blk + 1) * 128],
                        in0=banks[ob][:].rearrange("p (i q) -> p i q", q=128),
                        in1=rden.rearrange("p (i q) -> p i q", q=128), op=ALU.mult),
                        [bk(ob), "rden"], [("atf", i) for i in range(4)])
                    bfree(ob)
                    yield
                if l == 0 and ti_ == 0:
                    tap("at0", atf[:, 0, :], [("atf", 0)])
                nb_ = yield from balloc()
                warm("norm")
                rmsnorm(lambda k: atf[:, k, :], 4, lambda k: gv[:, gvb + 28 + k:gvb + 29 + k], ones9, "ones9", lambda k: mixed[:, 4 + k, :],
                        [("atf", k) for k in range(4)], [("hn", 4 + k) for k in range(4)], bank=nb_)
                bfree(nb_)

            def ssm_chain():
                slots = [sget((ti_, l, "S%d" % pt)) for pt in range(4)]
                if not first:
                    rb = yield from balloc()
                    pe(lambda e, rb=rb: e.matmul(banks[rb][:, 0:NG], lhsT=perm[:], rhs=Zcar[:, l, :], start=True, stop=True),
                       ["perm", "Zcar"], [bk(rb)])
                    dve(lambda e, rb=rb: e.tensor_tensor(out=zrot[:, 0, :], in0=banks[rb][:, 0:NG], in1=c64[:, l, 1, :], op=ALU.mult),
                        [bk(rb), "c64"], ["zrot0"])
                    bfree(rb)
                    dve(lambda e: e.tensor_tensor(out=zrot[:, 1, :], in0=Zcar[:, l, :], in1=c64[:, l, 0, :], op=ALU.mult),
                        ["Zcar", "c64"], ["zrot1"])
                    dve(lambda e: e.tensor_tensor(out=Zcar[:, l, :], in0=zrot[:, 0, :], in1=zrot[:, 1, :], op=ALU.add),
                        ["zrot0", "zrot1"], ["Zcar"])
                for pp in range(2):
                    dve(lambda e, pp=pp: e.tensor_copy(
                        out=Zb[:, pp, :, 0].rearrange("p (q t e) -> p q t e", q=4, t=2),
                        in_=Zcar[:, l, 16 * pp:16 * pp + 16].rearrange("p (t q e) -> p q t e", t=2, q=4)),
                        ["Zcar"], [("Zb", pp, q) for q in range(4)])
                for pp in range(2):
                    sb4 = []
                    for q in range(4):
                        b = yield from balloc()
                        sb4.append(b)
                    for ptl in range(2):
                        pt = 2 * pp + ptl
                        sl = slots[pt]
                        wv = ring[:, sl, 1024:4096].rearrange("p (e s c) -> p e s c", e=2, c=192)
                        for e_ in range(2):
                            for v_ in range(2):
                                for s_ in range(8):
                                    for q in range(4):
                                        b = sb4[q]
                                        col = ((ptl * 2 + e_) * 2 + v_) * NCH
                                        pe(lambda e, b=b, q=q, e_=e_, v_=v_, s_=s_, pt=pt, wv=wv, col=col: e.matmul(
                                            banks[b][:, col:col + NCH],
                                            lhsT=wv[32 * q:32 * q + 32, e_, s_, 64 * v_:64 * v_ + 128],
                                            rhs=uT[32 * q:32 * q + 32, pt, :].rearrange("p (c j) -> p c j", j=8)[:, :, s_],
                                            start=(s_ == 0), stop=(s_ == 7), tile_position=(32 * q, 0)),
                                           [("ring", sl), ("uT", pt)], [bk(b)])
                    yield
                    for q in range(4):
                        b = sb4[q]
                        rs = q % 2
                        bv = banks[b][:].rearrange("p (a e v c) -> p a e v c", a=2, e=2, v=2)
                        dve(lambda e, bv=bv, rs=rs, q=q, pp=pp: e.tensor_tensor(
                            out=srot[:, rs, 0, :].rearrange("p (a e c) -> p a e c", a=2, e=2),
                            in0=bv[:, :, :, 0, :], in1=gsel(cosv, pp, q), op=ALU.mult), [bk(b), "tabs"], [("srot", rs, 0)])
                        dve(lambda e, bv=bv, rs=rs, q=q, pp=pp: e.tensor_tensor(
                            out=srot[:, rs, 1, :].rearrange("p (a e c) -> p a e c", a=2, e=2),
                            in0=bv[:, :, :, 1, :], in1=gsel(sinv, pp, q), op=ALU.mult), [bk(b), "tabs"], [("srot", rs, 1)])
                        bfree(b)
                        dve(lambda e, rs=rs: e.tensor_tensor(out=srot[:, rs, 0, :], in0=srot[:, rs, 0, :], in1=srot[:, rs, 1, :], op=ALU.add),
                            [("srot", rs, 0), ("srot", rs, 1)], [("srot", rs, 0)])
                        zk = ("Zb", pp, q)
                        for idx in range(4):
                            ptl, e_ = idx // 2, idx % 2
                            g = 16 * pp + 8 * ptl + 2 * q + e_
                            r = 4 * q + idx
                            dve(lambda e, g=g, r=r, idx=idx, rs=rs, pp=pp: e.tensor_tensor_scan(
                                out=Zb[:, pp, r, 1:NCH + 1], data0=Rsb[:, l, g:g + 1].to_broadcast([128, NCH]),
                                data1=srot[:, rs, 0, idx * NCH:(idx + 1) * NCH], initial=Zb[:, pp, r, 0:1],
                                op0=ALU.mult, op1=ALU.add), [("srot", rs, 0), "Rsb", zk], [zk])
                        dve(lambda e, q=q, pp=pp: e.tensor_tensor(out=zsel(ZCb, pp, q, 0, NCH), in0=zsel(Zb, pp, q, 0, NCH),
                                                                  in1=gsel(cosv, pp, q), op=ALU.mult), [zk, "tabs"], [("ZCb", pp, q)])
                        dve(lambda e, q=q, pp=pp: e.tensor_tensor(out=zsel(ZSb, pp, q, 0, NCH), in0=zsel(Zb, pp, q, 0, NCH),
                                                                  in1=gsel(sinv, pp, q), op=ALU.mult), [zk, "tabs"], [("ZSb", pp, q)])
                        if q % 2 == 1:
                            yield
                    dve(lambda e, pp=pp: e.tensor_copy(
                        out=Zcar[:, l, 16 * pp:16 * pp + 16].rearrange("p (t q e) -> p q t e", t=2, q=4),
                        in_=Zb[:, pp, :, NCH].rearrange("p (q t e) -> p q t e", q=4, t=2)),
                        [("Zb", pp, q) for q in range(4)], ["Zcar"])
                for pt in range(4):
                    pp, ptl = pt // 2, pt % 2
                    sl = slots[pt]
                    yb = yield from balloc()
                    kd = ring[:, sl, 0:1024].rearrange("p (t n) -> p t n", n=128)
                    vv = ring[:, sl, 4096:6144].rearrange("p (g v n) -> p g v n", v=2, n=128)
                    uv = uT[:, pt, :].rearrange("p (c j) -> p c j", j=8)
                    yv = banks[yb][:].rearrange("p (c j) -> p c j", j=8)
                    for tau in range(8):
                        pe(lambda e, tau=tau, kd=kd, uv=uv, yv=yv: e.matmul(yv[:, :, tau:8], lhsT=kd[:, tau, :], rhs=uv[:, :, 0:8 - tau],
                                                                           start=(tau == 0), stop=False),
                           [("ring", sl), ("uT", pt)], [bk(yb)])
                    ycb = []
                    for _ in range(2):
                        b = yield from balloc()
                        ycb.append(b)
                    if pt == 0:
                        warm("s3")
                    for gi in range(8):
                        q, e_ = gi // 2, gi % 2
                        r = 4 * q + 2 * ptl + e_
                        b = ycb[gi // 4]
                        cs_ = (gi % 4) * 128
                        zkeys = [("ZCb", pp, q), ("ZSb", pp, q)]
                        pe(lambda e, r=r, gi=gi, b=b, cs_=cs_, vv=vv, pp=pp: e.matmul(banks[b][0:64, cs_:cs_ + 128], lhsT=ZCb[:, pp, r, :],
                                                                                     rhs=vv[:, gi, 0, :], start=True, stop=False),
                           zkeys + [("ring", sl)], [bk(b)])
                        pe(lambda e, r=r, gi=gi, b=b, cs_=cs_, vv=vv, pp=pp: e.matmul(banks[b][0:64, cs_:cs_ + 128], lhsT=ZSb[:, pp, r, :],
                                                                                     rhs=vv[:, gi, 1, :], start=False, stop=True),
                           zkeys + [("ring", sl)], [bk(b)])
                    yield
                    yk = ("Ycs", ptl)
                    ycv = Ycs[:, ptl].rearrange("p j g h -> p g j h")
                    act(lambda e, ycv=ycv, b0=ycb[0]: e.copy(out=ycv[:, 0:4], in_=banks[b0][0:64, :].rearrange("p (g j h) -> p g j h", j=8, h=16)),
                        [bk(ycb[0])], [yk])
                    act(lambda e, ycv=ycv, b1=ycb[1]: e.copy(out=ycv[:, 4:8], in_=banks[b1][0:64, :].rearrange("p (g j h) -> p g j h", j=8, h=16)),
                        [bk(ycb[1])], [yk])
                    bfree(ycb[0])
                    bfree(ycb[1])
                    warm("tr")
                    for j in range(8):
                        pe(lambda e, j=j, ptl=ptl, yv=yv: e.matmul(yv[:, :, j], lhsT=Ycs[:, ptl, j].rearrange("p g h -> p (g h)"),
                                                                  rhs=identb[0:64, 0:64], start=False, stop=(j == 7)),
                           [yk, "identb"], [bk(yb)])
                    srel((ti_, l, "S%d" % pt))
                    yield
                    ys = pt % 2
                    dcol = gv[:, gvb + 32 + pt:gvb + 33 + pt]
                    dve(lambda e, ys=ys, pt=pt, yb=yb, dcol=dcol: e.scalar_tensor_tensor(
                        out=ysb[:, ys, :], in0=uT[:, pt, :], scalar=dcol, in1=banks[yb][:], op0=ALU.mult, op1=ALU.add),
                        [("uT", pt), bk(yb), "gv"], [("ysb", ys)])
                    bfree(yb)
                    if l == 0 and ti_ == 0 and pt == 0:
                        tap("y0", ysb[:, 0, :], [("ysb", 0)])
                    dve(lambda e, ys=ys: e.scalar_tensor_tensor(out=t1[:, ys, :], in0=ysb[:, ys, :], scalar=0.044715, in1=ysb[:, ys, :],
                                                                op0=ALU.mult, op1=ALU.mult), [("ysb", ys)], [("t1", ys)])
                    dve(lambda e, ys=ys: e.scalar_tensor_tensor(out=t1[:, ys, :], in0=t1[:, ys, :], scalar=1.0, in1=ysb[:, ys, :],
                                                                op0=ALU.add, op1=ALU.mult), [("t1", ys), ("ysb", ys)], [("t1", ys)])
                    act(lambda e, ys=ys: e.activation(out=t1[:, ys, :], in_=t1[:, ys, :], func=AF.Tanh, scale=0.7978845608028654),
                        [("t1", ys)], [("t1", ys)])
                    dve(lambda e, ys=ys, pt=pt: e.scalar_tensor_tensor(out=gb[:, pt, :], in0=t1[:, ys, :], scalar=1.0, in1=ysb[:, ys, :],
                                                                      op0=ALU.add, op1=ALU.mult),
                        [("t1", ys), ("ysb", ys)], [("gb", pt)])
                    yield
                s = sget((ti_, l, "glu"))
                gbk = []
                for m in range(4):
                    b = yield from balloc()
                    gbk.append(b)
                for k in range(4):
                    for m in range(4):
                        b = gbk[m]
                        pe(lambda e, b=b, k=k, m=m, s=s: e.matmul(banks[b][:], lhsT=wtile(s, 512, k, m), rhs=gb[:, k, :],
                                                                  start=(k == 0), stop=(k == 3)), [("ring", s), ("gb", k)], [bk(b)])
                for m in range(4):
                    b = gbk[m]
                    ts_ = m % 2
                    act(lambda e, b=b, ts_=ts_: e.activation(out=tht[:, ts_, :], in_=banks[b][:], func=AF.Tanh, scale=0.25), [bk(b)], [("tht", ts_)])
                    bfree(b)
                    dve(lambda e, m=m, ts_=ts_: e.scalar_tensor_tensor(out=gl2[:, m, :], in0=tht[:, ts_, :], scalar=1.0, in1=gb[:, m, :],
                                                                      op0=ALU.add, op1=ALU.mult), [("tht", ts_), ("gb", m)], [("gl2", m)])
                srel((ti_, l, "glu"))
                yield
                if l == 0 and ti_ == 0:
                    tap("so0", gl2[:, 0, :], [("gl2", 0)])
                nb_ = yield from balloc()
                warm("norm")
                rmsnorm(lambda k: gl2[:, k, :], 4, lambda k: gs4[:, l, k:k + 1], ones13, "ones13", lambda k: mixed[:, k, :],
                        [("gl2", k) for k in range(4)], [("hn", k) for k in range(4)], bank=nb_)
                bfree(nb_)

            ssm_c = ssm_chain()
            att_c = attn_chain()
            chains = [ssm_c, att_c]
            stall = 0
            nstep = 0
            while chains:
                nfree0 = len(freeb)
                for c_ in list(chains):
                    if c_ is att_c and ssm_c in chains and nstep < ATT_DELAY:
                        continue
                    try:
                        next(c_)
                    except StopIteration:
                        chains.remove(c_)
                nstep += 1
                stall = stall + 1 if (len(freeb) == 0 and nfree0 == 0) else 0
                assert stall < 50, "bank allocation deadlock"

            LAG = 3

            def resid_mm(names, nkt, rhs_fn, rkeys):
                pendq = []
                for ci, name in enumerate(names):
                    s = sget((ti_, l, name))
                    _, kt_total, ncols = CH_OFF[name]
                    for mm_ in range(ncols // 128):
                        m = ci * (ncols // 128) + mm_
                        b = mmp.next()
                        for k in range(nkt):
                            pe(lambda e, b=b, k=k, mm_=mm_, s=s, ncols=ncols: e.matmul(
                                banks[b][:], lhsT=wtile(s, ncols, k, mm_), rhs=rhs_fn(k), start=(k == 0), stop=(k == nkt - 1)),
                               [("ring", s), rkeys[k]], [bk(b)])
                        if len(pendq) >= LAG:
                            p_ = pendq.pop(0)
                            stat_mm(p_[0], p_[1] == 0, False)
                        dve(lambda e, b=b, m=m: e.tensor_tensor(out=hT[:, m, :], in0=banks[b][:], in1=hT[:, m, :], op=ALU.add),
                            [bk(b), ("hT", m)], [("hT", m)])
                        pendq.append((stat_sq(m), m))
                    srel((ti_, l, name))
                while pendq:
                    p_ = pendq.pop(0)
                    stat_mm(p_[0], p_[1] == 0, len(pendq) == 0)
            warm("wout")
            resid_mm(["out0", "out1"], 8, lambda k: mixed[:, k, :], HNK)
            if l == 0 and ti_ == 0:
                tap("h1", hT[:, 0, :], [("hT", 0)])

            rmsnorm(lambda k: hT[:, k, :], 8, lambda k: gv[:, gvb + 8 + k:gvb + 9 + k], ones10, "ones10", lambda k: hn[:, k, :], HK, HNK, pre=True)
            warm("ffn")
            for c in range(11):
                s = sget((ti_, l, "ffi%d" % c))
                for jj in range(2):
                    j = 2 * c + jj
                    bg = mmp.next()
                    bu = mmp.next()
                    for k in range(8):
                        pe(lambda e, k=k, s=s, jj=jj, bg=bg: e.matmul(banks[bg][:], lhsT=wtile(s, 512, k, 2 * jj), rhs=hn[:, k, :],
                                                                     start=(k == 0), stop=(k == 7)), [("ring", s), ("hn", k)], [bk(bg)])
                    for k in range(8):
                        pe(lambda e, k=k, s=s, jj=jj, bu=bu: e.matmul(banks[bu][:], lhsT=wtile(s, 512, k, 2 * jj + 1), rhs=hn[:, k, :],
                                                                     start=(k == 0), stop=(k == 7)), [("ring", s), ("hn", k)], [bk(bu)])
                    ss = j % 2
                    act(lambda e, bg=bg, ss=ss: e.activation(out=sgt[:, ss, :], in_=banks[bg][:], func=AF.Silu), [bk(bg)], [("sgt", ss)])
                    dve(lambda e, bu=bu, ss=ss, j=j: e.tensor_tensor(out=hid[:, j, :], in0=banks[bu][:], in1=sgt[:, ss, :], op=ALU.mult),
                        [bk(bu), ("sgt", ss)], [("hid", j)])
                srel((ti_, l, "ffi%d" % c))
            resid_mm(["ffo%d" % c for c in range(4)], 22, lambda k: hid[:, k, :], [("hid", k) for k in range(22)])
            if l == 0 and ti_ == 0:
                tap("h2", hT[:, 0, :], [("hT", 0)])

            rmsnorm(lambda k: hT[:, k, :], 8, lambda k: gv[:, gvb + 16 + k:gvb + 17 + k], ones10, "ones10", lambda k: hn[:, k, :], HK, HNK, pre=True)
            warm("ple")
            sg0 = sget((ti_, l, "gate0"))
            spj = sget((ti_, l, "proj"))
            pendq = []
            for half in range(2):
                if half == 1:
                    srel((ti_, l, "gate0"))
                    sg0 = sget((ti_, l, "gate1"))
                bps = []
                for mm_ in range(4):
                    m = 4 * half + mm_
                    bp = mmp.next()
                    bps.append(bp)
                    for k in range(2):
                        pe(lambda e, bp=bp, k=k, m=m, spj=spj: e.matmul(banks[bp][:], lhsT=wtile(spj, 1024, k, m), rhs=pTs[:, k, :],
                                                                       start=(k == 0), stop=(k == 1)), [("ring", spj), "pTs"], [bk(bp)])
                for mm_ in range(4):
                    m = 4 * half + mm_
                    bp = bps[mm_]
                    bg = mmp.next()
                    for k in range(8):
                        pe(lambda e, bg=bg, k=k, m=m, sg0=sg0: e.matmul(banks[bg][:], lhsT=wtile(sg0, 512, k, m % 4), rhs=hn[:, k, :],
                                                                       start=(k == 0), stop=(k == 7)), [("ring", sg0), ("hn", k)], [bk(bg)])
                    if len(pendq) >= LAG:
                        p_ = pendq.pop(0)
                        stat_mm(p_[0], p_[1] == 0, False)
                    ts_ = m % 2
                    act(lambda e, bg=bg, ts_=ts_: e.activation(out=tht[:, ts_, :], in_=banks[bg][:], func=AF.Tanh, scale=0.5),
                        [bk(bg)], [("tht", ts_)])
                    dve(lambda e, bp=bp, ts_=ts_: e.scalar_tensor_tensor(out=ott[:, 0, :], in0=tht[:, ts_, :], scalar=1.0, in1=banks[bp][:],
                                                                        op0=ALU.add, op1=ALU.mult), [("tht", ts_), bk(bp)], [("ott", 0)])
                    dve(lambda e, ts_=ts_, m=m: e.scalar_tensor_tensor(out=hT[:, m, :], in0=ott[:, 0, :], scalar=0.5, in1=hT[:, m, :],
                                                                      op0=ALU.mult, op1=ALU.add), [("ott", 0), ("hT", m)], [("hT", m)])
                    pendq.append((stat_sq(m), m))
            while pendq:
                p_ = pendq.pop(0)
                stat_mm(p_[0], p_[1] == 0, len(pendq) == 0)
            srel((ti_, l, "gate1"))
            srel((ti_, l, "proj"))
            if l == 0 and ti_ == 0:
                tap("h3", hT[:, 0, :], [("hT", 0)])

        for ti_ in range(ntiles):
            tsl = slice(ti_ * T, (ti_ + 1) * T)
            if ti_ == 0:
                for k in range(8):
                    P.op("sp", lambda e, tsl=tsl, k=k: e.dma_start(out=hT[:, k, :], in_=xT[k * 128:(k + 1) * 128, tsl]),
                         writes=[("hT", k)], dma="xld%d" % (k % 4))
            for l in range(nl):
                layer(ti_, l)
            gfb = NL * GV_L

            def store(k, tsl=tsl, ti_=ti_):
                o = P.op("sp", lambda e: e.dma_start(out=outT[k * 128:(k + 1) * 128, tsl], in_=(gl2 if k < 4 else atf)[:, k % 4, :]),
                         reads=[("gl2" if k < 4 else "atf", k % 4)], dma="out%d" % k)
                finals.append(o)
                if ti_ + 1 < ntiles:
                    nsl = slice((ti_ + 1) * T, (ti_ + 2) * T)
                    P.op("sp", lambda e: e.dma_start(out=hT[:, k, :], in_=xT[k * 128:(k + 1) * 128, nsl]),
                         writes=[("hT", k)], dma="xld%d" % (k % 4))
            rmsnorm(lambda k: hT[:, k, :], 8, lambda k: gv[:, gfb + k:gfb + k + 1], ones10, "ones10",
                    lambda k: (gl2 if k < 4 else atf)[:, k % 4, :], HK, [("gl2" if k < 4 else "atf", k % 4) for k in range(8)],
                    after=store, pre=True)
        P.emit(final_waits=finals)
    return nc


def kernel(**inputs):
    wpack, gv, sk, ssmp = _host_pack(inputs)
    x = np.asarray(inputs["x"], np.float32)
    p = np.asarray(inputs["p"], np.float32)
    nc = build()
    in_maps = []
    for c in range(8):
        in_maps.append({"xT": np.ascontiguousarray(x[c].T),
                        "pT": np.ascontiguousarray(p[:, c].transpose(0, 2, 1)),
                        "wpack": wpack, "gv": gv, "sk": sk, "ssmp": ssmp})
    res = run_bass_kernel_spmd(nc, in_maps, core_ids=list(range(8)))
    out = np.stack([np.asarray(res.results[c]["outT"]).T for c in range(8)])
    return np.ascontiguousarray(out.astype(np.float32))
```

```python
import contextlib
import math

import numpy as np
import concourse.bass as bass
import concourse.mybir as mybir
from concourse.bass_utils import run_bass_kernel_spmd

F32 = mybir.dt.float32
BF16 = mybir.dt.bfloat16
I32 = mybir.dt.int32
ALU = mybir.AluOpType
AF = mybir.ActivationFunctionType

D_MODEL = 1024
SEQ = 4096
NL = 4
T = 512
NTILES = SEQ // T
FFN = 2816
NG = 32
NCH = T // 8
EPS = 1e-6
TWO_PI = 2.0 * math.pi
MAGIC = 12582912.0
RING_COLS = 6144
NRING = 4

ENGS = ("pe", "act", "dve", "pool", "sp")


class _Op:
    __slots__ = ("eng", "fn", "deps", "dma", "marked", "tok", "raw", "idx")

    def __init__(self, eng, fn, deps, dma):
        self.eng = eng
        self.fn = fn
        self.deps = deps
        self.dma = dma
        self.marked = False
        self.tok = None


class Prog:
    def __init__(self, nc, strict_same=True):
        self.nc = nc
        self.strict_same = strict_same
        self.ops = {e: [] for e in ENGS}
        self.last_write = {}
        self.readers = {}
        self.dma_slots = {}
        self.all_ops = []
        self.pending = {}

    def barrier(self, exclude=("cast",)):
        fr = []
        for e in ENGS:
            lst = [o for o in self.ops[e] if o.dma is None]
            if lst:
                fr.append(lst[-1])
        for s, lst in self.dma_slots.items():
            if not s.startswith(exclude):
                fr.append(lst[-1])
        self.pending = {e: list(fr) for e in ENGS}

    def op(self, eng, fn, reads=(), writes=(), dma=None):
        deps = []
        if self.pending.get(eng):
            deps.extend(self.pending[eng])
            self.pending[eng] = None
        raw = set()
        for k in reads:
            w = self.last_write.get(k)
            if w is not None:
                deps.append(w)
                raw.add(id(w))
        for k in writes:
            w = self.last_write.get(k)
            if w is not None:
                deps.append(w)
            deps.extend(self.readers.get(k, ()))
        o = _Op(eng, fn, deps, dma)
        o.raw = raw
        if dma is not None:
            self.dma_slots.setdefault(dma, []).append(o)
        for k in writes:
            self.last_write[k] = o
            self.readers[k] = []
        for k in reads:
            self.readers.setdefault(k, []).append(o)
        o.idx = len(self.all_ops)
        self.ops[eng].append(o)
        self.all_ops.append(o)
        return o

    def emit(self, final_waits=()):
        nc = self.nc
        for o in self.all_ops:
            nd = []
            seen = set()
            for d in o.deps:
                if d is o or id(d) in seen:
                    continue
                seen.add(id(d))
                if d.dma is None and o.dma is None and d.eng == o.eng:
                    if d.eng in ("pe", "sp") or not self.strict_same:
                        continue
                    if id(d) not in o.raw and d.eng != "pool":
                        continue
                nd.append(d)
            best = {}
            for d in nd:
                g = ("d", d.dma) if d.dma is not None else ("e", d.eng)
                if g not in best or best[g].idx < d.idx:
                    best[g] = d
            o.deps = list(best.values())
            for d in o.deps:
                d.marked = True
        for o in final_waits:
            o.marked = True
        for e in ENGS:
            c = 0
            for o in self.ops[e]:
                if o.dma is None and o.marked:
                    c += 1
                    o.tok = (("e", e), c)
        for s, lst in self.dma_slots.items():
            c = 0
            for o in lst:
                c += 16
                o.tok = (("d", s), c)
        semnames = [("e", e) for e in ENGS] + [("d", s) for s in self.dma_slots]
        assert len(semnames) < 140, len(semnames)
        with contextlib.ExitStack() as st:
            sems = {}
            for i, k in enumerate(semnames):
                sems[k] = st.enter_context(nc.semaphore("s%d" % i))
            block = st.enter_context(nc.Block())
            prog = self

            def run(e, eng):
                waited = {}
                for o in prog.ops[e]:
                    need = {}
                    for d in o.deps:
                        k, v = d.tok
                        if need.get(k, 0) < v:
                            need[k] = v
                    for k, v in need.items():
                        if waited.get(k, 0) < v:
                            eng.wait_ge(sems[k], v)
                            waited[k] = v
                    ins = o.fn(eng)
                    if o.dma is not None:
                        ins.then_inc(sems[o.tok[0]], 16)
                    elif o.marked:
                        ins.then_inc(sems[o.tok[0]], 1)
                if e == "sp":
                    for o in final_waits:
                        k, v = o.tok
                        eng.wait_ge(sems[k], v)

            @block.tensor
            def _(eng):
                run("pe", eng)

            @block.scalar
            def _(eng):
                run("act", eng)

            @block.vector
            def _(eng):
                run("dve", eng)

            @block.gpsimd
            def _(eng):
                run("pool", eng)

            @block.sync
            def _(eng):
                run("sp", eng)


def _chunk_table():
    ch = [("inU", 8, 512), ("inQ", 8, 512), ("inKV", 8, 256), ("glu", 4, 512),
          ("out0", 8, 512), ("out1", 8, 512)]
    ch += [("ffi%d" % c, 8, 512) for c in range(11)]
    ch += [("ffo%d" % c, 22, 256) for c in range(4)]
    ch += [("gate0", 8, 512), ("gate1", 8, 512), ("proj", 2, 1024)]
    offs = {}
    o = 0
    for n, kt, nc_ in ch:
        offs[n] = (o, kt, nc_)
        o += kt * nc_
    return ch, offs, o


CHUNKS, CH_OFF, WCOLS = _chunk_table()
assert WCOLS == 98304

GV_L = 36
GV_COLS = NL * GV_L + 8
SSMP_L = 32 * (3 + 4 * 16)


def _pack_chunk(Wr, cols):
    kt = Wr.shape[0] // 128
    a = Wr[:, cols].reshape(kt, 128, len(cols)).transpose(1, 0, 2)
    return a.reshape(128, kt * len(cols))


def _host_pack(inp):
    f32 = np.float32
    wpack = np.empty((NL, 128, WCOLS), f32)
    qcols = np.concatenate([np.concatenate([512 + 64 * i + np.arange(64),
                                            512 + 64 * (4 + i) + np.arange(64)]) for i in range(4)])
    orow = np.concatenate([np.arange(512)] +
                          [np.concatenate([512 + 64 * i + np.arange(64),
                                           512 + 64 * (4 + i) + np.arange(64)]) for i in range(4)])
    for l in range(NL):
        parts = []
        w_in = np.asarray(inp["w_in"][l], f32)
        parts.append(_pack_chunk(w_in, np.arange(512)))
        parts.append(_pack_chunk(w_in, qcols))
        parts.append(_pack_chunk(w_in, np.arange(1024, 1280)))
        parts.append(_pack_chunk(np.asarray(inp["ssm_w_glu"][l], f32), np.arange(512)))
        w_out = np.asarray(inp["w_out"][l], f32)[orow]
        parts.append(_pack_chunk(w_out, np.arange(512)))
        parts.append(_pack_chunk(w_out, np.arange(512, 1024)))
        wfi = np.asarray(inp["w_ffn_in"][l], f32)
        for c in range(11):
            cols = np.concatenate([128 * (2 * c) + np.arange(128), FFN + 128 * (2 * c) + np.arange(128),
                                   128 * (2 * c + 1) + np.arange(128), FFN + 128 * (2 * c + 1) + np.arange(128)])
            parts.append(_pack_chunk(wfi, cols))
        wfo = np.asarray(inp["w_ffn_out"][l], f32)
        for c in range(4):
            parts.append(_pack_chunk(wfo, 256 * c + np.arange(256)))
        wg = np.asarray(inp["w_ple_gate"][l], f32)
        parts.append(_pack_chunk(wg, np.arange(512)))
        parts.append(_pack_chunk(wg, np.arange(512, 1024)))
        parts.append(_pack_chunk(np.asarray(inp["w_ple_proj"][l], f32), np.arange(1024)))
        wpack[l] = np.concatenate(parts, axis=1)

    def fm(v):
        return np.asarray(v, f32).reshape(-1, 128).T

    aperm = np.stack([np.concatenate([64 * i + np.arange(64), 64 * (4 + i) + np.arange(64)]) for i in range(4)])
    gv = np.empty((128, GV_COLS), f32)
    for l in range(NL):
        b = l * GV_L
        gv[:, b:b + 8] = fm(inp["norm_mix"][l])
        gv[:, b + 8:b + 16] = fm(inp["norm_ffn"][l])
        gv[:, b + 16:b + 24] = fm(inp["norm_ple"][l])
        gv[:, b + 24:b + 28] = fm(inp["norm_ssm_out"][l])
        gv[:, b + 28:b + 32] = np.asarray(inp["norm_attn_out"][l], f32)[aperm].T
        gv[:, b + 32:b + 36] = fm(inp["ssm_d"][l])
    gv[:, NL * GV_L:] = fm(inp["norm_final"])
    sk = np.asarray(inp["attn_sinks"], f32).reshape(1, NL * 8)
    ssmp = np.empty((128, NL, SSMP_L), f32)
    for l in range(NL):
        ar = np.asarray(inp["ssm_a_re"][l], f32).T
        ai = np.asarray(inp["ssm_a_im"][l], f32).T
        ld = np.broadcast_to(np.asarray(inp["ssm_log_dt"][l], f32)[None, :], (64, 32))
        bre = np.asarray(inp["ssm_b_re"][l], f32).transpose(1, 0, 2).reshape(64, 512)
        bim = np.asarray(inp["ssm_b_im"][l], f32).transpose(1, 0, 2).reshape(64, 512)
        cre = np.asarray(inp["ssm_c_re"][l], f32).transpose(2, 0, 1).reshape(64, 512)
        cim = np.asarray(inp["ssm_c_im"][l], f32).transpose(2, 0, 1).reshape(64, 512)
        row = np.concatenate([ar, ai, ld, bre, bim, cre, cim], axis=1)
        ssmp[0:64, l] = row
        ssmp[64:128, l] = row
    return wpack, gv, sk, ssmp


ATT_DELAY = 6
INCR = False
SQ_POOL = False
WARM = {}


def build(nl=NL, ntiles=NTILES, dbg=None, strict_same=True):
    nc = bass.Bass("TRN2", target_bir_lowering=False)
    seq = ntiles * T
    xT = nc.dram_tensor("xT", [D_MODEL, seq], F32, kind="ExternalInput").ap()
    pT = nc.dram_tensor("pT", [nl, 256, seq], F32, kind="ExternalInput").ap()
    wpack = nc.dram_tensor("wpack", [nl, 128, WCOLS], F32, kind="ExternalInput").ap()
    gvd = nc.dram_tensor("gv", [128, GV_COLS], F32, kind="ExternalInput").ap()
    skd = nc.dram_tensor("sk", [1, NL * 8], F32, kind="ExternalInput").ap()
    ssmpd = nc.dram_tensor("ssmp", [128, NL, SSMP_L], F32, kind="ExternalInput").ap()
    outT = nc.dram_tensor("outT", [D_MODEL, seq], F32, kind="ExternalOutput").ap()
    WS = nc.dram_tensor("WS", [nl, 128, WCOLS], BF16, kind="Internal").ap()
    SSMC = nc.dram_tensor("SSMC", [nl, 4, 128, RING_COLS], BF16, kind="Internal").ap()
    TAB = nc.dram_tensor("TAB", [nl, 128, 2 * NG * NCH], BF16, kind="Internal").ap()
    dbg = dbg or {}
    dbg_out = {}
    for name, shape in dbg.items():
        dbg_out[name] = nc.dram_tensor("dbg_" + name, list(shape), F32, kind="ExternalOutput").ap()

    st = contextlib.ExitStack()

    def sb(name, shape, dt=F32):
        return st.enter_context(nc.sbuf_tensor(name, list(shape), dt))

    with st:
        P = Prog(nc, strict_same=strict_same)
        finals = []
        ARENA_BYTES = 196 * 1024
        arena = sb("arena", [128, ARENA_BYTES // 4], F32)

        class Carver:
            def __init__(self):
                self.off = 0

            def __call__(self, shape, dt=F32, parts=128):
                esz = 4 if dt == F32 else 2
                n = 1
                for d in shape[1:]:
                    n *= d
                nbytes = (n * esz + 3) // 4 * 4
                a = arena[0:shape[0], self.off // 4:(self.off + nbytes) // 4]
                self.off += nbytes
                assert self.off <= ARENA_BYTES, self.off
                if dt != F32:
                    a = a.bitcast(dt)
                if len(shape) > 2:
                    names = " ".join("d%d" % i for i in range(1, len(shape)))
                    kw = {"d%d" % i: shape[i] for i in range(1, len(shape) - 1)}
                    a = a.rearrange("p (%s) -> p %s" % (names, names), **kw)
                return a

        gv = sb("gvs", [128, GV_COLS])
        gs4 = sb("gs4", [128, NL, 4])
        Rsb = sb("Rsb", [128, NL, NG])
        skeb = sb("skeb", [1, NL * 8], BF16)
        sks = sb("sks", [1, NL * 8])
        ske = sb("ske", [1, NL * 8])
        ones10 = sb("ones10", [128, 128], BF16)
        ones13 = sb("ones13", [128, 128], BF16)
        ones9 = sb("ones9", [128, 128], BF16)
        OA = sb("OA", [128, 128], BF16)
        OB = sb("OB", [128, 128], BF16)
        orow = sb("orow", [1, 2, 128], BF16)
        maskC = sb("maskC", [128, 128], BF16)
        maskP = sb("maskP", [128, 128], BF16)
        identb = sb("identb", [128, 128], BF16)
        mneg = sb("mneg", [128, 2, 4, 128], BF16)
        iot = sb("iot", [128, 128])
        kcar = sb("kcar", [128, NL, 128], BF16)
        vcar = sb("vcar", [128, NL, 256], BF16)
        Zcar = sb("Zcar", [128, NL, NG])
        sgnc = sb("sgnc", [128, 2])
        epsc = sb("epsc", [128, 1])
        perm = sb("perm", [128, 128])
        c64 = sb("c64", [128, NL, 2, NG])
        zrot = sb("zrot", [128, 2, NG])

        cm = Carver()
        hT = cm([128, 8, T])
        hn = cm([128, 8, T], BF16)
        mixed = hn
        sq = cm([128, 4, T], BF16)
        msb = cm([128, T])
        rstd = cm([128, T])
        uT = cm([128, 4, T], BF16)
        qT = cm([128, 4, T], BF16)
        kT = cm([128, 128 + T], BF16)
        Vx = cm([128, 5, 2, 128], BF16)
        hid = cm([128, 22, T], BF16)
        pTs = cm([128, 2, T], BF16)
        ring = cm([128, NRING, RING_COLS], BF16)
        tabs = cm([128, 2 * NG * NCH], BF16)
        ysb = cm([128, 2, T])
        t1 = cm([128, 2, T])
        gl2 = cm([128, 4, T])
        atf = cm([128, 4, T])
        gb = cm([128, 4, T], BF16)
        srot = cm([128, 2, 2, 4 * NCH])
        Zb = cm([128, 2, 16, NCH + 1])
        ZCb = cm([128, 2, 16, NCH], BF16)
        ZSb = cm([128, 2, 16, NCH], BF16)
        Ycs = cm([64, 2, 8, 8, 16], BF16)
        Pb = cm([128, 2, 2, 2, T], BF16)
        rden = cm([128, T])
        sgt = cm([128, 2, T], BF16)
        tht = cm([128, 2, T])
        ott = cm([128, 1, T])
        main_bytes = cm.off
        print('arena main bytes', main_bytes, 'of', ARENA_BYTES)

        cp = Carver()
        ssmp = cp([128, SSMP_L])
        I2 = cp([128, 64])
        bmask = cp([128, 128])
        cidx = cp([128, NCH])
        K07 = cp([128, 8])
        Kv = cp([128, 4, 8])
        offc = cp([128, 8])
        NLG = nl * NG
        ssm3 = cp([128, 3, NL, 32])
        sv = cp([128, 24, NLG])
        trg = cp([128, 8, NLG, 8])
        ttmp = cp([128, 4, NLG, 8])
        LBu = cp([128, NG, 16])
        LBp = cp([128, 2, NG, 32])
        CL = cp([128, 8, NG, 16])
        cltmp = cp([128, 2, 4, NG, 16])
        DS = cp([128, 2, 8, 64])
        CST = cp([128, 2, RING_COLS], BF16)
        vtmp = cp([128, 2, 8, 8, 16])
        tg = cp([128, 4, NG, NCH])
        tgo = cp([128, 1, 2, NG * NCH], BF16)
        thr = cp([128, NLG])
        pz1 = cp([128, 128])
        pz2 = cp([128, 128])

        banks = [st.enter_context(nc.psum_tensor("pb%d" % i, [128, 512], F32)) for i in range(8)]

        class RR:
            def __init__(self, ids):
                self.ids = ids
                self.i = 0

            def next(self):
                b = self.ids[self.i % len(self.ids)]
                self.i += 1
                return b

        mmp = RR([0, 1, 2, 3, 4, 5, 6])
        NB = 7
        auxp = mmp
        pA = RR([0, 1, 2, 3])
        pB = RR([4, 5, 6, 7])

        def bk(b):
            return ("ps", b)

        def dve(fn, reads, writes):
            return P.op("dve", fn, reads, writes)

        def act(fn, reads, writes):
            return P.op("act", fn, reads, writes)

        def pe(fn, reads, writes):
            return P.op("pe", fn, reads, writes)

        def pool(fn, reads, writes):
            return P.op("pool", fn, reads, writes)

        tapn = [0]

        def tap(name, src_ap, reads):
            if name in dbg_out:
                tapn[0] += 1
                o = P.op("pool", lambda e: e.dma_start(out=dbg_out[name], in_=src_ap), reads=reads, dma="dbg%d" % tapn[0])
                finals.append(o)

        P.op("sp", lambda e: e.dma_start(out=gv[:], in_=gvd), writes=["gv"], dma="c0")
        P.op("sp", lambda e: e.dma_start(out=sks[:], in_=skd), writes=["sks"], dma="c1")
        for t_, key, v in ((ones10, "ones10", 2.0 ** -10), (ones13, "ones13", 2.0 ** -13), (ones9, "ones9", 2.0 ** -9)):
            pool(lambda e, t_=t_, v=v: e.memset(t_[:], v), [], [key])
        pool(lambda e: e.memset(OA[:], 0.0), [], ["OA"])
        pool(lambda e: e.memset(OB[:], 0.0), [], ["OB"])
        pool(lambda e: e.memset(OA[:, 0:64], 1.0), [], ["OA"])
        pool(lambda e: e.memset(OB[:, 64:128], 1.0), [], ["OB"])
        pool(lambda e: e.memset(orow[:], 0.0), [], ["orow"])
        pool(lambda e: e.memset(orow[:, 0, 0:64], 1.0), [], ["orow"])
        pool(lambda e: e.memset(orow[:, 1, 64:128], 1.0), [], ["orow"])
        pool(lambda e: e.memset(epsc[:], EPS), [], ["epsc"])
        pool(lambda e: e.memset(vcar[:], 0.0), [], ["vcar"])
        pool(lambda e: e.memset(kcar[:], 0.0), [], ["kcar"])
        pool(lambda e: e.memset(Zcar[:], 0.0), [], ["Zcar"])
        pool(lambda e: e.memset(LBp.rearrange("p a g c -> p (a g c)"), 0.0), [], ["LBp"])
        pool(lambda e: e.iota(iot[:], pattern=[[1, 128]], base=0, channel_multiplier=-1,
                              allow_small_or_imprecise_dtypes=True), [], ["iot"])
        dve(lambda e: e.tensor_scalar(out=maskC[:], in0=iot[:], scalar1=0.0, scalar2=None, op0=ALU.is_ge), ["iot"], ["maskC"])
        dve(lambda e: e.tensor_scalar(out=maskP[:], in0=iot[:], scalar1=0.0, scalar2=None, op0=ALU.is_lt), ["iot"], ["maskP"])
        dve(lambda e: e.tensor_scalar(out=identb[:], in0=iot[:], scalar1=0.0, scalar2=None, op0=ALU.is_equal),
            ["iot"], ["identb"])
        for i_ in range(4):
            dve(lambda e, i_=i_: e.tensor_scalar(out=mneg[:, 0, i_, :], in0=iot[:], scalar1=0.0, scalar2=-30000.0, op0=ALU.is_lt, op1=ALU.mult),
                ["iot"], ["mneg"])
            dve(lambda e, i_=i_: e.tensor_scalar(out=mneg[:, 1, i_, :], in0=iot[:], scalar1=0.0, scalar2=-30000.0, op0=ALU.is_ge, op1=ALU.mult),
                ["iot"], ["mneg"])
        dve(lambda e: e.tensor_scalar(out=I2, in0=iot[:, 0:64], scalar1=0.0, scalar2=None, op0=ALU.is_equal), ["iot"], ["I2"])
        dve(lambda e: e.tensor_scalar(out=perm[:], in0=iot[:], scalar1=64.0, scalar2=None, op0=ALU.is_equal), ["iot"], ["perm"])
        dve(lambda e: e.tensor_scalar(out=pz2, in0=iot[:], scalar1=-64.0, scalar2=None, op0=ALU.is_equal), ["iot"], ["pz2"])
        dve(lambda e: e.tensor_tensor(out=perm[:], in0=perm[:], in1=pz2, op=ALU.add), ["perm", "pz2"], ["perm"])
        dve(lambda e: e.tensor_scalar(out=pz1[:, 0:64], in0=iot[:, 0:64], scalar1=-64.0, scalar2=None, op0=ALU.is_equal),
            ["iot"], ["pz1"])
        dve(lambda e: e.tensor_tensor(out=I2, in0=I2, in1=pz1[:, 0:64], op=ALU.add), ["I2", "pz1"], ["I2"])
        pool(lambda e: e.iota(bmask.rearrange("p (g h) -> p g h", h=16), pattern=[[-16, 8], [0, 16]], base=0,
                              channel_multiplier=1, allow_small_or_imprecise_dtypes=True), [], ["bmask"])
        dve(lambda e: e.tensor_scalar(out=pz1, in0=bmask, scalar1=0.0, scalar2=None, op0=ALU.is_ge), ["bmask", "I2"], ["pz1"])
        dve(lambda e: e.tensor_scalar(out=pz2, in0=bmask, scalar1=15.0, scalar2=None, op0=ALU.is_le), ["bmask"], ["pz2"])
        dve(lambda e: e.tensor_tensor(out=bmask, in0=pz1, in1=pz2, op=ALU.mult), ["pz1", "pz2"], ["bmask"])
        pool(lambda e: e.iota(cidx, pattern=[[1, NCH]], base=0, channel_multiplier=0,
                              allow_small_or_imprecise_dtypes=True), [], ["cidx"])
        pool(lambda e: e.iota(K07, pattern=[[1, 8]], base=0, channel_multiplier=0,
                              allow_small_or_imprecise_dtypes=True), [], ["K07"])
        dve(lambda e: e.tensor_copy(out=Kv[:, 0, :], in_=K07), ["K07"], ["Kv"])
        dve(lambda e: e.tensor_scalar(out=Kv[:, 1, :], in0=K07, scalar1=-1.0, scalar2=7.0, op0=ALU.mult, op1=ALU.add), ["K07"], ["Kv"])
        dve(lambda e: e.tensor_scalar(out=Kv[:, 2, :], in0=K07, scalar1=-7.0, scalar2=None, op0=ALU.add), ["K07"], ["Kv"])
        dve(lambda e: e.tensor_scalar(out=Kv[:, 3, :], in0=K07, scalar1=1.0, scalar2=None, op0=ALU.add), ["K07"], ["Kv"])
        HP = math.pi / 2
        offs = [(HP, 0.0), (math.pi, HP), (HP, math.pi), (math.pi, 3 * HP), (math.pi, HP), (3 * HP, math.pi)]
        for i, (a, b) in enumerate(offs):
            pool(lambda e, i=i, a=a: e.memset(offc[0:64, i:i + 1], a), [], ["offc"])
            pool(lambda e, i=i, b=b: e.memset(offc[64:128, i:i + 1], b), [], ["offc"])
        pool(lambda e: e.memset(sgnc[:, 0:1], 1.0), [], ["sgnc"])
        pool(lambda e: e.memset(sgnc[64:128, 0:1], -1.0), [], ["sgnc"])
        pool(lambda e: e.memset(sgnc[:, 1:2], HP), [], ["sgnc"])
        act(lambda e: e.activation(out=ske[:], in_=sks[:], func=AF.Exp), ["sks"], ["ske"])
        dve(lambda e: e.tensor_copy(out=skeb[:], in_=ske[:]), ["ske"], ["skeb"])
        for l in range(nl):
            dve(lambda e, l=l: e.tensor_scalar(out=gs4[:, l, :], in0=gv[:, l * GV_L + 24:l * GV_L + 28], scalar1=0.25,
                                               scalar2=None, op0=ALU.mult), ["gv"], ["gs4"])

        CAST_PIECE = 16384

        def cast_layer(l):
            for i_, c0 in enumerate(range(0, WCOLS, CAST_PIECE)):
                pace = []
                if l > 0:
                    pace = [("SSMC", l - 1, i_)] if i_ < 4 else [("TAB", l - 1)]
                P.op("pool", lambda e, l=l, c0=c0: e.dma_start(out=WS[l, :, c0:c0 + CAST_PIECE],
                                                                 in_=wpack[l, :, c0:c0 + CAST_PIECE]),
                     reads=pace, writes=[("WS", l, c0 // CAST_PIECE)], dma="cast%d" % ((c0 // CAST_PIECE) % 3))
        cast_layer(0)

        def rsin(out_ap, in_ap, off, tmpa, tmpb, rkeys, wkeys, tkeys):
            if off is None:
                src = in_ap
            else:
                dve(lambda e: e.tensor_scalar(out=tmpb, in0=in_ap, scalar1=off, scalar2=None, op0=ALU.add), rkeys + tkeys, tkeys)
                src = tmpb
            dve(lambda e: e.tensor_scalar(out=tmpa, in0=src, scalar1=1.0 / TWO_PI, scalar2=MAGIC, op0=ALU.mult, op1=ALU.add),
                rkeys + tkeys, tkeys)
            dve(lambda e: e.tensor_scalar(out=tmpa, in0=tmpa, scalar1=-MAGIC, scalar2=-TWO_PI, op0=ALU.add, op1=ALU.mult),
                tkeys, tkeys)
            dve(lambda e: e.tensor_tensor(out=tmpa, in0=tmpa, in1=src, op=ALU.add), rkeys + tkeys, tkeys)
            act(lambda e: e.activation(out=out_ap, in_=tmpa, func=AF.Sin), tkeys + rkeys, wkeys)

        def svv(i):
            return sv[:, i, :]

        def prepass_small():
            for q_ in range(3):
                P.op("sp", lambda e, q_=q_: e.dma_start(out=ssm3[:, q_, 0:nl, :], in_=ssmpd[:, 0:nl, q_ * 32:(q_ + 1) * 32]),
                     writes=["ssm3"], dma="c3")
            ar = ssm3[:, 0, 0:nl, :].rearrange("p l g -> p (l g)")
            ai = ssm3[:, 1, 0:nl, :].rearrange("p l g -> p (l g)")
            ldt = ssm3[:, 2, 0:nl, :].rearrange("p l g -> p (l g)")
            S = ["sv"]
            SP_ = ["ssm3", "sv"]
            act(lambda e: e.activation(out=svv(0), in_=ldt, func=AF.Exp), SP_, S)
            dve(lambda e: e.tensor_tensor(out=svv(1), in0=ar, in1=svv(0), op=ALU.mult), SP_, S)
            dve(lambda e: e.tensor_tensor(out=svv(2), in0=ai, in1=svv(0), op=ALU.mult), SP_, S)
            act(lambda e: e.activation(out=svv(3), in_=svv(1), func=AF.Exp), S, S)
            act(lambda e: e.activation(out=Rsb[:, 0:nl, :].rearrange("p l g -> p (l g)"), in_=svv(1), func=AF.Exp, scale=8.0), S, ["Rsb"])
            rsin(svv(4), svv(2), HP, svv(12), svv(13), S, S, S)
            rsin(svv(5), svv(2), None, svv(12), svv(13), S, S, S)
            dve(lambda e: e.tensor_tensor(out=svv(6), in0=svv(3), in1=svv(4), op=ALU.mult), S, S)
            dve(lambda e: e.tensor_tensor(out=svv(7), in0=svv(3), in1=svv(5), op=ALU.mult), S, S)
            dve(lambda e: e.tensor_tensor(out=svv(8), in0=ar, in1=ar, op=ALU.mult), SP_, S)
            dve(lambda e: e.tensor_tensor(out=svv(12), in0=ai, in1=ai, op=ALU.mult), SP_, S)
            dve(lambda e: e.tensor_tensor(out=svv(8), in0=svv(8), in1=svv(12), op=ALU.add), S, S)
            dve(lambda e: e.reciprocal(out=svv(8), in_=svv(8)), S, S)
            dve(lambda e: e.tensor_scalar(out=svv(9), in0=svv(6), scalar1=-1.0, scalar2=None, op0=ALU.add), S, S)
            dve(lambda e: e.tensor_tensor(out=svv(12), in0=svv(9), in1=ar, op=ALU.mult), SP_, S)
            dve(lambda e: e.tensor_tensor(out=svv(13), in0=svv(7), in1=ai, op=ALU.mult), SP_, S)
            dve(lambda e: e.tensor_tensor(out=svv(12), in0=svv(12), in1=svv(13), op=ALU.add), S, S)
            dve(lambda e: e.tensor_tensor(out=svv(10), in0=svv(12), in1=svv(8), op=ALU.mult), S, S)
            dve(lambda e: e.tensor_tensor(out=svv(12), in0=svv(7), in1=ar, op=ALU.mult), SP_, S)
            dve(lambda e: e.tensor_tensor(out=svv(13), in0=svv(9), in1=ai, op=ALU.mult), SP_, S)
            dve(lambda e: e.tensor_tensor(out=svv(12), in0=svv(12), in1=svv(13), op=ALU.subtract), S, S)
            dve(lambda e: e.tensor_tensor(out=svv(11), in0=svv(12), in1=svv(8), op=ALU.mult), S, S)
            dve(lambda e: e.tensor_copy(out=sv[0:64, 14, :], in_=sv[0:64, 10, :]), S, S)
            dve(lambda e: e.tensor_scalar(out=sv[64:128, 14, :], in0=sv[64:128, 11, :], scalar1=-1.0, scalar2=None, op0=ALU.mult), S, S)
            dve(lambda e: e.tensor_copy(out=sv[0:64, 15, :], in_=sv[0:64, 11, :]), S, S)
            dve(lambda e: e.tensor_copy(out=sv[64:128, 15, :], in_=sv[64:128, 10, :]), S, S)
            dve(lambda e: e.tensor_scalar(out=sv[0:64, 16, :], in0=sv[0:64, 10, :], scalar1=-1.0, scalar2=None, op0=ALU.mult), S, S)
            dve(lambda e: e.tensor_copy(out=sv[64:128, 16, :], in_=sv[64:128, 11, :]), S, S)

            spec = [(1, 1, 0), (0, 0, 0), (0, 0, 1), (3, 2, 2), (3, 2, 3), (3, 2, 4), (3, 2, 5)]
            TT_ = ["ttmp"]

            def kb(i):
                return Kv[:, i, :].unsqueeze(1).to_broadcast([128, NLG, 8])

            def gb8(i):
                return sv[:, i, :].unsqueeze(2).to_broadcast([128, NLG, 8])
            for ti, (ke, ka, oi) in enumerate(spec):
                dve(lambda e, ke=ke: e.tensor_tensor(out=ttmp[:, 0], in0=gb8(1), in1=kb(ke), op=ALU.mult), S + ["Kv"] + TT_, TT_)
                act(lambda e: e.activation(out=ttmp[:, 0], in_=ttmp[:, 0], func=AF.Exp), TT_, TT_)
                dve(lambda e, ka=ka: e.tensor_tensor(out=ttmp[:, 1], in0=gb8(2), in1=kb(ka), op=ALU.mult), S + ["Kv"] + TT_, TT_)
                rsin(ttmp[:, 1], ttmp[:, 1], offc[:, oi:oi + 1], ttmp[:, 2], ttmp[:, 3], ["offc"], TT_, TT_)
                dve(lambda e, ti=ti: e.tensor_tensor(out=trg[:, ti], in0=ttmp[:, 0], in1=ttmp[:, 1], op=ALU.mult),
                    TT_, [("trg", ti)])

            dve(lambda e: e.tensor_scalar(out=svv(17), in0=svv(2), scalar1=8.0, scalar2=None, op0=ALU.mult), S, S)
            dve(lambda e: e.tensor_scalar(out=svv(12), in0=svv(17), scalar1=1.0 / TWO_PI, scalar2=MAGIC, op0=ALU.mult, op1=ALU.add), S, S)
            dve(lambda e: e.tensor_scalar(out=svv(12), in0=svv(12), scalar1=-MAGIC, scalar2=-TWO_PI, op0=ALU.add, op1=ALU.mult), S, S)
            dve(lambda e: e.tensor_tensor(out=thr, in0=svv(12), in1=svv(17), op=ALU.add), S + ["thr"], ["thr"])
            dve(lambda e: e.tensor_scalar(out=svv(18), in0=thr, scalar1=float(NCH), scalar2=None, op0=ALU.mult), ["thr"] + S, S)
            dve(lambda e: e.tensor_scalar(out=svv(12), in0=svv(18), scalar1=1.0 / TWO_PI, scalar2=MAGIC, op0=ALU.mult, op1=ALU.add), S, S)
            dve(lambda e: e.tensor_scalar(out=svv(12), in0=svv(12), scalar1=-MAGIC, scalar2=-TWO_PI, op0=ALU.add, op1=ALU.mult), S, S)
            dve(lambda e: e.tensor_tensor(out=svv(12), in0=svv(12), in1=svv(18), op=ALU.add), S, S)
            act(lambda e: e.activation(out=svv(13), in_=svv(12), func=AF.Abs), S, S)
            act(lambda e: e.activation(out=c64[:, 0:nl, 0, :], in_=svv(13).rearrange("p (l g) -> p l g", l=nl), func=AF.Sin, scale=-1.0, bias=sgnc[:, 1:2]), S + ["sgnc"], ["c64"])
            act(lambda e: e.activation(out=svv(13), in_=svv(12), func=AF.Sin, scale=sgnc[:, 0:1]), S + ["sgnc"], S)
            dve(lambda e: e.tensor_scalar(out=c64[:, 0:nl, 1, :], in0=svv(13).rearrange("p (l g) -> p l g", l=nl), scalar1=-1.0, scalar2=None, op0=ALU.mult), S, ["c64"])

        def prepass(l):
            P.op("sp", lambda e: e.dma_start(out=ssmp, in_=ssmpd[:, l, :]), writes=["ssmp"], dma="c2")
            bre = ssmp[:, 96:608].rearrange("p (g h) -> p g h", h=16)
            bim = ssmp[:, 608:1120].rearrange("p (g h) -> p g h", h=16)
            cre = ssmp[:, 1120:1632].rearrange("p (g h) -> p g h", h=16)
            cim = ssmp[:, 1632:2144].rearrange("p (g h) -> p g h", h=16)
            S = ["sv"]
            SP_ = ["ssmp", "sv"]
            def bc_h(i):
                return sv[:, i, l * NG:(l + 1) * NG].unsqueeze(2).to_broadcast([128, NG, 16])
            ta = cltmp[:, 0, 0]
            tb = cltmp[:, 1, 0]
            C_ = ["cltmp"]
            dve(lambda e: e.tensor_tensor(out=ta, in0=bre, in1=bc_h(14), op=ALU.mult), SP_ + C_, C_)
            dve(lambda e: e.tensor_tensor(out=tb, in0=bim, in1=bc_h(15), op=ALU.mult), SP_ + C_, C_)
            dve(lambda e: e.tensor_tensor(out=LBu, in0=ta, in1=tb, op=ALU.subtract), C_, ["LBu"])
            LBuv = LBu.rearrange("p (a e) h -> p a e h", e=2)
            for e_ in range(2):
                dve(lambda e, e_=e_: e.tensor_copy(
                    out=LBp[:, 0].rearrange("p (a e) (f h) -> p a e f h", e=2, h=16)[:, :, e_, e_, :],
                    in_=LBuv[:, :, e_, :]), ["LBu"], ["LBp"])
            dve(lambda e: e.tensor_tensor(out=ta, in0=bre, in1=bc_h(15), op=ALU.mult), SP_ + C_, C_)
            dve(lambda e: e.tensor_tensor(out=tb, in0=bim, in1=bc_h(16), op=ALU.mult), SP_ + C_, C_)
            dve(lambda e: e.tensor_tensor(out=ta, in0=ta, in1=tb, op=ALU.subtract), C_, C_)
            tav = ta.rearrange("p (a e) h -> p a e h", e=2)
            for e_ in range(2):
                dve(lambda e, e_=e_: e.tensor_copy(
                    out=LBp[:, 1].rearrange("p (a e) (f h) -> p a e f h", e=2, h=16)[:, :, e_, e_, :],
                    in_=tav[:, :, e_, :]), C_, ["LBp"])
            def c_bc(c):
                return c.unsqueeze(1).to_broadcast([128, 4, NG, 16])

            def t_bc(ti, hf):
                return trg[:, ti, l * NG:(l + 1) * NG, 4 * hf:4 * hf + 4].rearrange("p g k -> p k g").unsqueeze(3).to_broadcast([128, 4, NG, 16])
            for hf in range(2):
                dve(lambda e, hf=hf: e.tensor_tensor(out=cltmp[:, 0], in0=c_bc(cre), in1=t_bc(1, hf), op=ALU.mult), ["ssmp", ("trg", 1), "LBp"] + C_, C_)
                dve(lambda e, hf=hf: e.tensor_tensor(out=cltmp[:, 1], in0=c_bc(cim), in1=t_bc(2, hf), op=ALU.mult), ["ssmp", ("trg", 2)] + C_, C_)
                dve(lambda e, hf=hf: e.tensor_tensor(out=CL[:, 4 * hf:4 * hf + 4], in0=cltmp[:, 0], in1=cltmp[:, 1], op=ALU.add), C_, ["CL"])

            for pt in range(4):
                cs = pt % 2
                CK = ("CST", cs)
                cst = CST[:, cs]
                for half in range(2):
                    b = mmp.next()
                    for tt in range(4):
                        tau = half * 4 + tt
                        pe(lambda e, b=b, tt=tt, tau=tau, pt=pt: e.matmul(
                            banks[b][:, tt * 128:(tt + 1) * 128],
                            lhsT=LBu[:, 8 * pt:8 * pt + 8, :].rearrange("p g h -> p (g h)"),
                            rhs=CL[:, tau, 8 * pt:8 * pt + 8, :].rearrange("p g h -> p (g h)"),
                            start=True, stop=True), ["LBu", "CL"], [bk(b)])
                    dve(lambda e, b=b, half=half, cst=cst: e.tensor_tensor(
                        out=cst[:, half * 512:(half + 1) * 512].rearrange("p (t n) -> p t n", n=128),
                        in0=banks[b][:].rearrange("p (t n) -> p t n", n=128),
                        in1=bmask.unsqueeze(1).to_broadcast([128, 4, 128]), op=ALU.mult), [bk(b), "bmask"], [CK])
                for e_ in range(2):
                    bx = mmp.next()
                    by = mmp.next()
                    for q in range(4):
                        g = 8 * pt + 2 * q + e_
                        ds = q % 2
                        dve(lambda e, ds=ds, g=g: e.tensor_tensor(
                            out=DS[:, ds], in0=I2.unsqueeze(1).to_broadcast([128, 8, 64]),
                            in1=trg[:, 0, l * NG + g, :].unsqueeze(2).to_broadcast([128, 8, 64]), op=ALU.mult),
                            ["I2", ("trg", 0)], [("DS", ds)])
                        for v_, b in ((0, bx), (1, by)):
                            pe(lambda e, b=b, v_=v_, g=g, q=q, ds=ds: e.matmul(
                                banks[b][32 * q:32 * q + 32, :], lhsT=LBp[:, v_, g, :],
                                rhs=DS[:, ds].rearrange("p s c -> p (s c)"), start=True, stop=True,
                                tile_position=(0, 32 * q)), ["LBp", ("DS", ds)], [bk(b)])
                    wv = cst[:, 1024:4096].rearrange("p (e s c) -> p e s c", e=2, c=192)
                    act(lambda e, bx=bx, wv=wv, e_=e_: e.copy(out=wv[:, e_, :, 0:64],
                                                               in_=banks[bx][:].rearrange("p (s c) -> p s c", c=64)), [bk(bx)], [CK])
                    dve(lambda e, bx=bx, wv=wv, e_=e_: e.tensor_copy(out=wv[:, e_, :, 128:192],
                                                                     in_=banks[bx][:].rearrange("p (s c) -> p s c", c=64)), [bk(bx)], [CK])
                    act(lambda e, by=by, wv=wv, e_=e_: e.copy(out=wv[:, e_, :, 64:128],
                                                               in_=banks[by][:].rearrange("p (s c) -> p s c", c=64)), [bk(by)], [CK])
                vv = cst[:, 4096:6144].rearrange("p (g v j h) -> p g v j h", v=2, j=8, h=16)
                gsl = slice(8 * pt, 8 * pt + 8)

                def c8(c, gsl=gsl):
                    return c[:, gsl, :].unsqueeze(2).to_broadcast([128, 8, 8, 16])

                def t8(ti, gsl=gsl):
                    return trg[:, ti, l * NG + gsl.start:l * NG + gsl.stop, :].unsqueeze(3).to_broadcast([128, 8, 8, 16])
                for v_, (t_a, t_b) in enumerate(((3, 4), (5, 6))):
                    dve(lambda e, t_a=t_a, c8=c8, t8=t8: e.tensor_tensor(out=vtmp[:, 0], in0=c8(cre), in1=t8(t_a), op=ALU.mult),
                        ["ssmp", ("trg", t_a), "vtmp"], ["vtmp"])
                    dve(lambda e, t_b=t_b, c8=c8, t8=t8: e.tensor_tensor(out=vtmp[:, 1], in0=c8(cim), in1=t8(t_b), op=ALU.mult),
                        ["ssmp", ("trg", t_b), "vtmp"], ["vtmp"])
                    dve(lambda e, v_=v_, vv=vv: e.tensor_tensor(out=vv[:, :, v_], in0=vtmp[:, 0], in1=vtmp[:, 1], op=ALU.add),
                        ["vtmp"], [CK])
                P.op("sp", lambda e, cst=cst, pt=pt: e.dma_start(out=SSMC[l, pt], in_=cst), reads=[CK],
                     writes=[("SSMC", l, pt)], dma="cst%d" % cs)
            TG = ["tg"]
            dve(lambda e: e.tensor_tensor(out=tg[:, 0], in0=thr[:, l * NG:(l + 1) * NG].unsqueeze(2).to_broadcast([128, NG, NCH]),
                                          in1=cidx.unsqueeze(1).to_broadcast([128, NG, NCH]), op=ALU.mult),
                ["thr", "cidx"] + TG, TG)
            dve(lambda e: e.tensor_scalar(out=tg[:, 1], in0=tg[:, 0], scalar1=1.0 / TWO_PI, scalar2=MAGIC, op0=ALU.mult, op1=ALU.add), TG, TG)
            dve(lambda e: e.tensor_scalar(out=tg[:, 1], in0=tg[:, 1], scalar1=-MAGIC, scalar2=-TWO_PI, op0=ALU.add, op1=ALU.mult), TG, TG)
            dve(lambda e: e.tensor_tensor(out=tg[:, 1], in0=tg[:, 1], in1=tg[:, 0], op=ALU.add), TG, TG)
            act(lambda e: e.activation(out=tg[:, 2], in_=tg[:, 1], func=AF.Abs), TG, TG)
            OK_ = ("tgo", 0)
            act(lambda e: e.activation(out=tgo[:, 0, 0], in_=tg[:, 2].rearrange("p g c -> p (g c)"), func=AF.Sin,
                                       scale=-1.0, bias=sgnc[:, 1:2]), TG + ["sgnc"], [OK_])
            act(lambda e: e.activation(out=tgo[:, 0, 1], in_=tg[:, 1].rearrange("p g c -> p (g c)"), func=AF.Sin,
                                       scale=sgnc[:, 0:1]), TG + ["sgnc"], [OK_])
            P.op("sp", lambda e: e.dma_start(out=TAB[l], in_=tgo[:, 0].rearrange("p a n -> p (a n)")),
                 reads=[OK_], writes=[("TAB", l)], dma="tgo0")

        prepass_small()
        for l in range(nl):
            prepass(l)
            if l + 1 < nl:
                cast_layer(l + 1)
        if "KD" in dbg_out:
            pass
        P.barrier()
        pool(lambda e: e.memset(Vx.rearrange("p a b c -> p (a b c)"), 0.0), [], [("Vx", i) for i in range(5)])

        order = ["inU", "inQ", "inKV", "S0", "S1", "S2", "S3", "glu", "out0", "out1"] + \
                ["ffi%d" % c for c in range(11)] + ["ffo%d" % c for c in range(4)] + ["gate0", "proj", "gate1"]
        items = []
        for ti_ in range(ntiles):
            for l in range(nl):
                for nm in order:
                    if nm[0] == "S":
                        pt = int(nm[1])
                        items.append(((ti_, l, nm), SSMC[l, pt], RING_COLS, [("SSMC", l, pt)]))
                    else:
                        o, kt, ncol = CH_OFF[nm]
                        rd = [("WS", l, c) for c in range(o // CAST_PIECE, (o + kt * ncol - 1) // CAST_PIECE + 1)]
                        items.append(((ti_, l, nm), WS[l, :, o:o + kt * ncol], kt * ncol, rd))
        issued = [0]
        free_slots = list(range(NRING))
        slot_of = {}

        def spump():
            while free_slots and issued[0] < len(items):
                k2, src, cols, rd = items[issued[0]]
                sl = free_slots.pop(0)
                slot_of[k2] = sl
                P.op("sp", lambda e, sl=sl, src=src, cols=cols: e.dma_start(out=ring[:, sl, 0:cols], in_=src),
                     reads=rd, writes=[("ring", sl)], dma="ring%d" % sl)
                issued[0] += 1

        def sget(key):
            spump()
            assert key in slot_of, ("ring chunk not resident", key)
            return slot_of[key]

        def srel(key):
            free_slots.append(slot_of.pop(key))
            spump()

        def wtile(slot, ncols, kt, m):
            v = ring[:, slot, :].rearrange("p (k n) -> p k n", n=ncols)
            return v[:, kt, m * 128:(m + 1) * 128]

        def warm(kind):
            for _ in range(WARM.get(kind, 0)):
                pe(lambda e: e.matmul(banks[NB][:], lhsT=identb[:], rhs=mneg[:, 0].rearrange("p i q -> p (i q)"), start=True, stop=True),
                   ["identb", "mneg"], [("warm",)])

        sqn = [0]

        def stat_sq(m):
            if not INCR:
                return 0
            sl_ = sqn[0] % 4
            sqn[0] += 1
            act(lambda e, m=m, sl_=sl_: e.activation(out=sq[:, sl_, :], in_=hT[:, m, :], func=AF.Square), [("hT", m)], [("sq", sl_)])
            return sl_

        def stat_mm(sl_, first_, last_):
            if not INCR:
                return
            pe(lambda e, sl_=sl_: e.matmul(banks[NB][:], lhsT=ones10[:], rhs=sq[:, sl_, :], start=first_, stop=last_),
               [("sq", sl_), "ones10"], [bk(NB)])

        def rmsnorm(src_fn, nk, gcol, ones_t, ones_key, dst_fn, srckeys, dstkeys, after=None, bank=None, pre=False):
            pre = pre and INCR
            b = (NB if pre else mmp.next()) if bank is None else bank
            for k in range(0 if pre else nk):
                if k % 2 == 0 or not SQ_POOL:
                    act(lambda e, k=k: e.activation(out=sq[:, k % 4, :], in_=src_fn(k), func=AF.Square), [srckeys[k]], [("sq", k % 4)])
                else:
                    pool(lambda e, k=k: e.tensor_tensor(out=sq[:, k % 4, :], in0=src_fn(k), in1=src_fn(k), op=ALU.mult), [srckeys[k]], [("sq", k % 4)])
                pe(lambda e, k=k, b=b: e.matmul(banks[b][:], lhsT=ones_t[:], rhs=sq[:, k % 4, :], start=(k == 0), stop=(k == nk - 1)),
                   [("sq", k % 4), ones_key], [bk(b)])
            act(lambda e, b=b: e.activation(out=msb, in_=banks[b][:], func=AF.Ln, bias=epsc[:, 0:1]), [bk(b), "epsc"], ["msb"])
            act(lambda e: e.activation(out=rstd, in_=msb, func=AF.Exp, scale=-0.5), ["msb"], ["rstd"])
            for k in range(nk):
                dve(lambda e, k=k: e.scalar_tensor_tensor(out=dst_fn(k), in0=src_fn(k), scalar=gcol(k), in1=rstd,
                                                          op0=ALU.mult, op1=ALU.mult),
                    [srckeys[k], "rstd", "gv", "gs4"], [dstkeys[k]])
                if after is not None:
                    after(k)

        HK = [("hT", k) for k in range(8)]
        HNK = [("hn", k) for k in range(8)]

        def layer(ti_, l):
            gvb = l * GV_L
            first = (ti_ == 0)
            tsl = slice(ti_ * T, (ti_ + 1) * T)
            P.op("sp", lambda e: e.dma_start(out=tabs, in_=TAB[l]), reads=[("TAB", l)], writes=["tabs"], dma="tabs")
            P.op("pool", lambda e: e.dma_start(out=pTs, in_=pT[l, :, tsl].rearrange("(k p) t -> p k t", p=128)),
                 writes=["pTs"], dma="pTs")
            dve(lambda e: e.tensor_copy(out=kT[:, 0:128], in_=kcar[:, l, :]), ["kcar"], [("kT", 0)])
            dve(lambda e: e.tensor_copy(out=Vx[:, 0].rearrange("p a n -> p (a n)"), in_=vcar[:, l, :]), ["vcar"], [("Vx", 0)])

            rmsnorm(lambda k: hT[:, k, :], 8, lambda k: gv[:, gvb + k:gvb + k + 1], ones10, "ones10", lambda k: hn[:, k, :], HK, HNK, pre=(l > 0))

            def proj_tiles(name, ntile, dst_fn, dkeys, bp=None, kouter=False):
                s = sget((ti_, l, name))
                _, kt_total, ncols = CH_OFF[name]
                bl = [(bp or mmp).next() for m in range(ntile)] if kouter else None
                if kouter:
                    for k in range(8):
                        for m in range(ntile):
                            pe(lambda e, b=bl[m], k=k, m=m, s=s: e.matmul(banks[b][:], lhsT=wtile(s, ncols, k, m), rhs=hn[:, k, :],
                                                                          start=(k == 0), stop=(k == 7)),
                               [("ring", s), ("hn", k)], [bk(bl[m])])
                for m in range(ntile):
                    b = bl[m] if kouter else (bp or mmp).next()
                    for k in range(0 if kouter else 8):
                        pe(lambda e, b=b, k=k, m=m, s=s: e.matmul(banks[b][:], lhsT=wtile(s, ncols, k, m), rhs=hn[:, k, :],
                                                                  start=(k == 0), stop=(k == 7)),
                           [("ring", s), ("hn", k)], [bk(b)])
                    if m % 2 == 0:
                        act(lambda e, b=b, m=m: e.copy(out=dst_fn(m), in_=banks[b][:]), [bk(b)], [dkeys[m]])
                    else:
                        dve(lambda e, b=b, m=m: e.tensor_copy(out=dst_fn(m), in_=banks[b][:]), [bk(b)], [dkeys[m]])
                return s
            warm("win")
            proj_tiles("inU", 4, lambda m: uT[:, m, :], [("uT", m) for m in range(4)], kouter=True)
            srel((ti_, l, "inU"))

            proj_tiles("inQ", 4, lambda m: qT[:, m, :], [("qT", m) for m in range(4)])
            srel((ti_, l, "inQ"))
            s = proj_tiles("inKV", 1, lambda m: kT[:, 128:128 + T], [("kT", 1)])
            for blk in range(4):
                b = mmp.next()
                for k in range(8):
                    pe(lambda e, b=b, k=k, blk=blk, s=s: e.matmul(
                        banks[b][:, 0:128], lhsT=hn[:, k, blk * 128:(blk + 1) * 128],
                        rhs=ring[:, s, 0:2048].rearrange("p (k n) -> p k n", n=256)[:, k, 128:256],
                        start=(k == 0), stop=(k == 7)), [("ring", s), ("hn", k)], [bk(b)])
                act(lambda e, b=b, blk=blk: e.copy(out=Vx[:, blk + 1, 0, 0:64], in_=banks[b][:, 0:64]), [bk(b)], [("Vx", blk + 1)])
                dve(lambda e, b=b, blk=blk: e.tensor_copy(out=Vx[:, blk + 1, 1, 64:128], in_=banks[b][:, 64:128]), [bk(b)], [("Vx", blk + 1)])
            srel((ti_, l, "inKV"))
            dve(lambda e: e.tensor_copy(out=kcar[:, l, :], in_=kT[:, T:T + 128]), [("kT", 1)], ["kcar"])
            dve(lambda e: e.tensor_copy(out=vcar[:, l, :], in_=Vx[:, 4].rearrange("p a n -> p (a n)")), [("Vx", 4)], ["vcar"])
            if l == 0 and ti_ == 0:
                tap("uT", uT[:, 0, :], [("uT", 0)])

            freeb = list(range(7))

            def balloc():
                while not freeb:
                    yield
                return freeb.pop(0)

            def bfree(b):
                freeb.append(b)

            cosv = tabs[:, 0:NG * NCH].rearrange("p (g c) -> p g c", c=NCH)
            sinv = tabs[:, NG * NCH:2 * NG * NCH].rearrange("p (g c) -> p g c", c=NCH)

            def gsel(tv, pp, q):
                return tv[:, 16 * pp:16 * pp + 16, :].rearrange("p (a r) c -> p a r c", a=2)[:, :, 2 * q:2 * q + 2, :]

            def zsel(tv, pp, q, lo, hi):
                return tv[:, pp, 4 * q:4 * q + 4, lo:hi].rearrange("p (a e) c -> p a e c", a=2)

            def attn_chain():
                for blk in range(4):
                    has_prev = not (first and blk == 0)
                    ps_ = blk % 2
                    sbk = {}
                    ncp = 2 if has_prev else 1
                    for kh in range(2):
                        for cp_ in range(ncp):
                            b = yield from balloc()
                            sbk[(kh, cp_)] = b
                            koff = 128 + blk * 128 - cp_ * 128
                            pe(lambda e, b=b, kh=kh, koff=koff, blk=blk: e.matmul(
                                banks[b][:].rearrange("p (i q) -> p i q", q=128),
                                lhsT=kT[64 * kh:64 * kh + 64, koff:koff + 128],
                                rhs=qT[64 * kh:64 * kh + 64, :, blk * 128:(blk + 1) * 128], start=True, stop=False,
                                tile_position=(64 * kh, 0)),
                               [("kT", 0), ("kT", 1)] + [("qT", m) for m in range(4)], [bk(b)])
                            pe(lambda e, b=b, cp_=cp_: e.matmul(banks[b][:], lhsT=identb[:], rhs=mneg[:, cp_].rearrange("p i q -> p (i q)"),
                                                                start=False, stop=True), ["identb", "mneg"], [bk(b)])
                    yield
                    for kh in range(2):
                        for cp_ in range(ncp):
                            b = sbk[(kh, cp_)]
                            pk = ("Pb", ps_, kh, cp_)
                            act(lambda e, b=b, kh=kh, cp_=cp_, ps_=ps_: e.activation(out=Pb[:, ps_, kh, cp_, :], in_=banks[b][:], func=AF.Exp, scale=0.125),
                                [bk(b)], [pk])
                            bfree(b)
                    yield
                    ob = yield from balloc()
                    db = yield from balloc()
                    warm("pv")
                    seq_ = [(kh, cp_) for kh in range(2) for cp_ in range(ncp)]
                    for n_, (kh, cp_) in enumerate(seq_):
                        pe(lambda e, kh=kh, cp_=cp_, n_=n_, ps_=ps_, blk=blk, ob=ob: e.matmul(
                            banks[ob][:], lhsT=Vx[:, blk + 1 - cp_, kh, :], rhs=Pb[:, ps_, kh, cp_, :],
                            start=(n_ == 0), stop=(n_ == len(seq_) - 1)),
                           [("Pb", ps_, kh, cp_), ("Vx", blk + 1 - cp_)], [bk(ob)])
                    for n_, (kh, cp_) in enumerate(seq_):
                        oo, okey = (OA, "OA") if kh == 0 else (OB, "OB")
                        pe(lambda e, kh=kh, cp_=cp_, n_=n_, oo=oo, ps_=ps_, db=db: e.matmul(
                            banks[db][:], lhsT=oo[:], rhs=Pb[:, ps_, kh, cp_, :], start=(n_ == 0), stop=False),
                           [("Pb", ps_, kh, cp_), okey], [bk(db)])
                    for kh in range(2):
                        pe(lambda e, kh=kh, db=db: e.matmul(banks[db][:].rearrange("p (i q) -> p i q", q=128), lhsT=orow[:, kh, :],
                                                            rhs=skeb[:, l * 8 + 4 * kh:l * 8 + 4 * kh + 4].unsqueeze(2).to_broadcast([1, 4, 128]),
                                                            start=False, stop=(kh == 1)),
                           ["orow", "skeb"], [bk(db)])
                    yield
                    dve(lambda e, db=db: e.reciprocal(out=rden, in_=banks[db][:]), [bk(db)], ["rden"])
                    bfree(db)
                    dve(lambda e, ob=ob, blk=blk: e.tensor_tensor(
                        out=atf[:, :, blk * 128:(blk + 1) * 128],
                        in0=banks[ob][:].rearrange("p (i q) -> p i q", q=128),
                        in1=rden.rearrange("p (i q) -> p i q", q=128), op=ALU.mult),
                        [bk(ob), "rden"], [("atf", i) for i in range(4)])
                    bfree(ob)
                    yield
                if l == 0 and ti_ == 0:
                    tap("at0", atf[:, 0, :], [("atf", 0)])
                nb_ = yield from balloc()
                warm("norm")
                rmsnorm(lambda k: atf[:, k, :], 4, lambda k: gv[:, gvb + 28 + k:gvb + 29 + k], ones9, "ones9", lambda k: mixed[:, 4 + k, :],
                        [("atf", k) for k in range(4)], [("hn", 4 + k) for k in range(4)], bank=nb_)
                bfree(nb_)

            def ssm_chain():
                slots = [sget((ti_, l, "S%d" % pt)) for pt in range(4)]
                if not first:
                    rb = yield from balloc()
                    pe(lambda e, rb=rb: e.matmul(banks[rb][:, 0:NG], lhsT=perm[:], rhs=Zcar[:, l, :], start=True, stop=True),
                       ["perm", "Zcar"], [bk(rb)])
                    dve(lambda e, rb=rb: e.tensor_tensor(out=zrot[:, 0, :], in0=banks[rb][:, 0:NG], in1=c64[:, l, 1, :], op=ALU.mult),
                        [bk(rb), "c64"], ["zrot0"])
                    bfree(rb)
                    dve(lambda e: e.tensor_tensor(out=zrot[:, 1, :], in0=Zcar[:, l, :], in1=c64[:, l, 0, :], op=ALU.mult),
                        ["Zcar", "c64"], ["zrot1"])
                    dve(lambda e: e.tensor_tensor(out=Zcar[:, l, :], in0=zrot[:, 0, :], in1=zrot[:, 1, :], op=ALU.add),
                        ["zrot0", "zrot1"], ["Zcar"])
                for pp in range(2):
                    dve(lambda e, pp=pp: e.tensor_copy(
                        out=Zb[:, pp, :, 0].rearrange("p (q t e) -> p q t e", q=4, t=2),
                        in_=Zcar[:, l, 16 * pp:16 * pp + 16].rearrange("p (t q e) -> p q t e", t=2, q=4)),
                        ["Zcar"], [("Zb", pp, q) for q in range(4)])
                for pp in range(2):
                    sb4 = []
                    for q in range(4):
                        b = yield from balloc()
                        sb4.append(b)
                    for ptl in range(2):
                        pt = 2 * pp + ptl
                        sl = slots[pt]
                        wv = ring[:, sl, 1024:4096].rearrange("p (e s c) -> p e s c", e=2, c=192)
                        for e_ in range(2):
                            for v_ in range(2):
                                for s_ in range(8):
                                    for q in range(4):
                                        b = sb4[q]
                                        col = ((ptl * 2 + e_) * 2 + v_) * NCH
                                        pe(lambda e, b=b, q=q, e_=e_, v_=v_, s_=s_, pt=pt, wv=wv, col=col: e.matmul(
                                            banks[b][:, col:col + NCH],
                                            lhsT=wv[32 * q:32 * q + 32, e_, s_, 64 * v_:64 * v_ + 128],
                                            rhs=uT[32 * q:32 * q + 32, pt, :].rearrange("p (c j) -> p c j", j=8)[:, :, s_],
                                            start=(s_ == 0), stop=(s_ == 7), tile_position=(32 * q, 0)),
                                           [("ring", sl), ("uT", pt)], [bk(b)])
                    yield
                    for q in range(4):
                        b = sb4[q]
                        rs = q % 2
                        bv = banks[b][:].rearrange("p (a e v c) -> p a e v c", a=2, e=2, v=2)
                        dve(lambda e, bv=bv, rs=rs, q=q, pp=pp: e.tensor_tensor(
                            out=srot[:, rs, 0, :].rearrange("p (a e c) -> p a e c", a=2, e=2),
                            in0=bv[:, :, :, 0, :], in1=gsel(cosv, pp, q), op=ALU.mult), [bk(b), "tabs"], [("srot", rs, 0)])
                        dve(lambda e, bv=bv, rs=rs, q=q, pp=pp: e.tensor_tensor(
                            out=srot[:, rs, 1, :].rearrange("p (a e c) -> p a e c", a=2, e=2),
                            in0=bv[:, :, :, 1, :], in1=gsel(sinv, pp, q), op=ALU.mult), [bk(b), "tabs"], [("srot", rs, 1)])
                        bfree(b)
                        dve(lambda e, rs=rs: e.tensor_tensor(out=srot[:, rs, 0, :], in0=srot[:, rs, 0, :], in1=srot[:, rs, 1, :], op=ALU.add),
                            [("srot", rs, 0), ("srot", rs, 1)], [("srot", rs, 0)])
                        zk = ("Zb", pp, q)
                        for idx in range(4):
                            ptl, e_ = idx // 2, idx % 2
                            g = 16 * pp + 8 * ptl + 2 * q + e_
                            r = 4 * q + idx
                            dve(lambda e, g=g, r=r, idx=idx, rs=rs, pp=pp: e.tensor_tensor_scan(
                                out=Zb[:, pp, r, 1:NCH + 1], data0=Rsb[:, l, g:g + 1].to_broadcast([128, NCH]),
                                data1=srot[:, rs, 0, idx * NCH:(idx + 1) * NCH], initial=Zb[:, pp, r, 0:1],
                                op0=ALU.mult, op1=ALU.add), [("srot", rs, 0), "Rsb", zk], [zk])
                        dve(lambda e, q=q, pp=pp: e.tensor_tensor(out=zsel(ZCb, pp, q, 0, NCH), in0=zsel(Zb, pp, q, 0, NCH),
                                                                  in1=gsel(cosv, pp, q), op=ALU.mult), [zk, "tabs"], [("ZCb", pp, q)])
                        dve(lambda e, q=q, pp=pp: e.tensor_tensor(out=zsel(ZSb, pp, q, 0, NCH), in0=zsel(Zb, pp, q, 0, NCH),
                                                                  in1=gsel(sinv, pp, q), op=ALU.mult), [zk, "tabs"], [("ZSb", pp, q)])
                        if q % 2 == 1:
                            yield
                    dve(lambda e, pp=pp: e.tensor_copy(
                        out=Zcar[:, l, 16 * pp:16 * pp + 16].rearrange("p (t q e) -> p q t e", t=2, q=4),
                        in_=Zb[:, pp, :, NCH].rearrange("p (q t e) -> p q t e", q=4, t=2)),
                        [("Zb", pp, q) for q in range(4)], ["Zcar"])
                for pt in range(4):
                    pp, ptl = pt // 2, pt % 2
                    sl = slots[pt]
                    yb = yield from balloc()
                    kd = ring[:, sl, 0:1024].rearrange("p (t n) -> p t n", n=128)
                    vv = ring[:, sl, 4096:6144].rearrange("p (g v n) -> p g v n", v=2, n=128)
                    uv = uT[:, pt, :].rearrange("p (c j) -> p c j", j=8)
                    yv = banks[yb][:].rearrange("p (c j) -> p c j", j=8)
                    for tau in range(8):
                        pe(lambda e, tau=tau, kd=kd, uv=uv, yv=yv: e.matmul(yv[:, :, tau:8], lhsT=kd[:, tau, :], rhs=uv[:, :, 0:8 - tau],
                                                                           start=(tau == 0), stop=False),
                           [("ring", sl), ("uT", pt)], [bk(yb)])
                    ycb = []
                    for _ in range(2):
                        b = yield from balloc()
                        ycb.append(b)
                    if pt == 0:
                        warm("s3")
                    for gi in range(8):
                        q, e_ = gi // 2, gi % 2
                        r = 4 * q + 2 * ptl + e_
                        b = ycb[gi // 4]
                        cs_ = (gi % 4) * 128
                        zkeys = [("ZCb", pp, q), ("ZSb", pp, q)]
                        pe(lambda e, r=r, gi=gi, b=b, cs_=cs_, vv=vv, pp=pp: e.matmul(banks[b][0:64, cs_:cs_ + 128], lhsT=ZCb[:, pp, r, :],
                                                                                     rhs=vv[:, gi, 0, :], start=True, stop=False),
                           zkeys + [("ring", sl)], [bk(b)])
                        pe(lambda e, r=r, gi=gi, b=b, cs_=cs_, vv=vv, pp=pp: e.matmul(banks[b][0:64, cs_:cs_ + 128], lhsT=ZSb[:, pp, r, :],
                                                                                     rhs=vv[:, gi, 1, :], start=False, stop=True),
                           zkeys + [("ring", sl)], [bk(b)])
                    yield
                    yk = ("Ycs", ptl)
                    ycv = Ycs[:, ptl].rearrange("p j g h -> p g j h")
                    act(lambda e, ycv=ycv, b0=ycb[0]: e.copy(out=ycv[:, 0:4], in_=banks[b0][0:64, :].rearrange("p (g j h) -> p g j h", j=8, h=16)),
                        [bk(ycb[0])], [yk])
                    act(lambda e, ycv=ycv, b1=ycb[1]: e.copy(out=ycv[:, 4:8], in_=banks[b1][0:64, :].rearrange("p (g j h) -> p g j h", j=8, h=16)),
                        [bk(ycb[1])], [yk])
                    bfree(ycb[0])
                    bfree(ycb[1])
                    warm("tr")
                    for j in range(8):
                        pe(lambda e, j=j, ptl=ptl, yv=yv: e.matmul(yv[:, :, j], lhsT=Ycs[:, ptl, j].rearrange("p g h -> p (g h)"),
                                                                  rhs=identb[0:64, 0:64], start=False, stop=(j == 7)),
                           [yk, "identb"], [bk(yb)])
                    srel((ti_, l, "S%d" % pt))
                    yield
                    ys = pt % 2
                    dcol = gv[:, gvb + 32 + pt:gvb + 33 + pt]
                    dve(lambda e, ys=ys, pt=pt, yb=yb, dcol=dcol: e.scalar_tensor_tensor(
                        out=ysb[:, ys, :], in0=uT[:, pt, :], scalar=dcol, in1=banks[yb][:], op0=ALU.mult, op1=ALU.add),
                        [("uT", pt), bk(yb), "gv"], [("ysb", ys)])
                    bfree(yb)
                    if l == 0 and ti_ == 0 and pt == 0:
                        tap("y0", ysb[:, 0, :], [("ysb", 0)])
                    dve(lambda e, ys=ys: e.scalar_tensor_tensor(out=t1[:, ys, :], in0=ysb[:, ys, :], scalar=0.044715, in1=ysb[:, ys, :],
                                                                op0=ALU.mult, op1=ALU.mult), [("ysb", ys)], [("t1", ys)])
                    dve(lambda e, ys=ys: e.scalar_tensor_tensor(out=t1[:, ys, :], in0=t1[:, ys, :], scalar=1.0, in1=ysb[:, ys, :],
                                                                op0=ALU.add, op1=ALU.mult), [("t1", ys), ("ysb", ys)], [("t1", ys)])
                    act(lambda e, ys=ys: e.activation(out=t1[:, ys, :], in_=t1[:, ys, :], func=AF.Tanh, scale=0.7978845608028654),
                        [("t1", ys)], [("t1", ys)])
                    dve(lambda e, ys=ys, pt=pt: e.scalar_tensor_tensor(out=gb[:, pt, :], in0=t1[:, ys, :], scalar=1.0, in1=ysb[:, ys, :],
                                                                      op0=ALU.add, op1=ALU.mult),
                        [("t1", ys), ("ysb", ys)], [("gb", pt)])
                    yield
                s = sget((ti_, l, "glu"))
                gbk = []
                for m in range(4):
                    b = yield from balloc()
                    gbk.append(b)
                for k in range(4):
                    for m in range(4):
                        b = gbk[m]
                        pe(lambda e, b=b, k=k, m=m, s=s: e.matmul(banks[b][:], lhsT=wtile(s, 512, k, m), rhs=gb[:, k, :],
                                                                  start=(k == 0), stop=(k == 3)), [("ring", s), ("gb", k)], [bk(b)])
                for m in range(4):
                    b = gbk[m]
                    ts_ = m % 2
                    act(lambda e, b=b, ts_=ts_: e.activation(out=tht[:, ts_, :], in_=banks[b][:], func=AF.Tanh, scale=0.25), [bk(b)], [("tht", ts_)])
                    bfree(b)
                    dve(lambda e, m=m, ts_=ts_: e.scalar_tensor_tensor(out=gl2[:, m, :], in0=tht[:, ts_, :], scalar=1.0, in1=gb[:, m, :],
                                                                      op0=ALU.add, op1=ALU.mult), [("tht", ts_), ("gb", m)], [("gl2", m)])
                srel((ti_, l, "glu"))
                yield
                if l == 0 and ti_ == 0:
                    tap("so0", gl2[:, 0, :], [("gl2", 0)])
                nb_ = yield from balloc()
                warm("norm")
                rmsnorm(lambda k: gl2[:, k, :], 4, lambda k: gs4[:, l, k:k + 1], ones13, "ones13", lambda k: mixed[:, k, :],
                        [("gl2", k) for k in range(4)], [("hn", k) for k in range(4)], bank=nb_)
                bfree(nb_)

            ssm_c = ssm_chain()
            att_c = attn_chain()
            chains = [ssm_c, att_c]
            stall = 0
            nstep = 0
            while chains:
                nfree0 = len(freeb)
                for c_ in list(chains):
                    if c_ is att_c and ssm_c in chains and nstep < ATT_DELAY:
                        continue
                    try:
                        next(c_)
                    except StopIteration:
                        chains.remove(c_)
                nstep += 1
                stall = stall + 1 if (len(freeb) == 0 and nfree0 == 0) else 0
                assert stall < 50, "bank allocation deadlock"

            LAG = 3

            def resid_mm(names, nkt, rhs_fn, rkeys, kouter=False):
                pendq = []
                for ci, name in enumerate(names):
                    s = sget((ti_, l, name))
                    _, kt_total, ncols = CH_OFF[name]
                    ko = (ci == 0 and kouter)
                    nt_ = ncols // 128
                    rb_ = [mmp.next() for _ in range(nt_)] if ko else None
                    if ko:
                        for k in range(nkt):
                            for mm_ in range(nt_):
                                pe(lambda e, b=rb_[mm_], k=k, mm_=mm_, s=s, ncols=ncols: e.matmul(
                                    banks[b][:], lhsT=wtile(s, ncols, k, mm_), rhs=rhs_fn(k), start=(k == 0), stop=(k == nkt - 1)),
                                   [("ring", s), rkeys[k]], [bk(rb_[mm_])])
                    for mm_ in range(ncols // 128):
                        m = ci * (ncols // 128) + mm_
                        b = rb_[mm_] if ko else mmp.next()
                        for k in range(0 if ko else nkt):
                            pe(lambda e, b=b, k=k, mm_=mm_, s=s, ncols=ncols: e.matmul(
                                banks[b][:], lhsT=wtile(s, ncols, k, mm_), rhs=rhs_fn(k), start=(k == 0), stop=(k == nkt - 1)),
                               [("ring", s), rkeys[k]], [bk(b)])
                        if len(pendq) >= LAG:
                            p_ = pendq.pop(0)
                            stat_mm(p_[0], p_[1] == 0, False)
                        dve(lambda e, b=b, m=m: e.tensor_tensor(out=hT[:, m, :], in0=banks[b][:], in1=hT[:, m, :], op=ALU.add),
                            [bk(b), ("hT", m)], [("hT", m)])
                        pendq.append((stat_sq(m), m))
                    srel((ti_, l, name))
                while pendq:
                    p_ = pendq.pop(0)
                    stat_mm(p_[0], p_[1] == 0, len(pendq) == 0)
            warm("wout")
            resid_mm(["out0", "out1"], 8, lambda k: mixed[:, k, :], HNK, kouter=True)
            if l == 0 and ti_ == 0:
                tap("h1", hT[:, 0, :], [("hT", 0)])

            rmsnorm(lambda k: hT[:, k, :], 8, lambda k: gv[:, gvb + 8 + k:gvb + 9 + k], ones10, "ones10", lambda k: hn[:, k, :], HK, HNK, pre=True)
            warm("ffn")
            for c in range(11):
                s = sget((ti_, l, "ffi%d" % c))
                fb = [mmp.next() for _ in range(4)] if c == 0 else None
                if c == 0:
                    for k in range(8):
                        for t4 in range(4):
                            pe(lambda e, k=k, s=s, t4=t4, b=fb[t4]: e.matmul(banks[b][:], lhsT=wtile(s, 512, k, t4), rhs=hn[:, k, :],
                                                                            start=(k == 0), stop=(k == 7)), [("ring", s), ("hn", k)], [bk(fb[t4])])
                for jj in range(2):
                    j = 2 * c + jj
                    bg = fb[2 * jj] if c == 0 else mmp.next()
                    bu = fb[2 * jj + 1] if c == 0 else mmp.next()
                    for k in range(0 if c == 0 else 8):
                        pe(lambda e, k=k, s=s, jj=jj, bg=bg: e.matmul(banks[bg][:], lhsT=wtile(s, 512, k, 2 * jj), rhs=hn[:, k, :],
                                                                     start=(k == 0), stop=(k == 7)), [("ring", s), ("hn", k)], [bk(bg)])
                    for k in range(0 if c == 0 else 8):
                        pe(lambda e, k=k, s=s, jj=jj, bu=bu: e.matmul(banks[bu][:], lhsT=wtile(s, 512, k, 2 * jj + 1), rhs=hn[:, k, :],
                                                                     start=(k == 0), stop=(k == 7)), [("ring", s), ("hn", k)], [bk(bu)])
                    ss = j % 2
                    act(lambda e, bg=bg, ss=ss: e.activation(out=sgt[:, ss, :], in_=banks[bg][:], func=AF.Silu), [bk(bg)], [("sgt", ss)])
                    dve(lambda e, bu=bu, ss=ss, j=j: e.tensor_tensor(out=hid[:, j, :], in0=banks[bu][:], in1=sgt[:, ss, :], op=ALU.mult),
                        [bk(bu), ("sgt", ss)], [("hid", j)])
                srel((ti_, l, "ffi%d" % c))
            resid_mm(["ffo%d" % c for c in range(4)], 22, lambda k: hid[:, k, :], [("hid", k) for k in range(22)])
            if l == 0 and ti_ == 0:
                tap("h2", hT[:, 0, :], [("hT", 0)])

            rmsnorm(lambda k: hT[:, k, :], 8, lambda k: gv[:, gvb + 16 + k:gvb + 17 + k], ones10, "ones10", lambda k: hn[:, k, :], HK, HNK, pre=True)
            warm("ple")
            sg0 = sget((ti_, l, "gate0"))
            spj = sget((ti_, l, "proj"))
            pendq = []
            for half in range(2):
                if half == 1:
                    srel((ti_, l, "gate0"))
                    sg0 = sget((ti_, l, "gate1"))
                bps = []
                for mm_ in range(4):
                    m = 4 * half + mm_
                    bp = mmp.next()
                    bps.append(bp)
                    for k in range(2):
                        pe(lambda e, bp=bp, k=k, m=m, spj=spj: e.matmul(banks[bp][:], lhsT=wtile(spj, 1024, k, m), rhs=pTs[:, k, :],
                                                                       start=(k == 0), stop=(k == 1)), [("ring", spj), "pTs"], [bk(bp)])
                gko = [mmp.next() for _ in range(3)] if half == 0 else None
                if half == 0:
                    for k in range(8):
                        for mm_ in range(3):
                            pe(lambda e, bg=gko[mm_], k=k, mm_=mm_, sg0=sg0: e.matmul(banks[bg][:], lhsT=wtile(sg0, 512, k, mm_), rhs=hn[:, k, :],
                                                                                   start=(k == 0), stop=(k == 7)), [("ring", sg0), ("hn", k)], [bk(gko[mm_])])
                for mm_ in range(4):
                    m = 4 * half + mm_
                    bp = bps[mm_]
                    pre_ = (half == 0 and mm_ < 3)
                    bg = gko[mm_] if pre_ else mmp.next()
                    for k in range(0 if pre_ else 8):
                        pe(lambda e, bg=bg, k=k, m=m, sg0=sg0: e.matmul(banks[bg][:], lhsT=wtile(sg0, 512, k, m % 4), rhs=hn[:, k, :],
                                                                       start=(k == 0), stop=(k == 7)), [("ring", sg0), ("hn", k)], [bk(bg)])
                    if len(pendq) >= LAG:
                        p_ = pendq.pop(0)
                        stat_mm(p_[0], p_[1] == 0, False)
                    ts_ = m % 2
                    act(lambda e, bg=bg, ts_=ts_: e.activation(out=tht[:, ts_, :], in_=banks[bg][:], func=AF.Tanh, scale=0.5),
                        [bk(bg)], [("tht", ts_)])
                    dve(lambda e, bp=bp, ts_=ts_: e.scalar_tensor_tensor(out=ott[:, 0, :], in0=tht[:, ts_, :], scalar=1.0, in1=banks[bp][:],
                                                                        op0=ALU.add, op1=ALU.mult), [("tht", ts_), bk(bp)], [("ott", 0)])
                    dve(lambda e, ts_=ts_, m=m: e.scalar_tensor_tensor(out=hT[:, m, :], in0=ott[:, 0, :], scalar=0.5, in1=hT[:, m, :],
                                                                      op0=ALU.mult, op1=ALU.add), [("ott", 0), ("hT", m)], [("hT", m)])
                    pendq.append((stat_sq(m), m))
            while pendq:
                p_ = pendq.pop(0)
                stat_mm(p_[0], p_[1] == 0, len(pendq) == 0)
            srel((ti_, l, "gate1"))
            srel((ti_, l, "proj"))
            if l == 0 and ti_ == 0:
                tap("h3", hT[:, 0, :], [("hT", 0)])

        for ti_ in range(ntiles):
            tsl = slice(ti_ * T, (ti_ + 1) * T)
            if ti_ == 0:
                for k in range(8):
                    P.op("sp", lambda e, tsl=tsl, k=k: e.dma_start(out=hT[:, k, :], in_=xT[k * 128:(k + 1) * 128, tsl]),
                         writes=[("hT", k)], dma="xld%d" % (k % 4))
            for l in range(nl):
                layer(ti_, l)
            gfb = NL * GV_L

            def store(k, tsl=tsl, ti_=ti_):
                o = P.op("sp", lambda e: e.dma_start(out=outT[k * 128:(k + 1) * 128, tsl], in_=(gl2 if k < 4 else atf)[:, k % 4, :]),
                         reads=[("gl2" if k < 4 else "atf", k % 4)], dma="out%d" % k)
                finals.append(o)
                if ti_ + 1 < ntiles:
                    nsl = slice((ti_ + 1) * T, (ti_ + 2) * T)
                    P.op("sp", lambda e: e.dma_start(out=hT[:, k, :], in_=xT[k * 128:(k + 1) * 128, nsl]),
                         writes=[("hT", k)], dma="xld%d" % (k % 4))
            rmsnorm(lambda k: hT[:, k, :], 8, lambda k: gv[:, gfb + k:gfb + k + 1], ones10, "ones10",
                    lambda k: (gl2 if k < 4 else atf)[:, k % 4, :], HK, [("gl2" if k < 4 else "atf", k % 4) for k in range(8)],
                    after=store, pre=True)
        P.emit(final_waits=finals)
    return nc


def kernel(**inputs):
    wpack, gv, sk, ssmp = _host_pack(inputs)
    x = np.asarray(inputs["x"], np.float32)
    p = np.asarray(inputs["p"], np.float32)
    nc = build()
    in_maps = []
    for c in range(8):
        in_maps.append({"xT": np.ascontiguousarray(x[c].T),
                        "pT": np.ascontiguousarray(p[:, c].transpose(0, 2, 1)),
                        "wpack": wpack, "gv": gv, "sk": sk, "ssmp": ssmp})
    res = run_bass_kernel_spmd(nc, in_maps, core_ids=list(range(8)))
    out = np.stack([np.asarray(res.results[c]["outT"]).T for c in range(8)])
    return np.ascontiguousarray(out.astype(np.float32))
```
